# Optimizing a Trainium2 kernel written in Bass

```python
import jax, jax.numpy as jnp
from jax import lax
import numpy as np

D_MODEL = 1024
BATCH = 8
SEQ = 2048
DEPTH = 1
DEC_BATCH = 32
DEC_SEQ = 1
PAST_LEN = 16384
PAGE_SIZE = 128

HEAD_DIM = 64
ATTN_GROUPS = ((128, 1), (512, 4), (2048, 16))
N_GROUPS = 3
HG = 4
N_ATTN_HEADS = N_GROUPS * HG
ATTN_WIDTH = N_ATTN_HEADS * HEAD_DIM
ATTN_OUT_WIDTH = HG * HEAD_DIM
BAND = 128
RWKV_N = 64
RWKV_WIDTH = D_MODEL // 2
RWKV_HEADS = RWKV_WIDTH // RWKV_N
DECAY_LORA = 64
AAA_LORA = 64
GATE_LORA = 128
SHIFT_WIDTH = 3 * RWKV_WIDTH + DECAY_LORA + AAA_LORA + GATE_LORA
D_FF = 4 * D_MODEL
IN_WIDTH = 3 * ATTN_WIDTH + SHIFT_WIDTH + 2 * D_MODEL
NORM_EPS = 1e-6
GN_EPS = 64e-5
NEG_INF = -1e30

kernel_name = "hybrid_dilated_attn_rwkv7_step"


def rms_norm(x, g):
    xf = x.astype(jnp.float32)
    y = xf * lax.rsqrt(jnp.mean(xf * xf, axis=-1, keepdims=True) + NORM_EPS)
    return (y * g.astype(jnp.float32)).astype(x.dtype)


def alibi_slopes():
    h = jnp.arange(1, N_ATTN_HEADS + 1, dtype=jnp.float32)
    return jnp.exp2(-8.0 * h / N_ATTN_HEADS)


def dilated_attn_prompt(q, k, v, dil, slopes):
    B, S, H, Dh = q.shape
    f32 = jnp.float32
    span = dil * BAND
    Sp = -(-S // span) * span
    L = Sp // dil
    nb = L // BAND

    def to_blocks(t):
        t = jnp.pad(t.astype(f32), ((0, 0), (0, Sp - S), (0, 0), (0, 0)))
        t = t.reshape(B, L, dil, H, Dh).transpose(0, 2, 1, 3, 4)
        return t.reshape(B, dil, nb, BAND, H, Dh)

    def with_prev(t):
        prev = jnp.concatenate([jnp.zeros_like(t[:, :, :1]), t[:, :, :-1]], axis=2)
        return jnp.concatenate([prev, t], axis=3)

    qb, kb, vb = to_blocks(q), to_blocks(k), to_blocks(v)
    k2, v2 = with_prev(kb), with_prev(vb)
    s = jnp.einsum('brnqhd,brnkhd->brnhqk', qb, k2) * (HEAD_DIM ** -0.5)
    qi = jnp.arange(BAND)[:, None]
    kj = jnp.arange(2 * BAND)[None, :]
    delta = BAND + qi - kj
    band = (delta >= 0) & (delta <= BAND)
    valid = band[None] & ((jnp.arange(nb)[:, None, None] > 0) | (kj[None] >= BAND))
    bias = -slopes[:, None, None] * (delta * dil).astype(f32)[None]
    s = jnp.where(valid[None, None, :, None], s + bias, NEG_INF)
    lse = jax.nn.logsumexp(s, axis=-1)
    p = jnp.exp(s - lse[..., None])
    o = jnp.einsum('brnhqk,brnkhd->brnqhd', p, v2)
    o = o.reshape(B, dil, L, H, Dh).transpose(0, 2, 1, 3, 4).reshape(B, Sp, H, Dh)[:, :S]
    lse = lse.transpose(0, 1, 2, 4, 3).reshape(B, dil, L, H).transpose(0, 2, 1, 3).reshape(B, Sp, H)[:, :S]
    return o, lse


def dilated_attn_sample(q, k_new, v_new, kv_buf, dil, slopes):
    B, T, H, Dh = q.shape
    f32 = jnp.float32
    Lb = kv_buf.shape[1]
    k_all = jnp.concatenate([kv_buf[:, :, 0].astype(f32), k_new.astype(f32)], axis=1)
    v_all = jnp.concatenate([kv_buf[:, :, 1].astype(f32), v_new.astype(f32)], axis=1)
    m = jnp.arange(BAND + 1)
    idx = Lb + jnp.arange(T)[:, None] - dil * m[None, :]
    valid = idx >= 0
    idx = jnp.maximum(idx, 0)
    kg = k_all[:, idx]
    vg = v_all[:, idx]
    s = jnp.einsum('bthd,btmhd->bthm', q.astype(f32), kg) * (HEAD_DIM ** -0.5)
    s = s - slopes[None, None, :, None] * (m * dil).astype(f32)[None, None, None, :]
    s = jnp.where(valid[None, :, None, :], s, NEG_INF)
    lse = jax.nn.logsumexp(s, axis=-1)
    p = jnp.exp(s - lse[..., None])
    o = jnp.einsum('bthm,btmhd->bthd', p, vg)
    return o, lse


def wkv_scan(r, decay, k, v, a_vec, b_vec, S0):
    def step(S, inp):
        r_t, w_t, k_t, v_t, a_t, b_t = inp
        sa = jnp.einsum('bhvk,bhk->bhv', S, a_t)
        S = S * w_t[:, :, None, :] + sa[..., None] * b_t[:, :, None, :] + v_t[..., None] * k_t[:, :, None, :]
        y = jnp.einsum('bhvk,bhk->bhv', S, r_t)
        return S, y
    xs = tuple(t.transpose(1, 0, 2, 3) for t in (r, decay, k, v, a_vec, b_vec))
    S, ys = lax.scan(step, S0.astype(jnp.float32), xs)
    return ys.transpose(1, 0, 2, 3), S


def rwkv_mix(zb, S0, w):
    B, T, _ = zb.shape
    H, N, R = RWKV_HEADS, RWKV_N, RWKV_WIDTH
    zf = zb.astype(jnp.float32)
    r, k, v, zw, za, zg = jnp.split(zf, [R, 2 * R, 3 * R, 3 * R + DECAY_LORA, 3 * R + DECAY_LORA + AAA_LORA], axis=-1)
    w_log = -jax.nn.softplus(-(w['w0'] + jnp.tanh(zw) @ w['w_lora_up'])) - 0.5
    decay = jnp.exp(-jnp.exp(w_log))
    a = jax.nn.sigmoid(w['a0'] + za @ w['a_lora_up'])
    g = jax.nn.sigmoid(zg) @ w['g_lora_up']
    kk = (k * w['k_k']).reshape(B, T, H, N)
    kk = kk / jnp.maximum(jnp.sqrt(jnp.sum(kk * kk, axis=-1, keepdims=True)), 1e-12)
    k = k * (1.0 + (a - 1.0) * w['k_a'])
    hd = lambda t: t.reshape(B, T, H, N)
    r_h, k_h, v_h, a_h, d_h = hd(r), hd(k), hd(v), hd(a), hd(decay)
    y, S = wkv_scan(r_h, d_h, k_h, v_h, -kk, kk * a_h, S0)
    mu = jnp.mean(y, axis=-1, keepdims=True)
    var = jnp.mean(jnp.square(y - mu), axis=-1, keepdims=True)
    yn = ((y - mu) * lax.rsqrt(var + GN_EPS)).reshape(B, T, R) * w['gn_g'] + w['gn_b']
    bonus = (jnp.sum(r_h * k_h * w['r_k'], axis=-1, keepdims=True) * v_h).reshape(B, T, R)
    return ((yn + bonus) * g).astype(zb.dtype), S


def block(x, kv_bufs, S0, shift_prev, w):
    B, T, _ = x.shape
    h = rms_norm(x, w['norm1_g'])
    z = h @ w['w_in']
    za, zb, zg = jnp.split(z, [3 * ATTN_WIDTH, 3 * ATTN_WIDTH + SHIFT_WIDTH], axis=-1)
    q, k, v = [t.reshape(B, T, N_ATTN_HEADS, HEAD_DIM) for t in jnp.split(za, 3, axis=-1)]
    slopes = alibi_slopes()
    outs, lses, kv_new = [], [], []
    for gi, (win, dil) in enumerate(ATTN_GROUPS):
        sl = slice(gi * HG, (gi + 1) * HG)
        qg, kg, vg = q[:, :, sl], k[:, :, sl], v[:, :, sl]
        if kv_bufs is None:
            o, l = dilated_attn_prompt(qg, kg, vg, dil, slopes[sl])
            rows = min(win, T)
            kv_new.append(jnp.stack([kg[:, T - rows:], vg[:, T - rows:]], axis=2))
        else:
            o, l = dilated_attn_sample(qg, kg, vg, kv_bufs[gi], dil, slopes[sl])
            kv_new.append(jnp.stack([kg, vg], axis=2))
        outs.append(o)
        lses.append(l)
    wts = jax.nn.softmax(jnp.stack(lses), axis=0)
    attn = jnp.sum(wts[..., None] * jnp.stack(outs), axis=0).reshape(B, T, ATTN_OUT_WIDTH).astype(x.dtype)
    z_prev = jnp.concatenate([shift_prev[:, None].astype(zb.dtype), zb[:, :-1]], axis=1)
    zb_mix = zb + (z_prev - zb) * w['mu_shift']
    rwkv, S = rwkv_mix(zb_mix, S0, w)
    gate = jax.nn.sigmoid((zg + w['b_gate']).astype(jnp.float32))
    gate_a, gate_b = gate[..., :D_MODEL], gate[..., D_MODEL:]
    merged = gate_a * (attn @ w['w_proj_a']) + gate_b * (rwkv @ w['w_proj_b'])
    x = x + merged.astype(x.dtype) @ w['w_out']
    hm = rms_norm(x, w['norm2_g'])
    x = x + jnp.square(jax.nn.relu(hm @ w['w_up'])) @ w['w_down']
    return x, kv_new[0], kv_new[1], kv_new[2], S, zb[:, -1]


def setup_inputs(seed: int = 0) -> dict:
    key = jax.random.key(seed)
    ks = list(jax.random.split(key, 40))
    f32 = jnp.float32

    def nrm(i, shape, scale):
        return scale * jax.random.normal(ks[i], shape, f32)

    def unif(i, shape, lo, hi):
        return jax.random.uniform(ks[i], shape, f32, lo, hi)

    Dd = DEPTH
    buf = lambda win: min(win, PAST_LEN)
    return {
        "x_prompt": nrm(0, (BATCH, SEQ, D_MODEL), 1.0),
        "x_sample": nrm(1, (DEC_BATCH, DEC_SEQ, D_MODEL), 1.0),
        "cache_kv_w128": nrm(2, (Dd, DEC_BATCH, buf(128), 2, HG, HEAD_DIM), 1.0),
        "cache_kv_w512": nrm(3, (Dd, DEC_BATCH, buf(512), 2, HG, HEAD_DIM), 1.0),
        "cache_kv_w2048": nrm(4, (Dd, DEC_BATCH, buf(2048), 2, HG, HEAD_DIM), 1.0),
        "state_wkv": nrm(5, (Dd, DEC_BATCH, RWKV_HEADS, RWKV_N, RWKV_N), 0.3),
        "state_shift": nrm(6, (Dd, DEC_BATCH, SHIFT_WIDTH), 1.0),
        "norm1_g": 1.0 + nrm(7, (Dd, D_MODEL), 0.05),
        "w_in": nrm(8, (Dd, D_MODEL, IN_WIDTH), D_MODEL ** -0.5),
        "b_gate": nrm(9, (Dd, 2 * D_MODEL), 0.1),
        "mu_shift": unif(10, (Dd, SHIFT_WIDTH), 0.0, 1.0),
        "w0": unif(11, (Dd, RWKV_WIDTH), -6.0, -0.5),
        "w_lora_up": nrm(12, (Dd, DECAY_LORA, RWKV_WIDTH), DECAY_LORA ** -0.5),
        "a0": nrm(13, (Dd, RWKV_WIDTH), 0.5),
        "a_lora_up": nrm(14, (Dd, AAA_LORA, RWKV_WIDTH), AAA_LORA ** -0.5),
        "g_lora_up": nrm(15, (Dd, GATE_LORA, RWKV_WIDTH), GATE_LORA ** -0.5),
        "k_k": 0.85 + nrm(16, (Dd, RWKV_WIDTH), 0.05),
        "k_a": 1.0 + nrm(17, (Dd, RWKV_WIDTH), 0.05),
        "r_k": nrm(18, (Dd, RWKV_HEADS, RWKV_N), 0.3),
        "gn_g": 1.0 + nrm(19, (Dd, RWKV_WIDTH), 0.05),
        "gn_b": nrm(20, (Dd, RWKV_WIDTH), 0.02),
        "w_proj_a": nrm(21, (Dd, ATTN_OUT_WIDTH, D_MODEL), ATTN_OUT_WIDTH ** -0.5),
        "w_proj_b": nrm(22, (Dd, RWKV_WIDTH, D_MODEL), RWKV_WIDTH ** -0.5),
        "w_out": nrm(23, (Dd, D_MODEL, D_MODEL), D_MODEL ** -0.5),
        "norm2_g": 1.0 + nrm(24, (Dd, D_MODEL), 0.05),
        "w_up": nrm(25, (Dd, D_MODEL, D_FF), D_MODEL ** -0.5),
        "w_down": nrm(26, (Dd, D_FF, D_MODEL), D_FF ** -0.5),
        "normf_g": 1.0 + nrm(27, (D_MODEL,), 0.05),
    }


def reference(x_prompt, x_sample, cache_kv_w128, cache_kv_w512, cache_kv_w2048, state_wkv, state_shift,
              norm1_g, w_in, b_gate, mu_shift, w0, w_lora_up, a0, a_lora_up, g_lora_up, k_k, k_a, r_k,
              gn_g, gn_b, w_proj_a, w_proj_b, w_out, norm2_g, w_up, w_down, normf_g):
    B = x_prompt.shape[0]
    xp, xs = x_prompt, x_sample
    per_p, per_s = [], []
    for l in range(DEPTH):
        w = dict(norm1_g=norm1_g[l], w_in=w_in[l], b_gate=b_gate[l], mu_shift=mu_shift[l], w0=w0[l],
                 w_lora_up=w_lora_up[l], a0=a0[l], a_lora_up=a_lora_up[l], g_lora_up=g_lora_up[l],
                 k_k=k_k[l], k_a=k_a[l], r_k=r_k[l], gn_g=gn_g[l], gn_b=gn_b[l], w_proj_a=w_proj_a[l],
                 w_proj_b=w_proj_b[l], w_out=w_out[l], norm2_g=norm2_g[l], w_up=w_up[l], w_down=w_down[l])
        S0p = jnp.zeros((B, RWKV_HEADS, RWKV_N, RWKV_N), jnp.float32)
        sh0p = jnp.zeros((B, SHIFT_WIDTH), xp.dtype)
        out_p = block(xp, None, S0p, sh0p, w)
        xp = out_p[0]
        per_p.append(out_p[1:])
        out_s = block(xs, (cache_kv_w128[l], cache_kv_w512[l], cache_kv_w2048[l]), state_wkv[l], state_shift[l], w)
        xs = out_s[0]
        per_s.append(out_s[1:])
    y_prompt = rms_norm(xp, normf_g)
    y_sample = rms_norm(xs, normf_g)
    kv128_p = jnp.stack([e[0] for e in per_p])
    kv512_p = jnp.stack([e[1] for e in per_p])
    kv2048_p = jnp.stack([e[2] for e in per_p])
    wkv_p = jnp.stack([e[3] for e in per_p])
    shift_p = jnp.stack([e[4] for e in per_p])
    kv128_s = jnp.stack([e[0] for e in per_s])
    kv512_s = jnp.stack([e[1] for e in per_s])
    kv2048_s = jnp.stack([e[2] for e in per_s])
    wkv_s = jnp.stack([e[3] for e in per_s])
    shift_s = jnp.stack([e[4] for e in per_s])
    return (y_prompt, y_sample, kv128_p, kv512_p, kv2048_p, wkv_p, shift_p, kv128_s, kv512_s, kv2048_s, wkv_s, shift_s)
```

```python
import contextlib
import os
import math
import numpy as np
import concourse.bass as bass
import concourse.mybir as mybir
from concourse.bass_utils import run_bass_kernel_spmd

F32 = mybir.dt.float32
BF16 = mybir.dt.bfloat16
ALU = mybir.AluOpType
AF = mybir.ActivationFunctionType
AX = mybir.AxisListType
ENGS = ("pe", "act", "dve", "pool", "sp")

T = 2048
NS = 4
NT = T + NS
D = 1024
NCORES = 8
C = 64
WT = 128
NRT = T + NS * C
CDEC = -math.exp(-0.5)
SLOPES = [2.0 ** (-8.0 * (h + 1) / 12.0) for h in range(12)]
DILS = [1, 4, 16]


class Op:
    __slots__ = ("idx", "eng", "fn", "deps", "is_dma", "needed", "sig", "sem", "semval", "prev_dma")

    def __init__(self, idx, eng, fn, is_dma):
        self.idx, self.eng, self.fn, self.is_dma = idx, eng, fn, is_dma
        self.deps = ()
        self.needed = False
        self.sig = self.sem = self.semval = self.prev_dma = None


class Prog:
    def __init__(self, nc, n_dma_sems=40, same_engine_sync=True):
        self.nc = nc
        self.ops = []
        self.last_w = {}
        self.readers = {}
        self.n_dma_sems = n_dma_sems
        self.same_engine_sync = same_engine_sync
        self.stack = contextlib.ExitStack()
        self._n = 0
        self.barrier_deps = ()
        self.barrier_id = 0
        self.eng_barrier = {e: 0 for e in ENGS}
        self.last_on_eng = {e: None for e in ENGS}
        self.dma_ops = []

    def sbuf(self, shape, dtype, st=None):
        self._n += 1
        return (st or self.stack).enter_context(self.nc.sbuf_tensor(f"sb{self._n}", list(shape), dtype))

    def psum(self, shape, dtype=F32, st=None):
        self._n += 1
        return (st or self.stack).enter_context(self.nc.psum_tensor(f"ps{self._n}", list(shape), dtype))

    def barrier(self):
        deps = set(i for i in self.last_on_eng.values() if i is not None)
        deps.update(self.dma_ops)
        self.barrier_deps = tuple(deps)
        self.barrier_id += 1
        self.dma_ops = []
        self.last_w = {}
        self.readers = {}

    def op(self, eng, fn, reads=(), writes=(), dma=False):
        o = Op(len(self.ops), eng, fn, dma)
        deps = set()
        if self.eng_barrier[eng] != self.barrier_id:
            deps.update(self.barrier_deps)
            self.eng_barrier[eng] = self.barrier_id
        for k in reads:
            w = self.last_w.get(k)
            if w is not None:
                deps.add(w)
        for k in writes:
            w = self.last_w.get(k)
            if w is not None:
                deps.add(w)
            deps.update(self.readers.get(k, ()))
        deps.discard(o.idx)
        latest = {}
        keep = set()
        for d_ in deps:
            dop = self.ops[d_]
            if dop.is_dma:
                keep.add(d_)
            elif latest.get(dop.eng, -1) < d_:
                latest[dop.eng] = d_
        keep.update(latest.values())
        o.deps = tuple(sorted(keep))
        for k in reads:
            self.readers.setdefault(k, []).append(o.idx)
        for k in writes:
            self.last_w[k] = o.idx
            self.readers[k] = []
        self.ops.append(o)
        self.last_on_eng[eng] = o.idx
        if dma:
            self.dma_ops.append(o.idx)
        return o

    def pe(self, fn, reads=(), writes=()):
        return self.op("pe", fn, reads, writes)

    def act(self, fn, reads=(), writes=()):
        return self.op("act", fn, reads, writes)

    def dve(self, fn, reads=(), writes=()):
        return self.op("dve", fn, reads, writes)

    def pool(self, fn, reads=(), writes=()):
        return self.op("pool", fn, reads, writes)

    def dma(self, fn, reads=(), writes=(), eng="sp"):
        return self.op(eng, fn, reads, writes, dma=True)

    def emit(self):
        nc, ops = self.nc, self.ops
        for o in ops:
            for d in o.deps:
                dop = ops[d]
                if dop.is_dma:
                    continue
                if dop.eng == o.eng and not o.is_dma and (dop.eng == "pe" or not self.same_engine_sync):
                    continue
                dop.needed = True
        cnt = {e: 0 for e in ENGS}
        for o in ops:
            if (not o.is_dma) and o.needed:
                cnt[o.eng] += 1
                o.sig = cnt[o.eng]
        ndma = 0
        dma_cnt = [0] * self.n_dma_sems
        last_on_sem = [None] * self.n_dma_sems
        for o in ops:
            if o.is_dma:
                s = ndma % self.n_dma_sems
                ndma += 1
                dma_cnt[s] += 16
                o.sem, o.semval, o.prev_dma = s, dma_cnt[s], last_on_sem[s]
                last_on_sem[s] = o.idx
        st = self.stack
        esem = {e: st.enter_context(nc.semaphore(f"s_{e}")) for e in ENGS}
        dsem = [st.enter_context(nc.semaphore(f"s_dma{i}")) for i in range(self.n_dma_sems)]
        per_eng = {e: [o for o in ops if o.eng == e] for e in ENGS}
        all_dma = [o for o in ops if o.is_dma]
        same = self.same_engine_sync

        def run(engname, eng):
            waited = {}

            def wait(key, semh, val):
                if waited.get(key, 0) >= val:
                    return
                eng.wait_ge(semh, val)
                waited[key] = val

            for o in per_eng[engname]:
                if o.is_dma and o.prev_dma is not None:
                    p = ops[o.prev_dma]
                    wait(("d", p.sem), dsem[p.sem], p.semval)
                for d in o.deps:
                    dop = ops[d]
                    if dop.is_dma:
                        wait(("d", dop.sem), dsem[dop.sem], dop.semval)
                    else:
                        if dop.eng == engname and not o.is_dma and (engname == "pe" or not same):
                            continue
                        wait(("e", dop.eng), esem[dop.eng], dop.sig)
                ins = o.fn(eng)
                if o.is_dma:
                    ins.then_inc(dsem[o.sem], 16)
                elif o.needed:
                    ins.then_inc(esem[o.eng], 1)
            if engname == "sp":
                lastv = {}
                for o in all_dma:
                    lastv[o.sem] = o.semval
                for s, v in lastv.items():
                    eng.wait_ge(dsem[s], v)

        with nc.Block() as block:
            @block.tensor
            def _(e):
                run("pe", e)

            @block.scalar
            def _(e):
                run("act", e)

            @block.vector
            def _(e):
                run("dve", e)

            @block.gpsimd
            def _(e):
                run("pool", e)

            @block.sync
            def _(e):
                run("sp", e)
        return cnt, ndma


PV = dict(norm1=0, norm2=8, bgate=16, mu=32, w0=46, a0=50, kk=54, ka=58, rk=62, gng=66, gnb=70, oka=74)
NPV = 78


def build_program(phases="AZB1234CD", debug=()):
    BSKIP = os.environ.get("BSKIP", "")
    nc = bass.Bass("TRN2", target_bir_lowering=False)
    P = Prog(nc)

    def din(name, shape):
        return nc.dram_tensor(name, list(shape), F32, kind="ExternalInput").ap()

    def dout(name, shape):
        return nc.dram_tensor(name, list(shape), F32, kind="ExternalOutput").ap()

    xp, xs = din("xp", [T, D]), din("xs", [NS, D])
    cache = [din("c128", [NS, 128, 512]), din("c512", [NS, 512, 512]), din("c2048", [NS, 2048, 512])]
    swkv, sshift = din("swkv", [NS, 8, 64, 64]), din("sshift", [NS, 1792])
    w_in = din("w_in", [D, 6144])
    w_pa, w_pb = din("w_proj_a", [256, D]), din("w_proj_b", [512, D])
    w_out, w_up, w_down = din("w_out", [D, D]), din("w_up", [D, 4096]), din("w_down", [4096, D])
    w_lora, a_lora, g_lora = din("w_lora_up", [64, 512]), din("a_lora_up", [64, 512]), din("g_lora_up", [128, 512])
    pvec_d, normf_d = din("pvec", [128, NPV]), din("normf_g", [D])
    emat_d, mk2_d = din("emat", [128, 12, 256]), din("mk2", [64, 256])
    ident_d, blk_d = din("ident", [128, 128]), din("blk", [128, 128])

    y_p, y_s = dout("y_p", [T, D]), dout("y_s", [NS, D])
    kvp = [dout("kv128_p", [128, 512]), dout("kv512_p", [512, 512]), dout("kv2048_p", [2048, 512])]
    kvs = [dout("kv128_s", [NS, 512]), dout("kv512_s", [NS, 512]), dout("kv2048_s", [NS, 512])]
    wkv_p, wkv_s = dout("wkv_p", [8, 64, 64]), dout("wkv_s", [NS, 8, 64, 64])
    shift_p, shift_s = dout("shift_p", [1792]), dout("shift_s", [NS, 1792])
    zscr = nc.dram_tensor("zscr", [14, 128, NRT], F32, kind="Internal").ap()
    dbg_out = {}

    pvec = P.sbuf([128, NPV], F32)
    identf = P.sbuf([128, 128], F32)
    identb = P.sbuf([128, 128], BF16)
    blk = P.sbuf([128, 128], F32)
    attnT = P.sbuf([128, 2, NT], BF16)
    rwkvT = P.sbuf([128, 4, NT], BF16)
    st_h = contextlib.ExitStack()
    hT = P.sbuf([128, 8, NT], BF16, st_h)
    hscr = nc.dram_tensor("hscr", [128, 8 * NT], BF16, kind="Internal").ap()

    class WS:
        def __init__(self, st, nbuf, width, reqs):
            self.bufs = [P.sbuf([128, 8, width], BF16, st) for _ in range(nbuf)]
            self.nbuf, self.reqs, self.issued, self.got = nbuf, reqs, 0, 0

        def _issue(self, j):
            t = self.bufs[j % self.nbuf]
            key = ("wp", j % self.nbuf)
            for (src2d, kc, co, ncols) in self.reqs[j]:
                src = src2d.rearrange("(k p) c -> p k c", p=128)
                P.dma(lambda e, t=t, src=src, kc=kc, co=co, ncols=ncols: e.dma_start(out=t[:, 0:kc, co:co + ncols], in_=src),
                      [], [key], eng="pool")

        def prime(self):
            while self.issued < min(len(self.reqs), self.nbuf):
                self._issue(self.issued)
                self.issued += 1

        def get(self):
            i = self.got
            self.got += 1
            while self.issued < min(len(self.reqs), i + self.nbuf):
                self._issue(self.issued)
                self.issued += 1
            return self.bufs[i % self.nbuf], ("wp", i % self.nbuf)
    psbig = [P.psum([128, 1024], F32) for _ in range(3)]
    psX = P.psum([128, 512], F32)
    psf = [psbig[i // 2][:, (i % 2) * 512:(i % 2) * 512 + 512] for i in range(6)] + [psX]
    pbig = [0]

    pslim = [7]

    def psbig_next():
        i = pbig[0] % (min(pslim[0], 6) // 2)
        pbig[0] += 1
        return psbig[i], [("psf", 2 * i), ("psf", 2 * i + 1)]

    psb = [P.psum([128, 1024], BF16) for _ in range(1)]
    pctr = [0]
    pbctr = [0]

    def ps_next():
        i = pctr[0] % pslim[0]
        pctr[0] += 1
        return psf[i], ("psf", i)

    def psb_next():
        i = pbctr[0] % len(psb)
        pbctr[0] += 1
        return psb[i], ("psb", i)

    def mm(out, lhsT, rhs, start, stop, reads, writes):
        P.pe(lambda e: e.matmul(out, lhsT=lhsT, rhs=rhs, start=start, stop=stop), reads, writes)

    def tr(out, in_, ident, reads, writes):
        P.pe(lambda e: e.transpose(out, in_, ident), reads, writes)

    def actf(out, in_, func, reads, writes, bias=None, scale=None, accum=None):
        kw = {}
        if bias is not None:
            kw["bias"] = bias
        if scale is not None:
            kw["scale"] = scale
        if accum is not None:
            kw["accum_out"] = accum
        P.act(lambda e: e.activation(out=out, in_=in_, func=func, **kw), reads, writes)

    def tt(out, in0, in1, op, reads, writes, eng="dve"):
        P.op(eng, lambda e: e.tensor_tensor(out=out, in0=in0, in1=in1, op=op), reads, writes)

    def ts(out, in0, s1, s2, op0, op1, reads, writes, eng="dve"):
        if op1 is None:
            P.op(eng, lambda e: e.tensor_scalar(out=out, in0=in0, scalar1=s1, scalar2=None, op0=op0), reads, writes)
        else:
            P.op(eng, lambda e: e.tensor_scalar(out=out, in0=in0, scalar1=s1, scalar2=s2, op0=op0, op1=op1), reads, writes)

    def stt(out, in0, scalar, in1, op0, op1, reads, writes):
        P.dve(lambda e: e.scalar_tensor_tensor(out=out, in0=in0, scalar=scalar, in1=in1, op0=op0, op1=op1), reads, writes)

    def cp(out, in_, reads, writes, eng="dve"):
        if eng == "act":
            P.act(lambda e: e.activation(out=out, in_=in_, func=AF.Copy), reads, writes)
        else:
            P.op(eng, lambda e: e.tensor_copy(out=out, in_=in_), reads, writes)

    def dmas(out, in_, reads, writes, slow=False):
        if slow:
            P.dma(lambda e: e.dma_start(out=out, in_=in_, allow_slow_non_contiguous=True), reads, writes)
        else:
            P.dma(lambda e: e.dma_start(out=out, in_=in_), reads, writes)

    def pcol(name, j=0, lo=0, hi=128):
        c = PV[name] + j
        return pvec[lo:hi, c:c + 1]

    dmas(pvec[:], pvec_d, [], ["pvec"])
    dmas(identf[:], ident_d, [], ["identf"])
    dmas(blk[:], blk_d, [], ["blk"])
    cp(identb[:], identf[:], ["identf"], ["identb"])
    ts(pvec[:, PV["oka"]:PV["oka"] + 4], pvec[:, PV["ka"]:PV["ka"] + 4], -1.0, 1.0, ALU.mult, ALU.add, ["pvec"], ["pvec"])

    tile_cols = [(n * 512, 512) for n in range(4)]
    SC = (T, NS)

    if "A" in phases:
        with contextlib.ExitStack() as st:
            xt = [P.sbuf([128, D], F32, st) for _ in range(6)]
            junk = P.sbuf([128, D], BF16, st)
            xn = [P.sbuf([128, D], BF16, st) for _ in range(5)]
            stat = P.sbuf([128, 4 * 20], F32, st)
            for grp in range(5):
                tiles = range(4 * grp, 4 * grp + 4) if grp < 4 else [16]
                tl = []
                for j, i in enumerate(tiles):
                    rows = 128 if i < 16 else NS
                    xb, xk = xt[i % 6], ("xt", i % 6)
                    src = xp[i * 128:(i + 1) * 128, :] if i < 16 else xs
                    dmas(xb[0:rows, :], src, [], [xk])
                    st_ = [stat[0:rows, 4 * i + q_:4 * i + q_ + 1] for q_ in range(3)]
                    tl.append((j, i, rows, xb, xk, st_, ("stat", i)))
                for (j, i, rows, xb, xk, (s0, s1, s2), sk) in tl:
                    actf(junk[0:rows, :], xb[0:rows, :], AF.Square, [xk], ["junk", sk], accum=s0)
                for (j, i, rows, xb, xk, (s0, s1, s2), sk) in tl:
                    ts(s1, s0, 1.0 / D, 1e-6, ALU.mult, ALU.add, [sk], [sk])
                for (j, i, rows, xb, xk, (s0, s1, s2), sk) in tl:
                    actf(s1, s1, AF.Sqrt, [sk], [sk])
                for (j, i, rows, xb, xk, (s0, s1, s2), sk) in tl:
                    P.dve(lambda e, s1=s1, s2=s2: e.reciprocal(out=s2, in_=s1), [sk], [sk])
                for (j, i, rows, xb, xk, (s0, s1, s2), sk) in tl:
                    xnb = xn[j if grp < 4 else 4]
                    ts(xnb[0:rows, :], xb[0:rows, :], s2, None, ALU.mult, None, [xk, sk], [("xn", j if grp < 4 else 4)])
                for k in range(8):
                    ptf_, pk = ps_next()
                    pt = ptf_.bitcast(BF16)
                    if grp < 4:
                        for j in range(4):
                            tr(pt[:, j * 128:(j + 1) * 128], xn[j][:, k * 128:(k + 1) * 128], identb[:],
                               [("xn", j), "identb"], [pk])
                        if k % 2 == 0:
                            actf(hT[:, k, grp * 512:(grp + 1) * 512], pt[:, 0:512], AF.Copy, [pk, "pvec"],
                                 [("hT", grp, k)], scale=pcol("norm1", k))
                        else:
                            ts(hT[:, k, grp * 512:(grp + 1) * 512], pt[:, 0:512], pcol("norm1", k), None, ALU.mult, None,
                               [pk, "pvec"], [("hT", grp, k)])
                    else:
                        tr(pt[:, 0:NS], xn[4][0:NS, k * 128:(k + 1) * 128], identb[0:NS, 0:NS], [("xn", 4), "identb"], [pk])
                        actf(hT[:, k, T:NT], pt[:, 0:NS], AF.Copy, [pk, "pvec"], [("hT", 4, k)], scale=pcol("norm1", k))
        P.barrier()

    def hkeys(ns):
        return [("hT", n, k) for n in ns for k in range(8)]

    dmas(hscr, hT[:].rearrange("p k t -> p (k t)"), [], ["hscr"])

    st_b0 = contextlib.ExitStack()
    wsB = WS(st_b0, 5, 128, [[(w_in[:, sec_ + g_ * 256 + p_ * 128: sec_ + g_ * 256 + p_ * 128 + 128], 8, 0, 128)]
                             for p_ in range(2) for g_ in range(3) for sec_ in (0, 768, 1536)])
    emat = P.sbuf([128, 12, 256], F32, st_b0)

    if "Z" in phases:
        with contextlib.ExitStack() as st:
            zraw = [P.sbuf([128, 516], F32, st) for _ in range(2)]
            dlt = [P.sbuf([128, 512], F32, st) for _ in range(2)]
            zmix = [P.sbuf([128, 512], F32, st) for _ in range(3)]
            zprev = P.sbuf([128, 14], F32, st)
            sprev = P.sbuf([128, 14, NS], F32, st)
            zsraw = P.sbuf([128, 14, NS], F32, st)
            zsd = P.sbuf([128, NS], F32, st)
            zspad = [P.sbuf([128, NS * C], F32, st) for _ in range(2)]
            sst = P.sbuf([NS, 1792], F32, st)
            dmas(sst[:], sshift, [], ["sst"])
            for c in range(14):
                pt, pk = ps_next()
                tr(pt[:, 0:NS], sst[0:NS, c * 128:(c + 1) * 128], identf[0:NS, 0:NS], ["sst", "identf"], [pk])
                cp(sprev[:, c, :], pt[:, 0:NS], [pk], [("sprev", c)])
            P.pool(lambda e: e.memset(zprev[:], 0.0), [], ["zprev"])
            for b in range(2):
                P.pool(lambda e, b=b: e.memset(zspad[b][:], 0.0), [], [("zspad", b)])
            it = 0
            ws = WS(st, 5, 128, [[(w_in[:, 2304 + c * 128: 2304 + (c + 1) * 128], 8, 0, 128)] for c in range(14)])
            for c in range(14):
                wt, wk = ws.get()
                for n in range(4):
                    pt, pk = ps_next()
                    for k in range(8):
                        mm(pt[:, 0:512], wt[:, k, 0:128], hT[:, k, n * 512:(n + 1) * 512], k == 0, k == 7,
                           [wk, ("hT", n, k)], [pk])
                    zr, zk = zraw[it % 2], ("zraw", it % 2)
                    dl, dk = dlt[it % 2], ("dlt", it % 2)
                    zm, mk = zmix[it % 3], ("zmix", it % 3)
                    it += 1
                    cp(zr[:, 1:513], pt[:, 0:512], [pk], [zk], eng="act")
                    cp(zr[:, 0:1], zprev[:, c:c + 1], ["zprev"], [zk], eng="pool")
                    tt(dl[:], zr[:, 0:512], zr[:, 1:513], ALU.subtract, [zk], [dk])
                    stt(zm[:], dl[:], pcol("mu", c), zr[:, 1:513], ALU.mult, ALU.add, [dk, zk, "pvec"], [mk])
                    cp(zprev[:, c:c + 1], zr[:, 512:513], [zk], ["zprev"], eng="pool")
                    dmas(zscr[c, :, n * 512:(n + 1) * 512], zm[:], [mk], [("zscr", c, n)])
                pt, pk = ps_next()
                for k in range(8):
                    mm(pt[:, 0:NS], wt[:, k, 0:128], hT[:, k, T:NT], k == 0, k == 7, [wk, ("hT", 4, k)], [pk])
                cp(zsraw[:, c, :], pt[:, 0:NS], [pk], [("zsraw", c)], eng="act")
                tt(zsd[:], sprev[:, c, :], zsraw[:, c, :], ALU.subtract, [("sprev", c), ("zsraw", c)], ["zsd"])
                zp, zpk = zspad[c % 2], ("zspad", c % 2)
                stt(zp[:, 0:NS * C:C], zsd[:], pcol("mu", c), zsraw[:, c, :], ALU.mult, ALU.add,
                    ["zsd", ("zsraw", c), "pvec"], [zpk])
                dmas(zscr[c, :, T:NRT], zp[:], [zpk], [("zscr", c, 4)])
            shst = P.sbuf([14, 5, 128], F32, st)
            for q_ in range(5):
                src_ = zprev[:, 0:14] if q_ == 0 else zsraw[:, :, q_ - 1]
                rk_ = ["zprev"] if q_ == 0 else [("zsraw", c_) for c_ in range(14)]
                pt, pk = ps_next()
                tr(pt[0:14, 0:128], src_, identf[:, :], rk_ + ["identf"], [pk])
                cp(shst[:, q_, :], pt[0:14, 0:128], [pk], [("shst", q_)])
                dst_ = shift_p if q_ == 0 else shift_s[q_ - 1]
                dmas(dst_.rearrange("(c p) -> c p", p=128), shst[:, q_, :], [("shst", q_)], [])
        if "B" in phases:
            wsB.prime()
            P.dma(lambda e: e.dma_start(out=emat[:], in_=emat_d), [], ["emat_pre"])
        P.barrier()

    if "B" in phases:
        with contextlib.ExitStack() as st:
            qT = [P.sbuf([128, NT], BF16, st) for _ in range(3)]
            kT = [P.sbuf([128, NT], BF16, st) for _ in range(3)]
            vaug = [P.sbuf([128, 16, 2, 128], BF16, st) for _ in range(3)]
            oacc = [P.sbuf([128, NT], F32, st) for _ in range(2)]
            exb = [P.sbuf([128, 256], F32, st) for _ in range(4)]
            ptb = [P.sbuf([128, 256], BF16, st) for _ in range(8)]
            kvst = [P.sbuf([128, 2, 128], F32, st) for _ in range(3)]
            rden = P.sbuf([128, NT], F32, st)
            cch = [P.sbuf([128, 2, 128], F32, st) for _ in range(2)]
            kcb = P.sbuf([128, 128], BF16, st)
            kcT = P.sbuf([128, 128], BF16, st)
            vca = P.sbuf([128, 2, 128], BF16, st)
            vnew = P.sbuf([1, NS, 3, 2, 128], BF16, st)
            knst = P.sbuf([1, 2, 128], F32, st)
            vnst = P.sbuf([1, 2, 128], F32, st)
            pnew = P.sbuf([1, 4], BF16, st)
            exs = P.sbuf([128, 4], F32, st)
            pts = P.sbuf([128, 4], BF16, st)
            for g in range(3):
                P.pool(lambda e, g=g: e.memset(vaug[g][:], 1.0), [], [("vaug", g, bl_, h_) for bl_ in range(16) for h_ in range(2)])
            P.pool(lambda e: e.memset(vnew[:], 1.0), [], ["vnew"])
            P.pool(lambda e: e.memset(vca[:], 1.0), [], ["vca"])
            ws = wsB
            for ps_ in range(2):
                wq, wkk, wv = [], [], []
                for g in range(3):
                    d = DILS[g]
                    L = T // d
                    col = g * 256 + ps_ * 128
                    for sec, dst in ((0, qT), (768, kT)):
                        wt, wk = ws.get()
                        for n in range(4):
                            pt, pk = ps_next()
                            for k in range(8):
                                mm(pt[:, 0:512], wt[:, k, 0:128], hT[:, k, n * 512:(n + 1) * 512], k == 0, k == 7,
                                   [wk, ("hT", n, k)], [pk])
                            if d == 1 or "p" in BSKIP:
                                cp(dst[g][:, n * 512:(n + 1) * 512], pt[:, 0:512], [pk], [("qk", sec, g)], eng="act")
                            else:
                                ov = dst[g][:, 0:T].rearrange("p (r i) -> p r i", r=d)[:, :, n * 512 // d:(n + 1) * 512 // d]
                                iv = pt[:, 0:512].rearrange("p (i r) -> p r i", r=d)
                                cp(ov, iv, [pk], [("qk", sec, g)], eng=("dve" if "q" in BSKIP else "act"))
                        pt, pk = ps_next()
                        for k in range(8):
                            mm(pt[:, 0:NS], wt[:, k, 0:128], hT[:, k, T:NT], k == 0, k == 7, [wk, ("hT", 4, k)], [pk])
                        cp(dst[g][:, T:NT], pt[:, 0:NS], [pk], [("qk", sec, g)], eng="act")
                        if sec == 768 and "k" not in BSKIP:
                            need = {0: [15], 1: [3, 7, 11, 15], 2: list(range(16))}[g]
                            rows_g = [128, 512, 2048][g]
                            for bl in need:
                                r_, i0 = (bl * 128) // L, (bl * 128) % L
                                t0 = i0 * d + r_
                                pt, pk = ps_next()
                                for k in range(8):
                                    lh = hT[:, k, bl * 128:(bl + 1) * 128] if "n" in BSKIP else hT[:, k, t0:t0 + 127 * d + 1:d]
                                    mm(pt[:, 0:128], lh, wt[:, k, 0:128], k == 0, k == 7,
                                       [wk] + hkeys(range(4)), [pk])
                                sb, sbk = kvst[bl % 3], ("kvst", bl % 3)
                                cp(sb[:, 0, :], pt[:, 0:128], [pk], [sbk])
                                row0 = t0 - (T - rows_g)
                                dst_ap = kvp[g].rearrange("t (kv h c) -> t kv h c", kv=2, h=2)[row0:row0 + 127 * d + 1:d, 0, ps_, :]
                                if "d" not in BSKIP:
                                    dmas(dst_ap, sb[:, 0, :], [sbk], [])
                            for s in (range(NS) if "s" not in BSKIP else []):
                                pt, pk = ps_next()
                                for k in range(8):
                                    mm(pt[0:1, 0:128], hT[:, k, T + s:T + s + 1], wt[:, k, 0:128], k == 0, k == 7,
                                       [wk, ("hT", 4, k)], [pk])
                                cp(knst[0:1, s % 2, :], pt[0:1, 0:128], [pk], [("knst", s % 2)])
                                dst_ap = kvs[g].rearrange("s (kv h c) -> s kv h c", kv=2, h=2)[s:s + 1, 0, ps_, :]
                                dmas(dst_ap, knst[0:1, s % 2, :], [("knst", s % 2)], [])
                    if "v" in BSKIP:
                        continue
                    wt, wk = ws.get()
                    rows_g = [128, 512, 2048][g]
                    need = {0: [15], 1: [3, 7, 11, 15], 2: list(range(16))}[g]
                    for bl in range(16):
                        r_, i0 = (bl * 128) // L, (bl * 128) % L
                        t0 = i0 * d + r_
                        pt, pk = ps_next()
                        for k in range(8):
                            mm(pt[:, 0:128], hT[:, k, t0:t0 + 127 * d + 1:d], wt[:, k, 0:128], k == 0, k == 7,
                               [wk] + hkeys(range(4)), [pk])
                        sb, sbk = kvst[bl % 3], ("kvst", bl % 3)
                        cp(sb[:, 1, :], pt[:, 0:128], [pk], [sbk])
                        cp(vaug[g][:, bl, 0, 0:64], sb[:, 1, 0:64], [sbk], [("vaug", g, bl, 0)], eng="act")
                        cp(vaug[g][:, bl, 1, 64:128], sb[:, 1, 64:128], [sbk], [("vaug", g, bl, 1)], eng="act")
                        if bl in need:
                            row0 = t0 - (T - rows_g)
                            dst_ap = kvp[g].rearrange("t (kv h c) -> t kv h c", kv=2, h=2)[row0:row0 + 127 * d + 1:d, 1, ps_, :]
                            dmas(dst_ap, sb[:, 1, :], [sbk], [])
                    for s in (range(NS) if "s" not in BSKIP else []):
                        pt, pk = ps_next()
                        for k in range(8):
                            mm(pt[0:1, 0:128], hT[:, k, T + s:T + s + 1], wt[:, k, 0:128], k == 0, k == 7,
                               [wk, ("hT", 4, k)], [pk])
                        cp(vnst[0:1, s % 2, :], pt[0:1, 0:128], [pk], [("vnst", s % 2)])
                        cp(vnew[0:1, s, g, 0, 0:64], vnst[0:1, s % 2, 0:64], [("vnst", s % 2)], ["vnew"], eng="act")
                        cp(vnew[0:1, s, g, 1, 64:128], vnst[0:1, s % 2, 64:128], [("vnst", s % 2)], ["vnew"], eng="act")
                        dst_ap = kvs[g].rearrange("s (kv h c) -> s kv h c", kv=2, h=2)[s:s + 1, 1, ps_, :]
                        dmas(dst_ap, vnst[0:1, s % 2, :], [("vnst", s % 2)], [])
                it = 0
                for hl in (range(2) if "1" in phases else []):
                    pb = hl * 64
                    oa, oak = oacc[hl], ("oacc", hl)
                    oask = ("oaccS", hl)

                    def samp_gen():
                        for s in (range(NS) if "3" in phases else []):
                            for g in range(3):
                                d = DILS[g]
                                head = 4 * g + 2 * ps_ + hl
                                cb, cbk = cch[(s * 3 + g) % 2], ("cch", (s * 3 + g) % 2)
                                src = cache[g].rearrange("s t (kv h c) -> s t kv h c", kv=2, h=2)[s, 0:127 * d + 1:d, :, ps_, :]
                                dmas(cb[:], src, [], [cbk])
                                yield
                                cp(kcb[:], cb[:, 0, :], [cbk], ["kcb"], eng="pool")
                                if hl == 0:
                                    cp(vca[:, 0, 0:64], cb[:, 1, 0:64], [cbk], ["vca"], eng="pool")
                                else:
                                    cp(vca[:, 1, 64:128], cb[:, 1, 64:128], [cbk], ["vca"], eng="pool")
                                yield
                                ptr, ptrk = psb_next()
                                tr(ptr[:, 0:128], kcb[:], identb[:], ["kcb", "identb"], [ptrk])
                                cp(kcT[:], ptr[:, 0:128], [ptrk], ["kcT"], eng="act")
                                yield
                                sp_, spk = ps_next()
                                qcol = qT[g][pb:pb + 64, T + s:T + s + 1]
                                mm(sp_[:, 0:1], kcT[pb:pb + 64, :], qcol, True, True, ["kcT", ("qk", 0, g)], [spk])
                                mm(sp_[0:1, 1:2], kT[g][pb:pb + 64, T + s:T + s + 1], qcol, True, True,
                                   [("qk", 0, g), ("qk", 768, g)], [spk])
                                actf(exs[:, 0:1], sp_[:, 0:1], AF.Exp, [spk], ["exs"], scale=0.125)
                                actf(pnew[0:1, 0:1], sp_[0:1, 1:2], AF.Exp, [spk], ["pnew"], scale=0.125)
                                yield
                                tt(pts[:, 0:1], exs[:, 0:1], emat[:, head, 128:129], ALU.mult, ["exs", "emat"], ["pts"])
                                yield
                                op_, opk = ps_next()
                                mm(op_[:, 0:1], vca[:, hl, :], pts[:, 0:1], True, False, ["vca", "pts"], [opk])
                                mm(op_[:, 0:1], vnew[0:1, s, g, hl, :], pnew[0:1, 0:1], False, True, ["vnew", "pnew"], [opk])
                                ov = oa[:, T + s:T + s + 1]
                                if g == 0:
                                    cp(ov, op_[:, 0:1], [opk], [oask])
                                else:
                                    tt(ov, ov, op_[:, 0:1], ALU.add, [opk, oask], [oask])
                                yield

                    steps = []
                    for g in (range(3) if "2" in phases else []):
                        d = DILS[g]
                        L = T // d
                        nb = L // 128
                        for r_ in range(d):
                            for kb in range(nb):
                                steps.append((g, r_, kb, nb, L, d))
                    LOOK = 3
                    NPB = len(ptb)
                    sgen = samp_gen()
                    pend = []
                    prevs = {}
                    accq = []

                    def stage_pv(info):
                        (g, r_, kb, nb, L, d, pt_, ptk, bl) = info
                        op_, opk = ps_next()
                        prev = prevs.get((g, r_)) if kb > 0 else None
                        if prev is not None:
                            ppt, pptk, pbl = prev
                            mm(op_[:, 0:128], vaug[g][:, pbl, hl, :], ppt[:, 128:256], True, False, [("vaug", g, pbl, hl), pptk], [opk])
                        mm(op_[:, 0:128], vaug[g][:, bl, hl, :], pt_[:, 0:128], prev is None, True, [("vaug", g, bl, hl), ptk], [opk])
                        prevs[(g, r_)] = (pt_, ptk, bl)
                        accq.append((g, r_, kb, d, op_, opk))
                        if len(accq) > 1:
                            stage_acc(accq.pop(0))

                    def stage_acc(a_):
                        (g, r_, kb, d, op_, opk) = a_
                        t0 = kb * 128 * d + r_
                        ov = oa[:, t0:t0 + 127 * d + 1:d]
                        if g == 0:
                            okeys = [("oacc", hl, kb)]
                        elif g == 1:
                            okeys = [("oacc", hl, 4 * kb + i_) for i_ in range(4)]
                        else:
                            okeys = [("oacc", hl, i_) for i_ in range(16)]
                        if g == 0:
                            cp(ov, op_[:, 0:128], [opk], okeys)
                        else:
                            tt(ov, ov, op_[:, 0:128], ALU.add, [opk] + okeys, okeys)

                    for (g, r_, kb, nb, L, d) in steps:
                        head = 4 * g + 2 * ps_ + hl
                        base = r_ * L + kb * 128
                        ncols = 256 if kb + 1 < nb else 128
                        sp_, spk = ps_next()
                        mm(sp_[:, 0:ncols], kT[g][pb:pb + 64, base:base + 128], qT[g][pb:pb + 64, base:base + ncols],
                           True, True, [("qk", 0, g), ("qk", 768, g)], [spk])
                        ex, exk = exb[it % 4], ("exb", it % 4)
                        pt_, ptk = ptb[it % NPB], ("ptb", it % NPB)
                        it += 1
                        actf(ex[:, 0:ncols], sp_[:, 0:ncols], AF.Exp, [spk], [exk], scale=0.125)
                        tt(pt_[:, 0:ncols], ex[:, 0:ncols], emat[:, head, 0:ncols], ALU.mult, [exk, "emat"], [ptk],
                           eng="pool" if it % 2 else "dve")
                        pend.append((g, r_, kb, nb, L, d, pt_, ptk, base // 128))
                        if len(pend) > LOOK:
                            stage_pv(pend.pop(0))
                        next(sgen, None)
                    while pend:
                        stage_pv(pend.pop(0))
                    while accq:
                        stage_acc(accq.pop(0))
                    for _ in sgen:
                        pass
                    nb_, db_ = (0, 64) if hl == 0 else (64, 0)
                    if "4" not in phases:
                        continue
                    oall = [("oacc", hl, i_) for i_ in range(16)]
                    actf(rden[nb_:nb_ + 64, :], oa[db_:db_ + 64, :], AF.Ln, oall + [oask], ["rden"])
                    actf(rden[nb_:nb_ + 64, :], rden[nb_:nb_ + 64, :], AF.Exp, ["rden"], ["rden"], scale=-1.0)
                    tt(attnT[nb_:nb_ + 64, ps_, :], oa[nb_:nb_ + 64, :], rden[nb_:nb_ + 64, :], ALU.mult, oall + [oask, "rden"],
                       [("attnT", ps_, hl)])
        P.barrier()


    st_b0.close()
    st_h.close()

    if "C" in phases:
        with contextlib.ExitStack() as st:
            F32R = mybir.dt.float32r
            USE_R = "R" not in os.environ.get("BSKIP", "")
            r32 = (lambda ap: ap.bitcast(F32R)) if USE_R else (lambda ap: ap)
            NCH = WT // C
            NW = 4 * WT
            mk2 = P.sbuf([64, 256], F32, st)
            dmas(mk2[:], mk2_d, [], ["mk2"])
            cmask = P.sbuf([128, NW], BF16, st)
            smask = P.sbuf([128, WT], F32, st)
            blkr = P.sbuf([128, 128], F32, st)
            idr = P.sbuf([64, 64], F32, st)
            P.pool(lambda e: e.memset(cmask[:], 1.0), [], ["cmask"])
            P.pool(lambda e: e.memset(cmask[:, 0:NW:C], 0.0), ["cmask"], ["cmask"])
            P.pool(lambda e: e.memset(smask[:], 0.0), [], ["smask"])
            P.pool(lambda e: e.memset(smask[:, 0:WT:C], 1.0), ["smask"], ["smask"])
            cp(r32(blkr[:]), blk[:], ["blk"], ["blkr"])
            cp(r32(idr[:]), identf[0:64, 0:64], ["identf"], ["idr"])
            wl_b = P.sbuf([128, 512], BF16, st)
            al_b = P.sbuf([128, 512], BF16, st)
            gl_b = P.sbuf([128, 512], BF16, st)
            P.dma(lambda e: e.dma_start(out=wl_b[0:64, :], in_=w_lora), [], ["wl_b"], eng="pool")
            P.dma(lambda e: e.dma_start(out=al_b[64:128, :], in_=a_lora), [], ["al_b"], eng="pool")
            P.dma(lambda e: e.dma_start(out=gl_b[:], in_=g_lora), [], ["gl_b"], eng="pool")
            z12 = P.sbuf([128, WT], F32, st)
            z13 = P.sbuf([128, WT], F32, st)
            twza = P.sbuf([128, WT], BF16, st)
            sgb = P.sbuf([128, WT], BF16, st)
            NTB = 12
            TB = [P.sbuf([128, NW], F32, st) for _ in range(NTB)]
            SQ = P.sbuf([128, NW], F32, st)
            TVB = [[P.sbuf([128 if s_ else 64, 8 * 64], F32, st) for _ in range(6)] for s_ in range(2)]
            SHB = P.sbuf([128, 8 * 64], F32, st)
            SCB = [P.sbuf([128, NW], BF16, st) for _ in range(2)]
            HG = P.sbuf([64, 8, 64], F32, st)
            TK = [("T", i) for i in range(NTB)]
            natbs = [[P.sbuf([128, NW], BF16, st) for _ in range(5)] for _ in range(2)]
            nks = [[("natb", p_, i) for i in range(5)] for p_ in range(2)]
            gbufs = [P.sbuf([128, NW], F32, st) for _ in range(3)]
            bons = [P.sbuf([128, NW], F32, st) for _ in range(3)]
            GNB = [P.sbuf([128, NW], F32, st) for _ in range(2)]
            SQ2 = P.sbuf([128, NW], F32, st)
            gams = [P.sbuf([128, 4 * NCH], F32, st) for _ in range(2)]
            gamFs = [P.sbuf([64, 8, NCH], F32, st) for _ in range(3)]
            opFs = [P.sbuf([64, 8, NCH, 4, 64], BF16, st) for _ in range(3)]
            TMs = [[P.sbuf([64, NCH, 3, 128], BF16, st) for _ in range(4)] for _ in range(3)]
            Mbs = [P.sbuf([64, 8 * NCH, 192], BF16, st) for _ in range(2)]
            TinvTs = [P.sbuf([64, 8, NCH, 64], BF16, st) for _ in range(2)]
            Hf = P.sbuf([64, 8, 64], F32, st)
            Hb = P.sbuf([64, 8, 64], BF16, st)
            Wsb = P.sbuf([64, 8, 64], BF16, st)
            Usb = P.sbuf([64, 8, 64], BF16, st)
            ynat = P.sbuf([128, NW], F32, st)
            s0in = P.sbuf([64, 8, 64], F32, st)
            sout = P.sbuf([64, 8, 64], F32, st)
            P.pool(lambda e: e.memset(Hf[:], 0.0), [], ["Hf"])
            P.pool(lambda e: e.memset(Hb[:], 0.0), [], ["Hb"])
            w4 = lambda t: t[:, :].rearrange("p (h t) -> p h t", h=4)
            pool_ctr = {"T": 0, "S": 0}
            pool_ids = {"T": [0, 1, 2], "S": [3, 6]}

            def psn(stage):
                ids = pool_ids[stage]
                i = ids[pool_ctr[stage] % len(ids)]
                pool_ctr[stage] += 1
                return psf[i], ("psf", i)

            def emit_state_out(dst_view):
                pt, pk = psn("S")
                for h in range(8):
                    tr(pt[0:64, h * 64:(h + 1) * 64], Hf[:, h, :], identf[0:64, 0:64], ["Hf", "identf"], [pk])
                cp(sout[:], pt[0:64, 0:512].rearrange("p (h k) -> p h k", h=8), [pk], ["sout"])
                dmas(dst_view, sout[:], ["sout"], [])

            NTILES = T // WT + NS * C // WT
            pslim[0] = 4
            zk = lambda c: [("zscr", c, n) for n in range(5)]

            def preA(ti, par):
                samp = ti >= T // WT
                c0 = ti * WT
                natb, nk, gbuf, bon, gam = natbs[par], nks[par], gbufs[ti % 3], bons[ti % 3], gams[par]
                gbk, bonk, gamk = ("gbuf", ti % 3), ("bon", ti % 3), ("gam", par)
                bt, kt, at, rt, vb = natb
                sgy, a_, kk, rn, f_, cs_, eg, egi, r_, k_, v_, egm = TB[0:12]
                sgyk, ak_, kkk, rnk, fk, csk, egk, egik, rk_, kk_, vk_, egmk = TK[0:12]
                P0, P0k, P1, P1k = psf[4], [("psf", 4)], psf[5], [("psf", 5)]
                hsl = [(slice(hp * 128, (hp + 1) * 128), slice(hp * WT, (hp + 1) * WT)) for hp in range(4)]
                dmas(z12[:], zscr[12, :, c0:c0 + WT], zk(12), ["z12"])
                dmas(z13[:], zscr[13, :, c0:c0 + WT], zk(13), ["z13"])
                for j, (dst, key) in enumerate(((r_, rk_), (k_, kk_), (v_, vk_))):
                    dmas(w4(dst), zscr[4 * j:4 * j + 4, :, c0:c0 + WT].rearrange("c p t -> p c t"),
                         [x for c in range(4 * j, 4 * j + 4) for x in zk(c)], [key])
                yield
                actf(twza[0:64, :], z12[0:64, :], AF.Tanh, ["z12"], ["twza0"])
                cp(twza[64:128, :], z12[64:128, :], ["z12"], ["twza1"], eng="pool")
                actf(sgb[:], z13[:], AF.Sigmoid, ["z13"], ["sgb"])
                for hp, (cs, ws_) in enumerate(hsl):
                    ts(kk[:, ws_], k_[:, ws_], pcol("kk", hp), None, ALU.mult, None, [kk_, "pvec"], [kkk])
                yield
                for hp, (cs, ws_) in enumerate(hsl):
                    mm(P0[:, ws_], wl_b[0:64, cs], twza[0:64, :], True, True, ["wl_b", "twza0"], P0k)
                for hp, (cs, ws_) in enumerate(hsl):
                    mm(P1[:, ws_], al_b[64:128, cs], twza[64:128, :], True, True, ["al_b", "twza1"], P1k)
                tt(r32(SQ[:]), kk[:], kk[:], ALU.mult, [kkk], ["SQ"])
                yield
                for hp, (cs, ws_) in enumerate(hsl):
                    actf(sgy[:, ws_], P0[:, ws_], AF.Sigmoid, P0k + ["pvec"], [sgyk], bias=pcol("w0", hp))
                for hp, (cs, ws_) in enumerate(hsl):
                    actf(a_[:, ws_], P1[:, ws_], AF.Sigmoid, P1k + ["pvec"], [ak_], bias=pcol("a0", hp))
                if samp:
                    tt(w4(sgy), w4(sgy), smask[:, :].unsqueeze(1).to_broadcast([128, 4, WT]), ALU.mult, [sgyk, "smask"], [sgyk])
                yield
                P.dve(lambda e: e.tensor_tensor_scan(out=cs_[:], data0=cmask[:], data1=sgy[:], initial=0.0,
                                                     op0=ALU.mult, op1=ALU.add), ["cmask", sgyk], [csk])
                for hp, (cs, ws_) in enumerate(hsl):
                    mm(P0[:, ws_], r32(blkr[:]), r32(SQ[:, ws_]), True, True, ["blkr", "SQ"], P0k)
                for hp, (cs, ws_) in enumerate(hsl):
                    mm(P1[:, ws_], gl_b[:, cs], sgb[:], True, True, ["gl_b", "sgb"], P1k)
                for hp, (cs, ws_) in enumerate(hsl):
                    ts(f_[:, ws_], a_[:, ws_], pcol("ka", hp), pcol("oka", hp), ALU.mult, ALU.add, [ak_, "pvec"], [fk], eng="pool")
                yield
                actf(eg[:], cs_[:], AF.Exp, [csk], [egk], scale=CDEC)
                actf(egi[:], cs_[:], AF.Exp, [csk], [egik], scale=-CDEC)
                tt(egm[:], cs_[:], sgy[:], ALU.subtract, [csk, sgyk], [egmk])
                ts(rn[:], P0[:, 0:NW], 6e-20, None, ALU.max, None, P0k, [rnk])
                cp(gbuf[:], P1[:, 0:NW], P1k, [gbk], eng="act")
                tt(f_[:], k_[:], f_[:], ALU.mult, [kk_, fk], [fk], eng="pool")
                yield
                actf(egm[:], egm[:], AF.Exp, [egmk], [egmk], scale=CDEC)
                actf(rn[:], rn[:], AF.Ln, [rnk], [rnk])
                tt(rt[:], r_[:], eg[:], ALU.mult, [rk_, egk], [nk[3]])
                tt(kt[:], f_[:], egi[:], ALU.mult, [fk, egik], [nk[1]])
                cp(gam[:], eg[:, C - 1:NW:C], [egk], [gamk], eng="pool")
                tt(cs_[:], r_[:], f_[:], ALU.mult, [rk_, fk, csk], [csk], eng="pool")
                cp(vb[:], v_[:], [vk_], [nk[4]], eng="pool")
                yield
                actf(rn[:], rn[:], AF.Exp, [rnk], [rnk], scale=-0.5)
                for hp, (cs, ws_) in enumerate(hsl):
                    ts(r32(SQ[:, ws_]), cs_[:, ws_], pcol("rk", hp), None, ALU.mult, None, [csk, "pvec"], ["SQ"])
                yield
                tt(kk[:], kk[:], rn[:], ALU.mult, [kkk, rnk], [kkk])
                for hp, (cs, ws_) in enumerate(hsl):
                    mm(P0[:, ws_], r32(blkr[:]), r32(SQ[:, ws_]), True, True, ["blkr", "SQ"], P0k)
                yield
                tt(sgy[:], kk[:], a_[:], ALU.mult, [kkk, ak_], [sgyk])
                stt(at[:], kk[:], -1.0, egm[:], ALU.mult, ALU.mult, [kkk, egmk], [nk[2]])
                tt(bon[:], P0[:, 0:NW], v_[:], ALU.mult, P0k + [vk_], [bonk])
                yield
                tt(bt[:], sgy[:], egi[:], ALU.mult, [sgyk, egik], [nk[0]])
                yield

            def preB(ti, par):
                natb, nk, gam = natbs[par], nks[par], gams[par]
                gamk = ("gam", par)
                p3 = ti % 3
                opF, TM, gamF = opFs[p3], TMs[p3], gamFs[p3]
                opk, gfk = ("opF", p3), ("gamF", p3)
                bt, kt, at, rt, vb = natb
                for e_ in range(2):
                    for j, src in enumerate((bt, kt, at, rt)):
                        cp(opF[:, e_:8:2, :, j, :], src[e_ * 64:(e_ + 1) * 64, :].rearrange("p (h c t) -> p h c t", h=4, c=NCH),
                           [nk[j]], [opk], eng="act")
                    cp(gamF[:, e_:8:2, :], gam[e_ * 64:(e_ + 1) * 64, :].rearrange("p (h c) -> p h c", h=4), [gamk], [gfk], eng="pool")
                gB = gam[:, :].unsqueeze(2).to_broadcast([128, 4 * NCH, C])
                kts, bts = SCB
                tt(kts[:, :].rearrange("p (c t) -> p c t", t=C), kt[:, :].rearrange("p (c t) -> p c t", t=C), gB, ALU.mult,
                   [nk[1], gamk], ["kts"], eng="pool")
                tt(bts[:, :].rearrange("p (c t) -> p c t", t=C), bt[:, :].rearrange("p (c t) -> p c t", t=C), gB, ALU.mult,
                   [nk[0], gamk], ["bts"], eng="pool")
                yield
                for hp in range(4):
                    for c2 in range(0, NCH, 2):
                        ptr, ptrk = psb_next()
                        for cc in range(2):
                            c = c2 + cc
                            for j, (src, skey) in enumerate(((vb, nk[4]), (kts, "kts"), (bts, "bts"))):
                                col = hp * WT + c * C
                                tr(ptr[0:64, (cc * 3 + j) * 128:(cc * 3 + j + 1) * 128], src[:, col:col + C], identb[:],
                                   [skey, "identb"], [ptrk])
                        cp(TM[hp][:, c2:c2 + 2, :, :], ptr[0:64, 0:768].rearrange("p (c j f) -> p c j f", c=2, j=3), [ptrk], [("TM", p3, hp)])
                yield

            HPG = 8 // NCH
            NGRP = 8 // HPG
            NP_ = HPG * NCH
            v3 = lambda t: t[0:64, :].rearrange("p (c t) -> p c t", c=NP_)

            def tinv_gen(grp, slot, ti):
                par = ti % 2
                opF, Mb, TinvT = opFs[ti % 3], Mbs[par], TinvTs[par]
                opk = ("opF", ti % 3)
                Y, X, Yn, Xn, Q, Qn = TVB[slot]
                kY, kX, kYn, kXn, kQ, kQn = [("tv", slot, i) for i in range(6)]
                for hh in range(HPG):
                    h = HPG * grp + hh
                    pt, pk0 = psn("T")
                    pk = [pk0]
                    for c in range(NCH):
                        ar = opF[:, h, c, 2:4, :]
                        mm(pt[0:64, c * 256:c * 256 + 128], opF[:, h, c, 0, :], ar, True, True, [opk], pk)
                        mm(pt[0:64, c * 256 + 128:c * 256 + 256], opF[:, h, c, 1, :], ar, True, True, [opk], pk)
                    pv3 = pt[0:64, 0:NCH * 256].rearrange("p (a b) -> p a b", a=NCH)
                    tt(r32(v3(Y)[:, hh * NCH:(hh + 1) * NCH, :]), pv3[:, :, 0:64],
                       mk2[:, 0:64].unsqueeze(1).to_broadcast([64, NCH, 64]), ALU.mult, pk + ["mk2"], [kY])
                    tt(Mb[:, h * NCH:(h + 1) * NCH, :], pv3[:, :, 64:256],
                       mk2[:, 64:256].unsqueeze(1).to_broadcast([64, NCH, 192]), ALU.mult, pk + ["mk2"], [("Mb", par, h)])
                yield
                if ti >= T // WT:
                    cp(TinvT[:, grp * HPG:(grp + 1) * HPG, :, :].rearrange("p h c t -> p (h c) t"),
                       identf[0:64, 0:64].unsqueeze(1).to_broadcast([64, NP_, 64]), ["identf"], [("TinvT", par, grp)], eng="pool")
                    yield
                    return
                pt, pk = psn("T")
                for c in range(NP_):
                    tr(pt[0:64, c * 64:(c + 1) * 64], Y[0:64, c * 64:(c + 1) * 64], identf[0:64, 0:64], [kY, "identf"], [pk])
                cp(r32(X[0:64, :]), pt[0:64, 0:512], [pk], [kX])
                tt(r32(v3(Q)), v3(Y), identf[0:64, 0:64].unsqueeze(1).to_broadcast([64, NP_, 64]), ALU.add, [kY, "identf"], [kQ])
                yield
                for lvl in range(5):
                    pt, pk = psn("T")
                    for c in range(NP_):
                        sl_ = slice(c * 64, (c + 1) * 64)
                        mm(pt[0:64, sl_], r32(Y[0:64, sl_]), r32(X[0:64, sl_]), True, True, [kY, kX], [pk])
                    cp(r32(Xn[0:64, :]), pt[0:64, 0:512], [pk], [kXn], eng="act")
                    if lvl < 4:
                        pt2, pk2 = psn("T")
                        for c in range(NP_):
                            sl_ = slice(c * 64, (c + 1) * 64)
                            mm(pt2[0:64, sl_], r32(X[0:64, sl_]), r32(Y[0:64, sl_]), True, True, [kY, kX], [pk2])
                        cp(r32(Yn[0:64, :]), pt2[0:64, 0:512], [pk2], [kYn])
                    yield
                    pt3, pk3 = psn("T")
                    for c in range(NP_):
                        sl_ = slice(c * 64, (c + 1) * 64)
                        mm(pt3[0:64, sl_], r32(Xn[0:64, sl_]), r32(Q[0:64, sl_]), True, True, [kXn, kQ], [pk3])
                    tt(r32(Qn[0:64, :]), pt3[0:64, 0:512], Q[0:64, :], ALU.add, [pk3, kQ], [kQn])
                    Y, Yn, kY, kYn = Yn, Y, kYn, kY
                    X, Xn, kX, kXn = Xn, X, kXn, kX
                    Q, Qn, kQ, kQn = Qn, Q, kQn, kQ
                    yield
                cp(TinvT[:, grp * HPG:(grp + 1) * HPG, :, :], Q[0:64, :].rearrange("p (h c t) -> p h c t", h=HPG, c=NCH),
                   [kQ], [("TinvT", par, grp)], eng="pool")
                yield


            def tinv_pair_gen(ti):
                par = ti % 2
                opF, Mb, TinvT = opFs[ti % 3], Mbs[par], TinvTs[par]
                opk = ("opF", ti % 3)
                PBS = (0, 64)
                B = [list(TVB[0]), list(TVB[1])]
                K = [[("tv", s_, i) for i in range(6)] for s_ in range(2)]
                rows = lambda t, s_: t[PBS[s_]:PBS[s_] + 64, :]
                v3r = lambda t, s_: rows(t, s_).rearrange("p (c t) -> p c t", c=NP_)
                for s_ in range(2):
                    Y, kY = B[s_][0], K[s_][0]
                    for hh in range(HPG):
                        h = HPG * s_ + hh
                        pt, pk0 = psn("T")
                        pk = [pk0]
                        for c in range(NCH):
                            ar = opF[:, h, c, 2:4, :]
                            mm(pt[0:64, c * 256:c * 256 + 128], opF[:, h, c, 0, :], ar, True, True, [opk], pk)
                            mm(pt[0:64, c * 256 + 128:c * 256 + 256], opF[:, h, c, 1, :], ar, True, True, [opk], pk)
                        pv3 = pt[0:64, 0:NCH * 256].rearrange("p (a b) -> p a b", a=NCH)
                        tt(r32(v3r(Y, s_)[:, hh * NCH:(hh + 1) * NCH, :]), pv3[:, :, 0:64],
                           mk2[:, 0:64].unsqueeze(1).to_broadcast([64, NCH, 64]), ALU.mult, pk + ["mk2"], [kY])
                        tt(Mb[:, h * NCH:(h + 1) * NCH, :], pv3[:, :, 64:256],
                           mk2[:, 64:256].unsqueeze(1).to_broadcast([64, NCH, 192]), ALU.mult, pk + ["mk2"], [("Mb", par, h)])
                    yield
                pts = [psn("T") for _ in range(2)]
                for c in range(NP_):
                    for s_ in range(2):
                        pb_ = PBS[s_]
                        tr(pts[s_][0][0:64, c * 64:(c + 1) * 64], B[s_][0][pb_:pb_ + 64, c * 64:(c + 1) * 64],
                           identf[pb_:pb_ + 64, pb_:pb_ + 64], [K[s_][0], "identf"], [pts[s_][1]])
                for s_ in range(2):
                    pb_ = PBS[s_]
                    cp(r32(rows(B[s_][1], s_)), pts[s_][0][0:64, 0:512], [pts[s_][1]], [K[s_][1]])
                    tt(r32(v3r(B[s_][4], s_)), v3r(B[s_][0], s_),
                       identf[pb_:pb_ + 64, pb_:pb_ + 64].unsqueeze(1).to_broadcast([64, NP_, 64]), ALU.add,
                       [K[s_][0], "identf"], [K[s_][4]])
                yield
                for lvl in range(5):
                    pts = [psn("T") for _ in range(2)]
                    for c in range(NP_):
                        sl_ = slice(c * 64, (c + 1) * 64)
                        for s_ in range(2):
                            Y, X = B[s_][0], B[s_][1]
                            mm(pts[s_][0][0:64, sl_], r32(rows(Y, s_)[:, sl_]), r32(rows(X, s_)[:, sl_]), True, True,
                               [K[s_][0], K[s_][1]], [pts[s_][1]])
                    for s_ in range(2):
                        cp(r32(rows(B[s_][3], s_)), pts[s_][0][0:64, 0:512], [pts[s_][1]], [K[s_][3]], eng="act")
                    if lvl < 4:
                        pts2 = [psn("T") for _ in range(2)]
                        for c in range(NP_):
                            sl_ = slice(c * 64, (c + 1) * 64)
                            for s_ in range(2):
                                Y, X = B[s_][0], B[s_][1]
                                mm(pts2[s_][0][0:64, sl_], r32(rows(X, s_)[:, sl_]), r32(rows(Y, s_)[:, sl_]), True, True,
                                   [K[s_][0], K[s_][1]], [pts2[s_][1]])
                        for s_ in range(2):
                            cp(r32(rows(B[s_][2], s_)), pts2[s_][0][0:64, 0:512], [pts2[s_][1]], [K[s_][2]])
                    yield
                    pts3 = [psn("T") for _ in range(2)]
                    for c in range(NP_):
                        sl_ = slice(c * 64, (c + 1) * 64)
                        for s_ in range(2):
                            Xn, Q = B[s_][3], B[s_][4]
                            mm(pts3[s_][0][0:64, sl_], r32(rows(Xn, s_)[:, sl_]), r32(rows(Q, s_)[:, sl_]), True, True,
                               [K[s_][3], K[s_][4]], [pts3[s_][1]])
                    for s_ in range(2):
                        Q, Qn = B[s_][4], B[s_][5]
                        if s_ == 0:
                            tt(r32(rows(Qn, 0)), pts3[0][0][0:64, 0:512], rows(Q, 0), ALU.add, [pts3[0][1], K[0][4]], [K[0][5]])
                        else:
                            cp(rows(SHB, 1), pts3[1][0][0:64, 0:512], [pts3[1][1]], ["shb"], eng="act")
                            tt(r32(rows(Qn, 1)), rows(SHB, 1), rows(Q, 1), ALU.add, ["shb", K[1][4]], [K[1][5]])
                    for s_ in range(2):
                        b_, k_ = B[s_], K[s_]
                        b_[0], b_[2], k_[0], k_[2] = b_[2], b_[0], k_[2], k_[0]
                        b_[1], b_[3], k_[1], k_[3] = b_[3], b_[1], k_[3], k_[1]
                        b_[4], b_[5], k_[4], k_[5] = b_[5], b_[4], k_[5], k_[4]
                    yield
                for s_ in range(2):
                    cp(TinvT[:, s_ * HPG:(s_ + 1) * HPG, :, :], rows(B[s_][4], s_).rearrange("p (h c t) -> p h c t", h=HPG, c=NCH),
                       [K[s_][4]], [("TinvT", par, s_)], eng="act")
                yield

            def scan_gn(ti, par):
                samp = ti >= T // WT
                sbase = (ti - T // WT) * NCH
                c0 = ti * WT
                gbuf, bon = gbufs[ti % 3], bons[ti % 3]
                gbk, bonk = ("gbuf", ti % 3), ("bon", ti % 3)
                p3 = ti % 3
                opF, TM, gamF, Mb, TinvT = opFs[p3], TMs[p3], gamFs[p3], Mbs[par], TinvTs[par]
                opk, gfk = ("opF", p3), ("gamF", p3)
                for c in range(NCH):
                    if samp:
                        dmas(s0in[:], swkv[sbase + c].rearrange("h v k -> v h k"), [], ["s0in"])
                        pt, pk = psn("S")
                        for h in range(8):
                            tr(pt[0:64, h * 64:(h + 1) * 64], s0in[:, h, :], identf[0:64, 0:64], ["s0in", "identf"], [pk])
                        cp(Hf[:], pt[0:64, 0:512].rearrange("p (h v) -> p h v", h=8), [pk], ["Hf"])
                        cp(Hb[:], Hf[:], ["Hf"], ["Hb"], eng="act")
                    Vh = lambda h: TM[h // 2][:, c, 0, (h % 2) * 64:(h % 2) * 64 + 64]
                    Kh = lambda h: TM[h // 2][:, c, 1, (h % 2) * 64:(h % 2) * 64 + 64]
                    Bh = lambda h: TM[h // 2][:, c, 2, (h % 2) * 64:(h % 2) * 64 + 64]
                    Mrb = lambda h: Mb[:, h * NCH + c, 0:64]
                    Mak = lambda h: Mb[:, h * NCH + c, 64:128]
                    Mrk = lambda h: Mb[:, h * NCH + c, 128:192]
                    tt(HG[:], Hf[:], gamF[:, :, c:c + 1].to_broadcast([64, 8, 64]), ALU.mult, ["Hf", gfk], ["HG"], eng="pool")
                    pw_, pwk_ = psn("S")
                    for h in range(8):
                        o_ = pw_[0:64, h * 64:(h + 1) * 64]
                        mm(o_, opF[:, h, c, 2, :], Hb[:, h, :], True, False, [opk, "Hb"], [pwk_])
                        mm(o_, Mak(h), Vh(h), False, True, [("Mb", par, h), ("TM", p3, h // 2)], [pwk_])
                    cp(Wsb[:], pw_[0:64, 0:512].rearrange("p (h v) -> p h v", h=8), [pwk_], ["Wsb"], eng="act")
                    yield
                    pu_, puk_ = psn("S")
                    for h in range(8):
                        mm(pu_[0:64, h * 64:(h + 1) * 64], TinvT[:, h, c, :], Wsb[:, h, :], True, True, [("TinvT", par, h // HPG), "Wsb"], [puk_])
                    cp(Usb[:], pu_[0:64, 0:512].rearrange("p (h v) -> p h v", h=8), [puk_], ["Usb"])
                    yield
                    py_, pyk_ = psn("S")
                    for h in range(8):
                        o_ = py_[0:64, h * 64:(h + 1) * 64]
                        mm(o_, Hb[:, h, :], opF[:, h, c, 3, :], True, False, ["Hb", opk], [pyk_])
                        mm(o_, Usb[:, h, :], Mrb(h), False, False, ["Usb", ("Mb", par, h)], [pyk_])
                        mm(o_, Vh(h), Mrk(h), False, True, [("TM", p3, h // 2), ("Mb", par, h)], [pyk_])
                    pyv = py_[0:64, 0:512].rearrange("p (h e t) -> p h e t", h=4, e=2)
                    for e_ in range(2):
                        cp(r32(w4(ynat)[e_ * 64:(e_ + 1) * 64, :, c * C:(c + 1) * C]), pyv[:, :, e_, :], [pyk_], ["ynat"], eng="act")
                    ph_, phk_ = psn("S")
                    for h in range(8):
                        o_ = ph_[0:64, h * 64:(h + 1) * 64]
                        mm(o_, Bh(h), Usb[:, h, :], True, False, [("TM", p3, h // 2), "Usb"], [phk_])
                        mm(o_, Kh(h), Vh(h), False, True, [("TM", p3, h // 2)], [phk_])
                    phv = ph_[0:64, 0:512].rearrange("p (h v) -> p h v", h=8)
                    tt(Hb[:], phv, HG[:], ALU.add, [phk_, "HG"], ["Hb"])
                    tt(Hf[:], phv, HG[:], ALU.add, [phk_, "HG"], ["Hf"])
                    yield
                    if samp:
                        emit_state_out(wkv_s[sbase + c].rearrange("h v k -> v h k"))
                if ti == T // WT - 1:
                    emit_state_out(wkv_p.rearrange("h v k -> v h k"))
                ync, ysq = GNB
                ynk, ysk = "gny", "gnq"
                SQ = SQ2
                pm, pmk0 = psn("S")
                pmk = [pmk0]
                for hp in range(4):
                    ws_ = slice(hp * WT, (hp + 1) * WT)
                    mm(pm[:, ws_], r32(blkr[:]), r32(ynat[:, ws_]), True, True, ["blkr", "ynat"], pmk)
                stt(ync[:], pm[:, 0:NW], -1.0 / 64, ynat[:], ALU.mult, ALU.add, pmk + ["ynat"], [ynk])
                tt(r32(SQ[:]), ync[:], ync[:], ALU.mult, [ynk], ["SQ2"])
                yield
                pv_, pvk0 = psn("S")
                pvk_ = [pvk0]
                for hp in range(4):
                    ws_ = slice(hp * WT, (hp + 1) * WT)
                    mm(pv_[:, ws_], r32(blkr[:]), r32(SQ[:, ws_]), True, True, ["blkr", "SQ2"], pvk_)
                ts(ysq[:], pv_[:, 0:NW], 1.0 / 64, 64e-5, ALU.mult, ALU.add, pvk_, [ysk])
                yield
                actf(ysq[:], ysq[:], AF.Ln, [ysk], [ysk])
                actf(ysq[:], ysq[:], AF.Exp, [ysk], [ysk], scale=-0.5)
                tt(ync[:], ync[:], ysq[:], ALU.mult, [ynk, ysk], [ynk])
                for hp in range(4):
                    ws_ = slice(hp * WT, (hp + 1) * WT)
                    ts(ync[:, ws_], ync[:, ws_], pcol("gng", hp), pcol("gnb", hp), ALU.mult, ALU.add, [ynk, "pvec"], [ynk])
                yield
                tt(ync[:], ync[:], bon[:], ALU.add, [ynk, bonk], [ynk])
                if not samp:
                    tt(rwkvT[:, :, c0:c0 + WT], w4(ync), w4(gbuf), ALU.mult, [ynk, gbk], ["rwkvT"])
                else:
                    rwt = SCB[0]
                    tt(rwt[:], ync[:], gbuf[:], ALU.mult, [ynk, gbk], ["kts"])
                    cp(rwkvT[:, :, T + sbase:T + sbase + NCH], w4(rwt)[:, :, 0:WT:C], ["kts"], ["rwkvT"], eng="pool")
                yield

            def drain(g_):
                for _ in g_:
                    pass

            def rr(gens):
                alive = True
                while alive:
                    alive = False
                    for g_ in gens:
                        try:
                            next(g_)
                            alive = True
                        except StopIteration:
                            pass

            def tgens(ti):
                if NGRP == 2 and "Q" in BSKIP:
                    return [tinv_pair_gen(ti)]
                return [tinv_gen(g_, g_ % 2, ti) for g_ in range(NGRP)]

            def pgen(ti):
                yield from preA(ti, ti % 2)
                yield from preB(ti, ti % 2)

            drain(pgen(0))
            rr(tgens(0) + ([pgen(1)] if NTILES > 1 else []))
            for ti in range(NTILES):
                gens = [scan_gn(ti, ti % 2)]
                if ti + 1 < NTILES:
                    gens += tgens(ti + 1)
                if ti + 2 < NTILES:
                    gens.append(pgen(ti + 2))
                if "S" in BSKIP:
                    for g_ in gens:
                        drain(g_)
                else:
                    rr(gens)
            pslim[0] = 7
        P.barrier()

    if "rwkvT" in debug:
        dbg_out["rwkvT"] = (rwkvT, [128, 4 * NT], BF16)
    if "attnT" in debug:
        dbg_out["attnT"] = (attnT, [128, 2 * NT], BF16)


    mergedT = P.sbuf([128, 8, NT], BF16)
    if "D" in phases:
        with contextlib.ExitStack() as st:
            hT = P.sbuf([128, 8, NT], BF16, st)
            hsv = hscr.rearrange("p (k t) -> p k t", k=8)
            for n_ in range(5):
                cs_ = slice(n_ * 512, (n_ + 1) * 512) if n_ < 4 else slice(T, NT)
                dmas(hT[:, :, cs_], hsv[:, :, cs_], [], [("hT", n_, k_) for k_ in range(8)])
            gA = [P.sbuf([128, 512], F32, st) for _ in range(2)]
            gB = [P.sbuf([128, 512], F32, st) for _ in range(2)]
            t1b = [P.sbuf([128, 512], F32, st) for _ in range(2)]
            t2b = [P.sbuf([128, 512], F32, st) for _ in range(2)]
            it = 0
            ws = WS(st, 3, 512, [[(w_pa[:, m_ * 128:(m_ + 1) * 128], 2, 0, 128), (w_pb[:, m_ * 128:(m_ + 1) * 128], 4, 128, 128),
                                  (w_in[:, 4096 + m_ * 128:4096 + (m_ + 1) * 128], 8, 256, 128),
                                  (w_in[:, 5120 + m_ * 128:5120 + (m_ + 1) * 128], 8, 384, 128)] for m_ in range(8)])
            for m in range(8):
                wt, wk = ws.get()
                for n in range(5):
                    c0, w_ = (n * 512, 512) if n < 4 else (T, NS)
                    hk = [("hT", n, k) for k in range(8)]
                    p1, p1k = ps_next()
                    for k in range(2):
                        mm(p1[:, 0:w_], wt[:, k, 0:128], attnT[:, k, c0:c0 + w_], k == 0, k == 1, [wk], [p1k])
                    p2, p2k = ps_next()
                    for k in range(4):
                        mm(p2[:, 0:w_], wt[:, k, 128:256], rwkvT[:, k, c0:c0 + w_], k == 0, k == 3, [wk], [p2k])
                    p3, p3k = ps_next()
                    for k in range(8):
                        mm(p3[:, 0:w_], wt[:, k, 256:384], hT[:, k, c0:c0 + w_], k == 0, k == 7, [wk, ("hT", n, k)], [p3k])
                    p4, p4k = ps_next()
                    for k in range(8):
                        mm(p4[:, 0:w_], wt[:, k, 384:512], hT[:, k, c0:c0 + w_], k == 0, k == 7, [wk, ("hT", n, k)], [p4k])
                    b_ = it % 2
                    it += 1
                    actf(gA[b_][:, 0:w_], p3[:, 0:w_], AF.Sigmoid, [p3k, "pvec"], [("gA", b_)], bias=pcol("bgate", m))
                    actf(gB[b_][:, 0:w_], p4[:, 0:w_], AF.Sigmoid, [p4k, "pvec"], [("gB", b_)], bias=pcol("bgate", 8 + m))
                    tt(t1b[b_][:, 0:w_], p1[:, 0:w_], gA[b_][:, 0:w_], ALU.mult, [p1k, ("gA", b_)], [("t1b", b_)])
                    tt(t2b[b_][:, 0:w_], p2[:, 0:w_], gB[b_][:, 0:w_], ALU.mult, [p2k, ("gB", b_)], [("t2b", b_)])
                    tt(mergedT[:, m, c0:c0 + w_], t1b[b_][:, 0:w_], t2b[b_][:, 0:w_], ALU.add, [("t1b", b_), ("t2b", b_)],
                       [("mg", n, m)])
        P.barrier()

    if "D" in phases:
        with contextlib.ExitStack() as st:
            gfin = P.sbuf([128, D], F32, st)
            dmas(gfin[:], normf_d.partition_broadcast(128), [], ["gfin"])
            x2s = [[P.sbuf([128, D], F32, st) for _ in range(5)] for _ in range(2)]
            junk2 = P.sbuf([128, D], BF16, st)
            hmb = [P.sbuf([128, D], BF16, st) for _ in range(2)]
            hmTs = [P.sbuf([128, 8, 516], BF16, st) for _ in range(2)]
            uT = P.sbuf([128, 32, 516], BF16, st)
            rl = [P.sbuf([128, 512], F32, st) for _ in range(2)]
            yst = [P.sbuf([128, D], F32, st) for _ in range(2)]
            st2 = P.sbuf([128, 128], F32, st)
            reqs = [[(w_out[:, h_ * 512:(h_ + 1) * 512], 8, 0, 512)] for h_ in range(2)]
            for n_ in range(4):
                reqs += [[(w_up[:, mb_ * 512:(mb_ + 1) * 512], 8, 0, 512)] for mb_ in range(8)]
                if n_ < 3:
                    reqs += [[(w_out[:, h_ * 512:(h_ + 1) * 512], 8, 0, 512)] for h_ in range(2)]
                reqs += [[(w_down[kb_ * 1024:(kb_ + 1) * 1024, h_ * 512:(h_ + 1) * 512], 8, 0, 512)] for h_ in range(2) for kb_ in range(4)]
            ws = WS(st, 4, 512, reqs)

            def subs_of(n):
                return [(j, 128, n * 512 + j * 128) for j in range(4)] + ([(4, NS, T)] if n == 3 else [])

            def WN(n, psum_fn):
                par = n % 2
                x2, hmT = x2s[par], hmTs[par]
                subs = subs_of(n)
                for (j, rows, t0) in subs:
                    src = xp[t0:t0 + rows, :] if j < 4 else xs
                    dmas(x2[j][0:rows, :], src, [], [("x2", par, j, 0), ("x2", par, j, 1)])
                for half in range(2):
                    wt, wk = ws.get()
                    for (j, rows, t0) in subs:
                        pt, pk = psum_fn()
                        for k in range(8):
                            mm(pt[0:rows, 0:512], mergedT[:, k, t0:t0 + rows], wt[:, k, 0:512], k == 0, k == 7, [wk], [pk])
                        xv = x2[j][0:rows, half * 512:(half + 1) * 512]
                        tt(xv, pt[0:rows, 0:512], xv, ALU.add, [pk, ("x2", par, j, half)], [("x2", par, j, half)])
                yield
                sl = []
                for (j, rows, t0) in subs:
                    sk = ("st2", par, j)
                    s0, s1, s2 = [st2[0:rows, 64 * par + 8 * j + i_:64 * par + 8 * j + i_ + 1] for i_ in range(3)]
                    sl.append((j, rows, sk, s0, s1, s2, [("x2", par, j, 0), ("x2", par, j, 1)]))
                for (j, rows, sk, s0, s1, s2, xk) in sl:
                    actf(junk2[0:rows, :], x2[j][0:rows, :], AF.Square, xk, ["junk2", sk], accum=s0)
                yield
                for (j, rows, sk, s0, s1, s2, xk) in sl:
                    ts(s1, s0, 1.0 / D, 1e-6, ALU.mult, ALU.add, [sk], [sk])
                for (j, rows, sk, s0, s1, s2, xk) in sl:
                    actf(s1, s1, AF.Sqrt, [sk], [sk])
                yield
                for (j, rows, sk, s0, s1, s2, xk) in sl:
                    P.dve(lambda e, s1=s1, s2=s2: e.reciprocal(out=s2, in_=s1), [sk], [sk])
                yield
                for (j, rows, t0) in subs:
                    (_, _, sk, s0, s1, s2, xk) = sl[j]
                    hb, hbk = hmb[j % 2], ("hmb", j % 2)
                    ts(hb[0:rows, :], x2[j][0:rows, :], s2, None, ALU.mult, None, xk + [sk], [hbk])
                    yield
                    cdst = j * 128 if j < 4 else 512
                    for k4 in range(2):
                        ptb_, ptbk = psb_next()
                        for kk_ in range(4):
                            k = 4 * k4 + kk_
                            tr(ptb_[:, kk_ * 128:kk_ * 128 + rows], hb[0:rows, k * 128:(k + 1) * 128], identb[0:rows, 0:rows],
                               [hbk, "identb"], [ptbk])
                        for kk_ in range(4):
                            k = 4 * k4 + kk_
                            actf(hmT[:, k, cdst:cdst + rows], ptb_[:, kk_ * 128:kk_ * 128 + rows], AF.Copy, [ptbk, "pvec"],
                                 [("hmT", par, k, j)], scale=pcol("norm2", k))
                        yield

            for _ in WN(0, ps_next):
                pass
            for n in range(4):
                par = n % 2
                x2, hmT = x2s[par], hmTs[par]
                subs = subs_of(n)
                ri = 0
                for mb in range(8):
                    wt, wk = ws.get()
                    for jj in range(4):
                        m = 4 * mb + jj
                        pieces = [(0, 512)] + ([(512, NS)] if n == 3 else [])
                        for (cc0, w_) in pieces:
                            pt, pk = ps_next()
                            for k in range(8):
                                mm(pt[:, 0:w_], wt[:, k, jj * 128:(jj + 1) * 128], hmT[:, k, cc0:cc0 + w_], k == 0, k == 7,
                                   [wk] + [("hmT", par, k, j_) for j_ in range(5)], [pk])
                            rb, rbk = rl[ri % 2], ("rl", ri % 2)
                            ri += 1
                            actf(rb[:, 0:w_], pt[:, 0:w_], AF.Relu, [pk], [rbk])
                            tt(uT[:, m, cc0:cc0 + w_], rb[:, 0:w_], rb[:, 0:w_], ALU.mult, [rbk], [("uT", m)])
                nxt = WN(n + 1, ps_next) if n + 1 < 4 else iter(())
                next(nxt, None)
                if "W" in BSKIP:
                    for _ in nxt:
                        pass
                for half in range(2):
                    for kb in range(4):
                        wt, wk = ws.get()
                        for (j, rows, t0) in subs:
                            cdst = j * 128 if j < 4 else 512
                            for k in range(8):
                                mm(psf[j][0:rows, 0:512], uT[:, kb * 8 + k, cdst:cdst + rows], wt[:, k, 0:512],
                                   kb == 0 and k == 0, kb == 3 and k == 7, [wk, ("uT", kb * 8 + k)], [("psf", j)])
                            next(nxt, None)
                    for (j, rows, t0) in subs:
                        xv = x2[j][0:rows, half * 512:(half + 1) * 512]
                        tt(xv, psf[j][0:rows, 0:512], xv, ALU.add, [("psf", j), ("x2", par, j, half)], [("x2", par, j, half)])
                for _ in nxt:
                    pass
                pctr[0] = 5
                fl = []
                for (j, rows, t0) in subs:
                    s0, s1, s2 = [st2[0:rows, 8 * j + 3 + i_:8 * j + 4 + i_] for i_ in range(3)]
                    fl.append((j, rows, t0, ("st2f", j), s0, s1, s2, [("x2", par, j, 0), ("x2", par, j, 1)]))
                for (j, rows, t0, sk, s0, s1, s2, xk) in fl:
                    actf(junk2[0:rows, :], x2[j][0:rows, :], AF.Square, xk, ["junk2", sk], accum=s0)
                for (j, rows, t0, sk, s0, s1, s2, xk) in fl:
                    ts(s1, s0, 1.0 / D, 1e-6, ALU.mult, ALU.add, [sk], [sk])
                for (j, rows, t0, sk, s0, s1, s2, xk) in fl:
                    actf(s1, s1, AF.Sqrt, [sk], [sk])
                for (j, rows, t0, sk, s0, s1, s2, xk) in fl:
                    P.dve(lambda e, s1=s1, s2=s2: e.reciprocal(out=s2, in_=s1), [sk], [sk])
                for (j, rows, t0, sk, s0, s1, s2, xk) in fl:
                    yb, ybk = yst[j % 2], ("yst", j % 2)
                    stt(yb[0:rows, :], x2[j][0:rows, :], s2, gfin[0:rows, :], ALU.mult, ALU.mult, xk + [sk, "gfin"], [ybk])
                    dst = y_p[t0:t0 + rows, :] if j < 4 else y_s
                    dmas(dst, yb[0:rows, :], [ybk], [])
        P.barrier()

    for name, (tile_, shp, dt_) in dbg_out.items():
        tmp = P.sbuf(shp, F32)
        do = dout("dbg_" + name, shp)
        cp(tmp[:], tile_[:].rearrange("p a b -> p (a b)") if len(tile_.shape) == 3 else tile_[:], [], ["dbgtmp" + name])
        dmas(do, tmp[:], ["dbgtmp" + name], [])

    P.emit()
    return nc, P


def _consts():
    kj = np.arange(128)[:, None]
    qc = np.arange(256)[None, :]
    delta = qc - kj
    valid = (delta >= 0) & (delta <= 128)
    emat = np.zeros((128, 12, 256), np.float32)
    for h in range(12):
        dil = DILS[h // 4]
        e = np.exp(-np.float64(np.float32(SLOPES[h])) * (delta * dil).astype(np.float64))
        emat[:, h, :] = np.where(valid, e, 0.0).astype(np.float32)
    s = np.arange(64)[:, None]
    t = np.arange(64)[None, :]
    strict = (s < t).astype(np.float32)
    incl = (s <= t).astype(np.float32)
    mk2 = np.concatenate([strict, incl, strict, incl], axis=1)
    ident = np.eye(128, dtype=np.float32)
    blk = np.zeros((128, 128), np.float32)
    blk[:64, :64] = 1.0
    blk[64:, 64:] = 1.0
    return emat, mk2, ident, blk


def _pvec(inp):
    def fm(v):
        v = np.asarray(v, np.float32).reshape(-1, 128)
        return np.ascontiguousarray(v.T)
    cols = [fm(inp["norm1_g"][0]), fm(inp["norm2_g"][0]), fm(inp["b_gate"][0]), fm(inp["mu_shift"][0]),
            fm(inp["w0"][0]), fm(inp["a0"][0]), fm(inp["k_k"][0]), fm(inp["k_a"][0]), fm(inp["r_k"][0].reshape(-1)),
            fm(inp["gn_g"][0]), fm(inp["gn_b"][0]), np.zeros((128, 4), np.float32)]
    return np.ascontiguousarray(np.concatenate(cols, axis=1))


_CACHE = {}


def make_in_maps(inp):
    emat, mk2, ident, blk = _consts()
    pv = _pvec(inp)
    f = lambda a: np.ascontiguousarray(np.asarray(a, np.float32))
    shared = dict(
        w_in=f(inp["w_in"][0]), w_proj_a=f(inp["w_proj_a"][0]), w_proj_b=f(inp["w_proj_b"][0]), w_out=f(inp["w_out"][0]),
        w_up=f(inp["w_up"][0]), w_down=f(inp["w_down"][0]), w_lora_up=f(inp["w_lora_up"][0]), a_lora_up=f(inp["a_lora_up"][0]),
        g_lora_up=f(inp["g_lora_up"][0]), pvec=pv, normf_g=f(inp["normf_g"]), emat=emat, mk2=mk2, ident=ident, blk=blk)
    maps = []
    for c in range(NCORES):
        sl = slice(c * NS, (c + 1) * NS)
        m = dict(shared)
        m["xp"] = f(inp["x_prompt"][c])
        m["xs"] = f(inp["x_sample"][sl, 0])
        m["c128"] = f(np.asarray(inp["cache_kv_w128"][0][sl]).reshape(NS, 128, 512))
        m["c512"] = f(np.asarray(inp["cache_kv_w512"][0][sl]).reshape(NS, 512, 512))
        m["c2048"] = f(np.asarray(inp["cache_kv_w2048"][0][sl]).reshape(NS, 2048, 512))
        m["swkv"] = f(inp["state_wkv"][0][sl])
        m["sshift"] = f(inp["state_shift"][0][sl])
        maps.append(m)
    return maps


def kernel(**inp):
    if "nc" not in _CACHE:
        _CACHE["nc"] = build_program()
    nc, P = _CACHE["nc"]
    maps = make_in_maps(inp)
    res = run_bass_kernel_spmd(nc, maps, core_ids=list(range(NCORES)))
    R = res.results
    cat = lambda name: np.stack([np.asarray(r[name], np.float32) for r in R])
    y_p = cat("y_p")
    y_s = np.concatenate([np.asarray(r["y_s"], np.float32) for r in R])[:, None, :]
    kv128_p = cat("kv128_p").reshape(1, 8, 128, 2, 4, 64)
    kv512_p = cat("kv512_p").reshape(1, 8, 512, 2, 4, 64)
    kv2048_p = cat("kv2048_p").reshape(1, 8, 2048, 2, 4, 64)
    wkv_p = cat("wkv_p").reshape(1, 8, 8, 64, 64)
    shift_p = cat("shift_p").reshape(1, 8, 1792)
    ks = lambda n: np.concatenate([np.asarray(r[n], np.float32) for r in R]).reshape(1, 32, 1, 2, 4, 64)
    wkv_s = np.concatenate([np.asarray(r["wkv_s"], np.float32) for r in R]).reshape(1, 32, 8, 64, 64)
    shift_s = np.concatenate([np.asarray(r["shift_s"], np.float32) for r in R]).reshape(1, 32, 1792)
    return (y_p, y_s, kv128_p, kv512_p, kv2048_p, wkv_p, shift_p, ks("kv128_s"), ks("kv512_s"), ks("kv2048_s"), wkv_s, shift_s)
```

```python
import contextlib
import os
import math
import numpy as np
import concourse.bass as bass
import concourse.mybir as mybir
from concourse.bass_utils import run_bass_kernel_spmd

F32 = mybir.dt.float32
BF16 = mybir.dt.bfloat16
ALU = mybir.AluOpType
AF = mybir.ActivationFunctionType
AX = mybir.AxisListType
ENGS = ("pe", "act", "dve", "pool", "sp")

T = 2048
NS = 4
NT = T + NS
D = 1024
NCORES = 8
C = 64
WT = 128
NRT = T + NS * C
CDEC = -math.exp(-0.5)
SLOPES = [2.0 ** (-8.0 * (h + 1) / 12.0) for h in range(12)]
DILS = [1, 4, 16]


class Op:
    __slots__ = ("idx", "eng", "fn", "deps", "is_dma", "needed", "sig", "sem", "semval", "prev_dma")

    def __init__(self, idx, eng, fn, is_dma):
        self.idx, self.eng, self.fn, self.is_dma = idx, eng, fn, is_dma
        self.deps = ()
        self.needed = False
        self.sig = self.sem = self.semval = self.prev_dma = None


class Prog:
    def __init__(self, nc, n_dma_sems=40, same_engine_sync=True):
        self.nc = nc
        self.ops = []
        self.last_w = {}
        self.readers = {}
        self.n_dma_sems = n_dma_sems
        self.same_engine_sync = same_engine_sync
        self.stack = contextlib.ExitStack()
        self._n = 0
        self.barrier_deps = ()
        self.barrier_id = 0
        self.eng_barrier = {e: 0 for e in ENGS}
        self.last_on_eng = {e: None for e in ENGS}
        self.dma_ops = []

    def sbuf(self, shape, dtype, st=None):
        self._n += 1
        return (st or self.stack).enter_context(self.nc.sbuf_tensor(f"sb{self._n}", list(shape), dtype))

    def psum(self, shape, dtype=F32, st=None):
        self._n += 1
        return (st or self.stack).enter_context(self.nc.psum_tensor(f"ps{self._n}", list(shape), dtype))

    def barrier(self):
        deps = set(i for i in self.last_on_eng.values() if i is not None)
        deps.update(self.dma_ops)
        self.barrier_deps = tuple(deps)
        self.barrier_id += 1
        self.dma_ops = []
        self.last_w = {}
        self.readers = {}

    def op(self, eng, fn, reads=(), writes=(), dma=False):
        o = Op(len(self.ops), eng, fn, dma)
        deps = set()
        if self.eng_barrier[eng] != self.barrier_id:
            deps.update(self.barrier_deps)
            self.eng_barrier[eng] = self.barrier_id
        for k in reads:
            w = self.last_w.get(k)
            if w is not None:
                deps.add(w)
        for k in writes:
            w = self.last_w.get(k)
            if w is not None:
                deps.add(w)
            deps.update(self.readers.get(k, ()))
        deps.discard(o.idx)
        latest = {}
        keep = set()
        for d_ in deps:
            dop = self.ops[d_]
            if dop.is_dma:
                keep.add(d_)
            elif latest.get(dop.eng, -1) < d_:
                latest[dop.eng] = d_
        keep.update(latest.values())
        o.deps = tuple(sorted(keep))
        for k in reads:
            self.readers.setdefault(k, []).append(o.idx)
        for k in writes:
            self.last_w[k] = o.idx
            self.readers[k] = []
        self.ops.append(o)
        self.last_on_eng[eng] = o.idx
        if dma:
            self.dma_ops.append(o.idx)
        return o

    def pe(self, fn, reads=(), writes=()):
        return self.op("pe", fn, reads, writes)

    def act(self, fn, reads=(), writes=()):
        return self.op("act", fn, reads, writes)

    def dve(self, fn, reads=(), writes=()):
        return self.op("dve", fn, reads, writes)

    def pool(self, fn, reads=(), writes=()):
        return self.op("pool", fn, reads, writes)

    def dma(self, fn, reads=(), writes=(), eng="sp"):
        return self.op(eng, fn, reads, writes, dma=True)

    def emit(self):
        nc, ops = self.nc, self.ops
        for o in ops:
            for d in o.deps:
                dop = ops[d]
                if dop.is_dma:
                    continue
                if dop.eng == o.eng and not o.is_dma and (dop.eng == "pe" or not self.same_engine_sync):
                    continue
                dop.needed = True
        cnt = {e: 0 for e in ENGS}
        for o in ops:
            if (not o.is_dma) and o.needed:
                cnt[o.eng] += 1
                o.sig = cnt[o.eng]
        ndma = 0
        dma_cnt = [0] * self.n_dma_sems
        last_on_sem = [None] * self.n_dma_sems
        for o in ops:
            if o.is_dma:
                s = ndma % self.n_dma_sems
                ndma += 1
                dma_cnt[s] += 16
                o.sem, o.semval, o.prev_dma = s, dma_cnt[s], last_on_sem[s]
                last_on_sem[s] = o.idx
        st = self.stack
        esem = {e: st.enter_context(nc.semaphore(f"s_{e}")) for e in ENGS}
        dsem = [st.enter_context(nc.semaphore(f"s_dma{i}")) for i in range(self.n_dma_sems)]
        per_eng = {e: [o for o in ops if o.eng == e] for e in ENGS}
        all_dma = [o for o in ops if o.is_dma]
        same = self.same_engine_sync

        def run(engname, eng):
            waited = {}

            def wait(key, semh, val):
                if waited.get(key, 0) >= val:
                    return
                eng.wait_ge(semh, val)
                waited[key] = val

            for o in per_eng[engname]:
                if o.is_dma and o.prev_dma is not None:
                    p = ops[o.prev_dma]
                    wait(("d", p.sem), dsem[p.sem], p.semval)
                for d in o.deps:
                    dop = ops[d]
                    if dop.is_dma:
                        wait(("d", dop.sem), dsem[dop.sem], dop.semval)
                    else:
                        if dop.eng == engname and not o.is_dma and (engname == "pe" or not same):
                            continue
                        wait(("e", dop.eng), esem[dop.eng], dop.sig)
                ins = o.fn(eng)
                if o.is_dma:
                    ins.then_inc(dsem[o.sem], 16)
                elif o.needed:
                    ins.then_inc(esem[o.eng], 1)
            if engname == "sp":
                lastv = {}
                for o in all_dma:
                    lastv[o.sem] = o.semval
                for s, v in lastv.items():
                    eng.wait_ge(dsem[s], v)

        with nc.Block() as block:
            @block.tensor
            def _(e):
                run("pe", e)

            @block.scalar
            def _(e):
                run("act", e)

            @block.vector
            def _(e):
                run("dve", e)

            @block.gpsimd
            def _(e):
                run("pool", e)

            @block.sync
            def _(e):
                run("sp", e)
        return cnt, ndma


PV = dict(norm1=0, norm2=8, bgate=16, mu=32, w0=46, a0=50, kk=54, ka=58, rk=62, gng=66, gnb=70, oka=74)
NPV = 78


def build_program(phases="AZB1234CD", debug=()):
    BSKIP = os.environ.get("BSKIP", "")
    nc = bass.Bass("TRN2", target_bir_lowering=False)
    P = Prog(nc)

    def din(name, shape):
        return nc.dram_tensor(name, list(shape), F32, kind="ExternalInput").ap()

    def dout(name, shape):
        return nc.dram_tensor(name, list(shape), F32, kind="ExternalOutput").ap()

    xp, xs = din("xp", [T, D]), din("xs", [NS, D])
    cache = [din("c128", [NS, 128, 512]), din("c512", [NS, 512, 512]), din("c2048", [NS, 2048, 512])]
    swkv, sshift = din("swkv", [NS, 8, 64, 64]), din("sshift", [NS, 1792])
    w_in = din("w_in", [D, 6144])
    w_pa, w_pb = din("w_proj_a", [256, D]), din("w_proj_b", [512, D])
    w_out, w_up, w_down = din("w_out", [D, D]), din("w_up", [D, 4096]), din("w_down", [4096, D])
    w_lora, a_lora, g_lora = din("w_lora_up", [64, 512]), din("a_lora_up", [64, 512]), din("g_lora_up", [128, 512])
    pvec_d, normf_d = din("pvec", [128, NPV]), din("normf_g", [D])
    emat_d, mk2_d = din("emat", [128, 12, 256]), din("mk2", [64, 256])
    ident_d, blk_d = din("ident", [128, 128]), din("blk", [128, 128])

    y_p, y_s = dout("y_p", [T, D]), dout("y_s", [NS, D])
    kvp = [dout("kv128_p", [128, 512]), dout("kv512_p", [512, 512]), dout("kv2048_p", [2048, 512])]
    kvs = [dout("kv128_s", [NS, 512]), dout("kv512_s", [NS, 512]), dout("kv2048_s", [NS, 512])]
    wkv_p, wkv_s = dout("wkv_p", [8, 64, 64]), dout("wkv_s", [NS, 8, 64, 64])
    shift_p, shift_s = dout("shift_p", [1792]), dout("shift_s", [NS, 1792])
    zscr = nc.dram_tensor("zscr", [14, 128, NRT], F32, kind="Internal").ap()
    dbg_out = {}

    pvec = P.sbuf([128, NPV], F32)
    identf = P.sbuf([128, 128], F32)
    identb = P.sbuf([128, 128], BF16)
    blk = P.sbuf([128, 128], F32)
    attnT = P.sbuf([128, 2, NT], BF16)
    rwkvT = P.sbuf([128, 4, NT], BF16)
    st_h = contextlib.ExitStack()
    hT = P.sbuf([128, 8, NT], BF16, st_h)
    hscr = nc.dram_tensor("hscr", [128, 8 * NT], BF16, kind="Internal").ap()

    class WS:
        def __init__(self, st, nbuf, width, reqs):
            self.bufs = [P.sbuf([128, 8, width], BF16, st) for _ in range(nbuf)]
            self.nbuf, self.reqs, self.issued, self.got = nbuf, reqs, 0, 0

        def _issue(self, j):
            t = self.bufs[j % self.nbuf]
            key = ("wp", j % self.nbuf)
            for (src2d, kc, co, ncols) in self.reqs[j]:
                src = src2d.rearrange("(k p) c -> p k c", p=128)
                P.dma(lambda e, t=t, src=src, kc=kc, co=co, ncols=ncols: e.dma_start(out=t[:, 0:kc, co:co + ncols], in_=src),
                      [], [key], eng="pool")

        def prime(self):
            while self.issued < min(len(self.reqs), self.nbuf):
                self._issue(self.issued)
                self.issued += 1

        def get(self):
            i = self.got
            self.got += 1
            while self.issued < min(len(self.reqs), i + self.nbuf):
                self._issue(self.issued)
                self.issued += 1
            return self.bufs[i % self.nbuf], ("wp", i % self.nbuf)
    psbig = [P.psum([128, 1024], F32) for _ in range(3)]
    psX = P.psum([128, 512], F32)
    psf = [psbig[i // 2][:, (i % 2) * 512:(i % 2) * 512 + 512] for i in range(6)] + [psX]
    pbig = [0]

    pslim = [7]

    def psbig_next():
        i = pbig[0] % (min(pslim[0], 6) // 2)
        pbig[0] += 1
        return psbig[i], [("psf", 2 * i), ("psf", 2 * i + 1)]

    psb = [P.psum([128, 1024], BF16) for _ in range(1)]
    pctr = [0]
    pbctr = [0]

    def ps_next():
        i = pctr[0] % pslim[0]
        pctr[0] += 1
        return psf[i], ("psf", i)

    def psb_next():
        i = pbctr[0] % len(psb)
        pbctr[0] += 1
        return psb[i], ("psb", i)

    def mm(out, lhsT, rhs, start, stop, reads, writes):
        P.pe(lambda e: e.matmul(out, lhsT=lhsT, rhs=rhs, start=start, stop=stop), reads, writes)

    def tr(out, in_, ident, reads, writes):
        P.pe(lambda e: e.transpose(out, in_, ident), reads, writes)

    def actf(out, in_, func, reads, writes, bias=None, scale=None, accum=None):
        kw = {}
        if bias is not None:
            kw["bias"] = bias
        if scale is not None:
            kw["scale"] = scale
        if accum is not None:
            kw["accum_out"] = accum
        P.act(lambda e: e.activation(out=out, in_=in_, func=func, **kw), reads, writes)

    def tt(out, in0, in1, op, reads, writes, eng="dve"):
        P.op(eng, lambda e: e.tensor_tensor(out=out, in0=in0, in1=in1, op=op), reads, writes)

    def ts(out, in0, s1, s2, op0, op1, reads, writes, eng="dve"):
        if op1 is None:
            P.op(eng, lambda e: e.tensor_scalar(out=out, in0=in0, scalar1=s1, scalar2=None, op0=op0), reads, writes)
        else:
            P.op(eng, lambda e: e.tensor_scalar(out=out, in0=in0, scalar1=s1, scalar2=s2, op0=op0, op1=op1), reads, writes)

    def stt(out, in0, scalar, in1, op0, op1, reads, writes):
        P.dve(lambda e: e.scalar_tensor_tensor(out=out, in0=in0, scalar=scalar, in1=in1, op0=op0, op1=op1), reads, writes)

    def cp(out, in_, reads, writes, eng="dve"):
        if eng == "act":
            P.act(lambda e: e.activation(out=out, in_=in_, func=AF.Copy), reads, writes)
        else:
            P.op(eng, lambda e: e.tensor_copy(out=out, in_=in_), reads, writes)

    def dmas(out, in_, reads, writes, slow=False):
        if slow:
            P.dma(lambda e: e.dma_start(out=out, in_=in_, allow_slow_non_contiguous=True), reads, writes)
        else:
            P.dma(lambda e: e.dma_start(out=out, in_=in_), reads, writes)

    def pcol(name, j=0, lo=0, hi=128):
        c = PV[name] + j
        return pvec[lo:hi, c:c + 1]

    dmas(pvec[:], pvec_d, [], ["pvec"])
    dmas(identf[:], ident_d, [], ["identf"])
    dmas(blk[:], blk_d, [], ["blk"])
    cp(identb[:], identf[:], ["identf"], ["identb"])
    ts(pvec[:, PV["oka"]:PV["oka"] + 4], pvec[:, PV["ka"]:PV["ka"] + 4], -1.0, 1.0, ALU.mult, ALU.add, ["pvec"], ["pvec"])

    tile_cols = [(n * 512, 512) for n in range(4)]
    SC = (T, NS)

    if "A" in phases:
        with contextlib.ExitStack() as st:
            xt = [P.sbuf([128, D], F32, st) for _ in range(6)]
            junk = P.sbuf([128, D], BF16, st)
            xn = [P.sbuf([128, D], BF16, st) for _ in range(5)]
            stat = P.sbuf([128, 4 * 20], F32, st)
            for grp in range(5):
                tiles = range(4 * grp, 4 * grp + 4) if grp < 4 else [16]
                for j, i in enumerate(tiles):
                    rows = 128 if i < 16 else NS
                    xb = xt[i % 6]
                    xk = ("xt", i % 6)
                    src = xp[i * 128:(i + 1) * 128, :] if i < 16 else xs
                    dmas(xb[0:rows, :], src, [], [xk])
                    s0 = stat[0:rows, 4 * i:4 * i + 1]
                    s1 = stat[0:rows, 4 * i + 1:4 * i + 2]
                    s2 = stat[0:rows, 4 * i + 2:4 * i + 3]
                    sk = ("stat", i)
                    actf(junk[0:rows, :], xb[0:rows, :], AF.Square, [xk], ["junk", sk], accum=s0)
                    ts(s1, s0, 1.0 / D, 1e-6, ALU.mult, ALU.add, [sk], [sk])
                    actf(s1, s1, AF.Sqrt, [sk], [sk])
                    P.dve(lambda e, s1=s1, s2=s2: e.reciprocal(out=s2, in_=s1), [sk], [sk])
                    xnb = xn[j if grp < 4 else 4]
                    ts(xnb[0:rows, :], xb[0:rows, :], s2, None, ALU.mult, None, [xk, sk], [("xn", j if grp < 4 else 4)])
                for k in range(8):
                    ptf_, pk = ps_next()
                    pt = ptf_.bitcast(BF16)
                    if grp < 4:
                        for j in range(4):
                            tr(pt[:, j * 128:(j + 1) * 128], xn[j][:, k * 128:(k + 1) * 128], identb[:],
                               [("xn", j), "identb"], [pk])
                        if k % 2 == 0:
                            actf(hT[:, k, grp * 512:(grp + 1) * 512], pt[:, 0:512], AF.Copy, [pk, "pvec"],
                                 [("hT", grp, k)], scale=pcol("norm1", k))
                        else:
                            ts(hT[:, k, grp * 512:(grp + 1) * 512], pt[:, 0:512], pcol("norm1", k), None, ALU.mult, None,
                               [pk, "pvec"], [("hT", grp, k)])
                    else:
                        tr(pt[:, 0:NS], xn[4][0:NS, k * 128:(k + 1) * 128], identb[0:NS, 0:NS], [("xn", 4), "identb"], [pk])
                        actf(hT[:, k, T:NT], pt[:, 0:NS], AF.Copy, [pk, "pvec"], [("hT", 4, k)], scale=pcol("norm1", k))
        P.barrier()

    def hkeys(ns):
        return [("hT", n, k) for n in ns for k in range(8)]

    dmas(hscr, hT[:].rearrange("p k t -> p (k t)"), [], ["hscr"])

    st_b0 = contextlib.ExitStack()
    wsB = WS(st_b0, 5, 128, [[(w_in[:, sec_ + g_ * 256 + p_ * 128: sec_ + g_ * 256 + p_ * 128 + 128], 8, 0, 128)]
                             for p_ in range(2) for g_ in range(3) for sec_ in (0, 768, 1536)])
    emat = P.sbuf([128, 12, 256], F32, st_b0)

    if "Z" in phases:
        with contextlib.ExitStack() as st:
            zraw = [P.sbuf([128, 516], F32, st) for _ in range(2)]
            dlt = [P.sbuf([128, 512], F32, st) for _ in range(2)]
            zmix = [P.sbuf([128, 512], F32, st) for _ in range(3)]
            zprev = P.sbuf([128, 14], F32, st)
            sprev = P.sbuf([128, 14, NS], F32, st)
            zsraw = P.sbuf([128, 14, NS], F32, st)
            zsd = P.sbuf([128, NS], F32, st)
            zspad = [P.sbuf([128, NS * C], F32, st) for _ in range(2)]
            sst = P.sbuf([NS, 1792], F32, st)
            dmas(sst[:], sshift, [], ["sst"])
            for c in range(14):
                pt, pk = ps_next()
                tr(pt[:, 0:NS], sst[0:NS, c * 128:(c + 1) * 128], identf[0:NS, 0:NS], ["sst", "identf"], [pk])
                cp(sprev[:, c, :], pt[:, 0:NS], [pk], [("sprev", c)])
            P.pool(lambda e: e.memset(zprev[:], 0.0), [], ["zprev"])
            for b in range(2):
                P.pool(lambda e, b=b: e.memset(zspad[b][:], 0.0), [], [("zspad", b)])
            it = 0
            ws = WS(st, 5, 128, [[(w_in[:, 2304 + c * 128: 2304 + (c + 1) * 128], 8, 0, 128)] for c in range(14)])
            for c in range(14):
                wt, wk = ws.get()
                for n in range(4):
                    pt, pk = ps_next()
                    for k in range(8):
                        mm(pt[:, 0:512], wt[:, k, 0:128], hT[:, k, n * 512:(n + 1) * 512], k == 0, k == 7,
                           [wk, ("hT", n, k)], [pk])
                    zr, zk = zraw[it % 2], ("zraw", it % 2)
                    dl, dk = dlt[it % 2], ("dlt", it % 2)
                    zm, mk = zmix[it % 3], ("zmix", it % 3)
                    it += 1
                    cp(zr[:, 1:513], pt[:, 0:512], [pk], [zk], eng="act")
                    cp(zr[:, 0:1], zprev[:, c:c + 1], ["zprev"], [zk], eng="pool")
                    tt(dl[:], zr[:, 0:512], zr[:, 1:513], ALU.subtract, [zk], [dk])
                    stt(zm[:], dl[:], pcol("mu", c), zr[:, 1:513], ALU.mult, ALU.add, [dk, zk, "pvec"], [mk])
                    cp(zprev[:, c:c + 1], zr[:, 512:513], [zk], ["zprev"], eng="pool")
                    dmas(zscr[c, :, n * 512:(n + 1) * 512], zm[:], [mk], [("zscr", c, n)])
                pt, pk = ps_next()
                for k in range(8):
                    mm(pt[:, 0:NS], wt[:, k, 0:128], hT[:, k, T:NT], k == 0, k == 7, [wk, ("hT", 4, k)], [pk])
                cp(zsraw[:, c, :], pt[:, 0:NS], [pk], [("zsraw", c)], eng="act")
                tt(zsd[:], sprev[:, c, :], zsraw[:, c, :], ALU.subtract, [("sprev", c), ("zsraw", c)], ["zsd"])
                zp, zpk = zspad[c % 2], ("zspad", c % 2)
                stt(zp[:, 0:NS * C:C], zsd[:], pcol("mu", c), zsraw[:, c, :], ALU.mult, ALU.add,
                    ["zsd", ("zsraw", c), "pvec"], [zpk])
                dmas(zscr[c, :, T:NRT], zp[:], [zpk], [("zscr", c, 4)])
            shst = P.sbuf([14, 5, 128], F32, st)
            for q_ in range(5):
                src_ = zprev[:, 0:14] if q_ == 0 else zsraw[:, :, q_ - 1]
                rk_ = ["zprev"] if q_ == 0 else [("zsraw", c_) for c_ in range(14)]
                pt, pk = ps_next()
                tr(pt[0:14, 0:128], src_, identf[:, :], rk_ + ["identf"], [pk])
                cp(shst[:, q_, :], pt[0:14, 0:128], [pk], [("shst", q_)])
                dst_ = shift_p if q_ == 0 else shift_s[q_ - 1]
                dmas(dst_.rearrange("(c p) -> c p", p=128), shst[:, q_, :], [("shst", q_)], [])
        if "B" in phases:
            wsB.prime()
            P.dma(lambda e: e.dma_start(out=emat[:], in_=emat_d), [], ["emat_pre"])
        P.barrier()

    if "B" in phases:
        with contextlib.ExitStack() as st:
            qT = [P.sbuf([128, NT], BF16, st) for _ in range(3)]
            kT = [P.sbuf([128, NT], BF16, st) for _ in range(3)]
            vaug = [P.sbuf([128, 16, 2, 128], BF16, st) for _ in range(3)]
            oacc = [P.sbuf([128, NT], F32, st) for _ in range(2)]
            exb = [P.sbuf([128, 256], F32, st) for _ in range(4)]
            ptb = [P.sbuf([128, 256], BF16, st) for _ in range(8)]
            kvst = [P.sbuf([128, 2, 128], F32, st) for _ in range(3)]
            rden = P.sbuf([128, NT], F32, st)
            cch = [P.sbuf([128, 2, 128], F32, st) for _ in range(2)]
            kcb = P.sbuf([128, 128], BF16, st)
            kcT = P.sbuf([128, 128], BF16, st)
            vca = P.sbuf([128, 2, 128], BF16, st)
            vnew = P.sbuf([1, NS, 3, 2, 128], BF16, st)
            knst = P.sbuf([1, 2, 128], F32, st)
            vnst = P.sbuf([1, 2, 128], F32, st)
            pnew = P.sbuf([1, 4], BF16, st)
            exs = P.sbuf([128, 4], F32, st)
            pts = P.sbuf([128, 4], BF16, st)
            for g in range(3):
                P.pool(lambda e, g=g: e.memset(vaug[g][:], 1.0), [], [("vaug", g, bl_, h_) for bl_ in range(16) for h_ in range(2)])
            P.pool(lambda e: e.memset(vnew[:], 1.0), [], ["vnew"])
            P.pool(lambda e: e.memset(vca[:], 1.0), [], ["vca"])
            ws = wsB
            for ps_ in range(2):
                wq, wkk, wv = [], [], []
                for g in range(3):
                    d = DILS[g]
                    L = T // d
                    col = g * 256 + ps_ * 128
                    for sec, dst in ((0, qT), (768, kT)):
                        wt, wk = ws.get()
                        for n in range(4):
                            pt, pk = ps_next()
                            for k in range(8):
                                mm(pt[:, 0:512], wt[:, k, 0:128], hT[:, k, n * 512:(n + 1) * 512], k == 0, k == 7,
                                   [wk, ("hT", n, k)], [pk])
                            if d == 1 or "p" in BSKIP:
                                cp(dst[g][:, n * 512:(n + 1) * 512], pt[:, 0:512], [pk], [("qk", sec, g)], eng="act")
                            else:
                                ov = dst[g][:, 0:T].rearrange("p (r i) -> p r i", r=d)[:, :, n * 512 // d:(n + 1) * 512 // d]
                                iv = pt[:, 0:512].rearrange("p (i r) -> p r i", r=d)
                                cp(ov, iv, [pk], [("qk", sec, g)], eng=("dve" if "q" in BSKIP else "act"))
                        pt, pk = ps_next()
                        for k in range(8):
                            mm(pt[:, 0:NS], wt[:, k, 0:128], hT[:, k, T:NT], k == 0, k == 7, [wk, ("hT", 4, k)], [pk])
                        cp(dst[g][:, T:NT], pt[:, 0:NS], [pk], [("qk", sec, g)], eng="act")
                        if sec == 768 and "k" not in BSKIP:
                            need = {0: [15], 1: [3, 7, 11, 15], 2: list(range(16))}[g]
                            rows_g = [128, 512, 2048][g]
                            for bl in need:
                                r_, i0 = (bl * 128) // L, (bl * 128) % L
                                t0 = i0 * d + r_
                                pt, pk = ps_next()
                                for k in range(8):
                                    lh = hT[:, k, bl * 128:(bl + 1) * 128] if "n" in BSKIP else hT[:, k, t0:t0 + 127 * d + 1:d]
                                    mm(pt[:, 0:128], lh, wt[:, k, 0:128], k == 0, k == 7,
                                       [wk] + hkeys(range(4)), [pk])
                                sb, sbk = kvst[bl % 3], ("kvst", bl % 3)
                                cp(sb[:, 0, :], pt[:, 0:128], [pk], [sbk])
                                row0 = t0 - (T - rows_g)
                                dst_ap = kvp[g].rearrange("t (kv h c) -> t kv h c", kv=2, h=2)[row0:row0 + 127 * d + 1:d, 0, ps_, :]
                                if "d" not in BSKIP:
                                    dmas(dst_ap, sb[:, 0, :], [sbk], [])
                            for s in (range(NS) if "s" not in BSKIP else []):
                                pt, pk = ps_next()
                                for k in range(8):
                                    mm(pt[0:1, 0:128], hT[:, k, T + s:T + s + 1], wt[:, k, 0:128], k == 0, k == 7,
                                       [wk, ("hT", 4, k)], [pk])
                                cp(knst[0:1, s % 2, :], pt[0:1, 0:128], [pk], [("knst", s % 2)])
                                dst_ap = kvs[g].rearrange("s (kv h c) -> s kv h c", kv=2, h=2)[s:s + 1, 0, ps_, :]
                                dmas(dst_ap, knst[0:1, s % 2, :], [("knst", s % 2)], [])
                    if "v" in BSKIP:
                        continue
                    wt, wk = ws.get()
                    rows_g = [128, 512, 2048][g]
                    need = {0: [15], 1: [3, 7, 11, 15], 2: list(range(16))}[g]
                    for bl in range(16):
                        r_, i0 = (bl * 128) // L, (bl * 128) % L
                        t0 = i0 * d + r_
                        pt, pk = ps_next()
                        for k in range(8):
                            mm(pt[:, 0:128], hT[:, k, t0:t0 + 127 * d + 1:d], wt[:, k, 0:128], k == 0, k == 7,
                               [wk] + hkeys(range(4)), [pk])
                        sb, sbk = kvst[bl % 3], ("kvst", bl % 3)
                        cp(sb[:, 1, :], pt[:, 0:128], [pk], [sbk])
                        cp(vaug[g][:, bl, 0, 0:64], sb[:, 1, 0:64], [sbk], [("vaug", g, bl, 0)], eng="act")
                        cp(vaug[g][:, bl, 1, 64:128], sb[:, 1, 64:128], [sbk], [("vaug", g, bl, 1)], eng="act")
                        if bl in need:
                            row0 = t0 - (T - rows_g)
                            dst_ap = kvp[g].rearrange("t (kv h c) -> t kv h c", kv=2, h=2)[row0:row0 + 127 * d + 1:d, 1, ps_, :]
                            dmas(dst_ap, sb[:, 1, :], [sbk], [])
                    for s in (range(NS) if "s" not in BSKIP else []):
                        pt, pk = ps_next()
                        for k in range(8):
                            mm(pt[0:1, 0:128], hT[:, k, T + s:T + s + 1], wt[:, k, 0:128], k == 0, k == 7,
                               [wk, ("hT", 4, k)], [pk])
                        cp(vnst[0:1, s % 2, :], pt[0:1, 0:128], [pk], [("vnst", s % 2)])
                        cp(vnew[0:1, s, g, 0, 0:64], vnst[0:1, s % 2, 0:64], [("vnst", s % 2)], ["vnew"], eng="act")
                        cp(vnew[0:1, s, g, 1, 64:128], vnst[0:1, s % 2, 64:128], [("vnst", s % 2)], ["vnew"], eng="act")
                        dst_ap = kvs[g].rearrange("s (kv h c) -> s kv h c", kv=2, h=2)[s:s + 1, 1, ps_, :]
                        dmas(dst_ap, vnst[0:1, s % 2, :], [("vnst", s % 2)], [])
                it = 0
                for hl in (range(2) if "1" in phases else []):
                    pb = hl * 64
                    oa, oak = oacc[hl], ("oacc", hl)
                    oask = ("oaccS", hl)

                    def samp_gen():
                        for s in (range(NS) if "3" in phases else []):
                            for g in range(3):
                                d = DILS[g]
                                head = 4 * g + 2 * ps_ + hl
                                cb, cbk = cch[(s * 3 + g) % 2], ("cch", (s * 3 + g) % 2)
                                src = cache[g].rearrange("s t (kv h c) -> s t kv h c", kv=2, h=2)[s, 0:127 * d + 1:d, :, ps_, :]
                                dmas(cb[:], src, [], [cbk])
                                yield
                                cp(kcb[:], cb[:, 0, :], [cbk], ["kcb"], eng="pool")
                                if hl == 0:
                                    cp(vca[:, 0, 0:64], cb[:, 1, 0:64], [cbk], ["vca"], eng="pool")
                                else:
                                    cp(vca[:, 1, 64:128], cb[:, 1, 64:128], [cbk], ["vca"], eng="pool")
                                yield
                                ptr, ptrk = psb_next()
                                tr(ptr[:, 0:128], kcb[:], identb[:], ["kcb", "identb"], [ptrk])
                                cp(kcT[:], ptr[:, 0:128], [ptrk], ["kcT"], eng="act")
                                yield
                                sp_, spk = ps_next()
                                qcol = qT[g][pb:pb + 64, T + s:T + s + 1]
                                mm(sp_[:, 0:1], kcT[pb:pb + 64, :], qcol, True, True, ["kcT", ("qk", 0, g)], [spk])
                                mm(sp_[0:1, 1:2], kT[g][pb:pb + 64, T + s:T + s + 1], qcol, True, True,
                                   [("qk", 0, g), ("qk", 768, g)], [spk])
                                actf(exs[:, 0:1], sp_[:, 0:1], AF.Exp, [spk], ["exs"], scale=0.125)
                                actf(pnew[0:1, 0:1], sp_[0:1, 1:2], AF.Exp, [spk], ["pnew"], scale=0.125)
                                yield
                                tt(pts[:, 0:1], exs[:, 0:1], emat[:, head, 128:129], ALU.mult, ["exs", "emat"], ["pts"])
                                yield
                                op_, opk = ps_next()
                                mm(op_[:, 0:1], vca[:, hl, :], pts[:, 0:1], True, False, ["vca", "pts"], [opk])
                                mm(op_[:, 0:1], vnew[0:1, s, g, hl, :], pnew[0:1, 0:1], False, True, ["vnew", "pnew"], [opk])
                                ov = oa[:, T + s:T + s + 1]
                                if g == 0:
                                    cp(ov, op_[:, 0:1], [opk], [oask])
                                else:
                                    tt(ov, ov, op_[:, 0:1], ALU.add, [opk, oask], [oask])
                                yield

                    steps = []
                    for g in (range(3) if "2" in phases else []):
                        d = DILS[g]
                        L = T // d
                        nb = L // 128
                        for r_ in range(d):
                            for kb in range(nb):
                                steps.append((g, r_, kb, nb, L, d))
                    LOOK = 3
                    NPB = len(ptb)
                    sgen = samp_gen()
                    pend = []
                    prevs = {}
                    accq = []

                    def stage_pv(info):
                        (g, r_, kb, nb, L, d, pt_, ptk, bl) = info
                        op_, opk = ps_next()
                        prev = prevs.get((g, r_)) if kb > 0 else None
                        if prev is not None:
                            ppt, pptk, pbl = prev
                            mm(op_[:, 0:128], vaug[g][:, pbl, hl, :], ppt[:, 128:256], True, False, [("vaug", g, pbl, hl), pptk], [opk])
                        mm(op_[:, 0:128], vaug[g][:, bl, hl, :], pt_[:, 0:128], prev is None, True, [("vaug", g, bl, hl), ptk], [opk])
                        prevs[(g, r_)] = (pt_, ptk, bl)
                        accq.append((g, r_, kb, d, op_, opk))
                        if len(accq) > 1:
                            stage_acc(accq.pop(0))

                    def stage_acc(a_):
                        (g, r_, kb, d, op_, opk) = a_
                        t0 = kb * 128 * d + r_
                        ov = oa[:, t0:t0 + 127 * d + 1:d]
                        if g == 0:
                            okeys = [("oacc", hl, kb)]
                        elif g == 1:
                            okeys = [("oacc", hl, 4 * kb + i_) for i_ in range(4)]
                        else:
                            okeys = [("oacc", hl, i_) for i_ in range(16)]
                        if g == 0:
                            cp(ov, op_[:, 0:128], [opk], okeys)
                        else:
                            tt(ov, ov, op_[:, 0:128], ALU.add, [opk] + okeys, okeys)

                    for (g, r_, kb, nb, L, d) in steps:
                        head = 4 * g + 2 * ps_ + hl
                        base = r_ * L + kb * 128
                        ncols = 256 if kb + 1 < nb else 128
                        sp_, spk = ps_next()
                        mm(sp_[:, 0:ncols], kT[g][pb:pb + 64, base:base + 128], qT[g][pb:pb + 64, base:base + ncols],
                           True, True, [("qk", 0, g), ("qk", 768, g)], [spk])
                        ex, exk = exb[it % 4], ("exb", it % 4)
                        pt_, ptk = ptb[it % NPB], ("ptb", it % NPB)
                        it += 1
                        actf(ex[:, 0:ncols], sp_[:, 0:ncols], AF.Exp, [spk], [exk], scale=0.125)
                        tt(pt_[:, 0:ncols], ex[:, 0:ncols], emat[:, head, 0:ncols], ALU.mult, [exk, "emat"], [ptk],
                           eng="pool" if it % 2 else "dve")
                        pend.append((g, r_, kb, nb, L, d, pt_, ptk, base // 128))
                        if len(pend) > LOOK:
                            stage_pv(pend.pop(0))
                        next(sgen, None)
                    while pend:
                        stage_pv(pend.pop(0))
                    while accq:
                        stage_acc(accq.pop(0))
                    for _ in sgen:
                        pass
                    nb_, db_ = (0, 64) if hl == 0 else (64, 0)
                    if "4" not in phases:
                        continue
                    oall = [("oacc", hl, i_) for i_ in range(16)]
                    actf(rden[nb_:nb_ + 64, :], oa[db_:db_ + 64, :], AF.Ln, oall + [oask], ["rden"])
                    actf(rden[nb_:nb_ + 64, :], rden[nb_:nb_ + 64, :], AF.Exp, ["rden"], ["rden"], scale=-1.0)
                    tt(attnT[nb_:nb_ + 64, ps_, :], oa[nb_:nb_ + 64, :], rden[nb_:nb_ + 64, :], ALU.mult, oall + [oask, "rden"],
                       [("attnT", ps_, hl)])
        P.barrier()


    st_b0.close()
    st_h.close()

    if "C" in phases:
        with contextlib.ExitStack() as st:
            F32R = mybir.dt.float32r
            USE_R = "R" not in os.environ.get("BSKIP", "")
            r32 = (lambda ap: ap.bitcast(F32R)) if USE_R else (lambda ap: ap)
            NCH = WT // C
            NW = 4 * WT
            mk2 = P.sbuf([64, 256], F32, st)
            dmas(mk2[:], mk2_d, [], ["mk2"])
            cmask = P.sbuf([128, NW], BF16, st)
            smask = P.sbuf([128, WT], F32, st)
            blkr = P.sbuf([128, 128], F32, st)
            idr = P.sbuf([64, 64], F32, st)
            P.pool(lambda e: e.memset(cmask[:], 1.0), [], ["cmask"])
            P.pool(lambda e: e.memset(cmask[:, 0:NW:C], 0.0), ["cmask"], ["cmask"])
            P.pool(lambda e: e.memset(smask[:], 0.0), [], ["smask"])
            P.pool(lambda e: e.memset(smask[:, 0:WT:C], 1.0), ["smask"], ["smask"])
            cp(r32(blkr[:]), blk[:], ["blk"], ["blkr"])
            cp(r32(idr[:]), identf[0:64, 0:64], ["identf"], ["idr"])
            wl_b = P.sbuf([128, 512], BF16, st)
            al_b = P.sbuf([128, 512], BF16, st)
            gl_b = P.sbuf([128, 512], BF16, st)
            P.dma(lambda e: e.dma_start(out=wl_b[0:64, :], in_=w_lora), [], ["wl_b"], eng="pool")
            P.dma(lambda e: e.dma_start(out=al_b[64:128, :], in_=a_lora), [], ["al_b"], eng="pool")
            P.dma(lambda e: e.dma_start(out=gl_b[:], in_=g_lora), [], ["gl_b"], eng="pool")
            z12 = P.sbuf([128, WT], F32, st)
            z13 = P.sbuf([128, WT], F32, st)
            twza = P.sbuf([128, WT], BF16, st)
            sgb = P.sbuf([128, WT], BF16, st)
            NTB = 12
            TB = [P.sbuf([128, NW], F32, st) for _ in range(NTB)]
            SQ = P.sbuf([128, NW], F32, st)
            TVB = [[P.sbuf([128 if s_ else 64, 8 * 64], F32, st) for _ in range(6)] for s_ in range(2)]
            SHB = P.sbuf([128, 8 * 64], F32, st)
            SCB = [P.sbuf([128, NW], BF16, st) for _ in range(2)]
            HG = P.sbuf([64, 8, 64], F32, st)
            TK = [("T", i) for i in range(NTB)]
            natbs = [[P.sbuf([128, NW], BF16, st) for _ in range(5)] for _ in range(2)]
            nks = [[("natb", p_, i) for i in range(5)] for p_ in range(2)]
            gbufs = [P.sbuf([128, NW], F32, st) for _ in range(3)]
            bons = [P.sbuf([128, NW], F32, st) for _ in range(3)]
            GNB = [P.sbuf([128, NW], F32, st) for _ in range(2)]
            SQ2 = P.sbuf([128, NW], F32, st)
            gams = [P.sbuf([128, 4 * NCH], F32, st) for _ in range(2)]
            gamFs = [P.sbuf([64, 8, NCH], F32, st) for _ in range(3)]
            opFs = [P.sbuf([64, 8, NCH, 4, 64], BF16, st) for _ in range(3)]
            TMs = [[P.sbuf([64, NCH, 3, 128], BF16, st) for _ in range(4)] for _ in range(3)]
            Mbs = [P.sbuf([64, 8 * NCH, 192], BF16, st) for _ in range(2)]
            TinvTs = [P.sbuf([64, 8, NCH, 64], BF16, st) for _ in range(2)]
            Hf = P.sbuf([64, 8, 64], F32, st)
            Hb = P.sbuf([64, 8, 64], BF16, st)
            Wsb = P.sbuf([64, 8, 64], BF16, st)
            Usb = P.sbuf([64, 8, 64], BF16, st)
            ynat = P.sbuf([128, NW], F32, st)
            s0in = P.sbuf([64, 8, 64], F32, st)
            sout = P.sbuf([64, 8, 64], F32, st)
            P.pool(lambda e: e.memset(Hf[:], 0.0), [], ["Hf"])
            P.pool(lambda e: e.memset(Hb[:], 0.0), [], ["Hb"])
            w4 = lambda t: t[:, :].rearrange("p (h t) -> p h t", h=4)
            pool_ctr = {"T": 0, "S": 0}
            pool_ids = {"T": [0, 1, 2], "S": [3, 6]}

            def psn(stage):
                ids = pool_ids[stage]
                i = ids[pool_ctr[stage] % len(ids)]
                pool_ctr[stage] += 1
                return psf[i], ("psf", i)

            def emit_state_out(dst_view):
                pt, pk = psn("S")
                for h in range(8):
                    tr(pt[0:64, h * 64:(h + 1) * 64], Hf[:, h, :], identf[0:64, 0:64], ["Hf", "identf"], [pk])
                cp(sout[:], pt[0:64, 0:512].rearrange("p (h k) -> p h k", h=8), [pk], ["sout"])
                dmas(dst_view, sout[:], ["sout"], [])

            NTILES = T // WT + NS * C // WT
            pslim[0] = 4
            zk = lambda c: [("zscr", c, n) for n in range(5)]

            def preA(ti, par):
                samp = ti >= T // WT
                c0 = ti * WT
                natb, nk, gbuf, bon, gam = natbs[par], nks[par], gbufs[ti % 3], bons[ti % 3], gams[par]
                gbk, bonk, gamk = ("gbuf", ti % 3), ("bon", ti % 3), ("gam", par)
                bt, kt, at, rt, vb = natb
                sgy, a_, kk, rn, f_, cs_, eg, egi, r_, k_, v_, egm = TB[0:12]
                sgyk, ak_, kkk, rnk, fk, csk, egk, egik, rk_, kk_, vk_, egmk = TK[0:12]
                P0, P0k, P1, P1k = psf[4], [("psf", 4)], psf[5], [("psf", 5)]
                hsl = [(slice(hp * 128, (hp + 1) * 128), slice(hp * WT, (hp + 1) * WT)) for hp in range(4)]
                dmas(z12[:], zscr[12, :, c0:c0 + WT], zk(12), ["z12"])
                dmas(z13[:], zscr[13, :, c0:c0 + WT], zk(13), ["z13"])
                for j, (dst, key) in enumerate(((r_, rk_), (k_, kk_), (v_, vk_))):
                    dmas(w4(dst), zscr[4 * j:4 * j + 4, :, c0:c0 + WT].rearrange("c p t -> p c t"),
                         [x for c in range(4 * j, 4 * j + 4) for x in zk(c)], [key])
                yield
                actf(twza[0:64, :], z12[0:64, :], AF.Tanh, ["z12"], ["twza0"])
                cp(twza[64:128, :], z12[64:128, :], ["z12"], ["twza1"], eng="pool")
                actf(sgb[:], z13[:], AF.Sigmoid, ["z13"], ["sgb"])
                for hp, (cs, ws_) in enumerate(hsl):
                    ts(kk[:, ws_], k_[:, ws_], pcol("kk", hp), None, ALU.mult, None, [kk_, "pvec"], [kkk])
                yield
                for hp, (cs, ws_) in enumerate(hsl):
                    mm(P0[:, ws_], wl_b[0:64, cs], twza[0:64, :], True, True, ["wl_b", "twza0"], P0k)
                for hp, (cs, ws_) in enumerate(hsl):
                    mm(P1[:, ws_], al_b[64:128, cs], twza[64:128, :], True, True, ["al_b", "twza1"], P1k)
                tt(r32(SQ[:]), kk[:], kk[:], ALU.mult, [kkk], ["SQ"])
                yield
                for hp, (cs, ws_) in enumerate(hsl):
                    actf(sgy[:, ws_], P0[:, ws_], AF.Sigmoid, P0k + ["pvec"], [sgyk], bias=pcol("w0", hp))
                for hp, (cs, ws_) in enumerate(hsl):
                    actf(a_[:, ws_], P1[:, ws_], AF.Sigmoid, P1k + ["pvec"], [ak_], bias=pcol("a0", hp))
                if samp:
                    tt(w4(sgy), w4(sgy), smask[:, :].unsqueeze(1).to_broadcast([128, 4, WT]), ALU.mult, [sgyk, "smask"], [sgyk])
                yield
                P.dve(lambda e: e.tensor_tensor_scan(out=cs_[:], data0=cmask[:], data1=sgy[:], initial=0.0,
                                                     op0=ALU.mult, op1=ALU.add), ["cmask", sgyk], [csk])
                for hp, (cs, ws_) in enumerate(hsl):
                    mm(P0[:, ws_], r32(blkr[:]), r32(SQ[:, ws_]), True, True, ["blkr", "SQ"], P0k)
                for hp, (cs, ws_) in enumerate(hsl):
                    mm(P1[:, ws_], gl_b[:, cs], sgb[:], True, True, ["gl_b", "sgb"], P1k)
                for hp, (cs, ws_) in enumerate(hsl):
                    ts(f_[:, ws_], a_[:, ws_], pcol("ka", hp), pcol("oka", hp), ALU.mult, ALU.add, [ak_, "pvec"], [fk], eng="pool")
                yield
                actf(eg[:], cs_[:], AF.Exp, [csk], [egk], scale=CDEC)
                actf(egi[:], cs_[:], AF.Exp, [csk], [egik], scale=-CDEC)
                tt(egm[:], cs_[:], sgy[:], ALU.subtract, [csk, sgyk], [egmk])
                ts(rn[:], P0[:, 0:NW], 6e-20, None, ALU.max, None, P0k, [rnk])
                cp(gbuf[:], P1[:, 0:NW], P1k, [gbk], eng="act")
                tt(f_[:], k_[:], f_[:], ALU.mult, [kk_, fk], [fk], eng="pool")
                yield
                actf(egm[:], egm[:], AF.Exp, [egmk], [egmk], scale=CDEC)
                actf(rn[:], rn[:], AF.Ln, [rnk], [rnk])
                tt(rt[:], r_[:], eg[:], ALU.mult, [rk_, egk], [nk[3]])
                tt(kt[:], f_[:], egi[:], ALU.mult, [fk, egik], [nk[1]])
                cp(gam[:], eg[:, C - 1:NW:C], [egk], [gamk], eng="pool")
                tt(cs_[:], r_[:], f_[:], ALU.mult, [rk_, fk, csk], [csk], eng="pool")
                cp(vb[:], v_[:], [vk_], [nk[4]], eng="pool")
                yield
                actf(rn[:], rn[:], AF.Exp, [rnk], [rnk], scale=-0.5)
                for hp, (cs, ws_) in enumerate(hsl):
                    ts(r32(SQ[:, ws_]), cs_[:, ws_], pcol("rk", hp), None, ALU.mult, None, [csk, "pvec"], ["SQ"])
                yield
                tt(kk[:], kk[:], rn[:], ALU.mult, [kkk, rnk], [kkk])
                for hp, (cs, ws_) in enumerate(hsl):
                    mm(P0[:, ws_], r32(blkr[:]), r32(SQ[:, ws_]), True, True, ["blkr", "SQ"], P0k)
                yield
                tt(sgy[:], kk[:], a_[:], ALU.mult, [kkk, ak_], [sgyk])
                stt(at[:], kk[:], -1.0, egm[:], ALU.mult, ALU.mult, [kkk, egmk], [nk[2]])
                tt(bon[:], P0[:, 0:NW], v_[:], ALU.mult, P0k + [vk_], [bonk])
                yield
                tt(bt[:], sgy[:], egi[:], ALU.mult, [sgyk, egik], [nk[0]])
                yield

            def preB(ti, par):
                natb, nk, gam = natbs[par], nks[par], gams[par]
                gamk = ("gam", par)
                p3 = ti % 3
                opF, TM, gamF = opFs[p3], TMs[p3], gamFs[p3]
                opk, gfk = ("opF", p3), ("gamF", p3)
                bt, kt, at, rt, vb = natb
                for e_ in range(2):
                    for j, src in enumerate((bt, kt, at, rt)):
                        cp(opF[:, e_:8:2, :, j, :], src[e_ * 64:(e_ + 1) * 64, :].rearrange("p (h c t) -> p h c t", h=4, c=NCH),
                           [nk[j]], [opk], eng="act")
                    cp(gamF[:, e_:8:2, :], gam[e_ * 64:(e_ + 1) * 64, :].rearrange("p (h c) -> p h c", h=4), [gamk], [gfk], eng="pool")
                gB = gam[:, :].unsqueeze(2).to_broadcast([128, 4 * NCH, C])
                kts, bts = SCB
                tt(kts[:, :].rearrange("p (c t) -> p c t", t=C), kt[:, :].rearrange("p (c t) -> p c t", t=C), gB, ALU.mult,
                   [nk[1], gamk], ["kts"], eng="pool")
                tt(bts[:, :].rearrange("p (c t) -> p c t", t=C), bt[:, :].rearrange("p (c t) -> p c t", t=C), gB, ALU.mult,
                   [nk[0], gamk], ["bts"], eng="pool")
                yield
                for hp in range(4):
                    for c2 in range(0, NCH, 2):
                        ptr, ptrk = psb_next()
                        for cc in range(2):
                            c = c2 + cc
                            for j, (src, skey) in enumerate(((vb, nk[4]), (kts, "kts"), (bts, "bts"))):
                                col = hp * WT + c * C
                                tr(ptr[0:64, (cc * 3 + j) * 128:(cc * 3 + j + 1) * 128], src[:, col:col + C], identb[:],
                                   [skey, "identb"], [ptrk])
                        cp(TM[hp][:, c2:c2 + 2, :, :], ptr[0:64, 0:768].rearrange("p (c j f) -> p c j f", c=2, j=3), [ptrk], [("TM", p3, hp)])
                yield

            HPG = 8 // NCH
            NGRP = 8 // HPG
            NP_ = HPG * NCH
            v3 = lambda t: t[0:64, :].rearrange("p (c t) -> p c t", c=NP_)

            def tinv_gen(grp, slot, ti):
                par = ti % 2
                opF, Mb, TinvT = opFs[ti % 3], Mbs[par], TinvTs[par]
                opk = ("opF", ti % 3)
                Y, X, Yn, Xn, Q, Qn = TVB[slot]
                kY, kX, kYn, kXn, kQ, kQn = [("tv", slot, i) for i in range(6)]
                for hh in range(HPG):
                    h = HPG * grp + hh
                    pt, pk0 = psn("T")
                    pk = [pk0]
                    for c in range(NCH):
                        ar = opF[:, h, c, 2:4, :]
                        mm(pt[0:64, c * 256:c * 256 + 128], opF[:, h, c, 0, :], ar, True, True, [opk], pk)
                        mm(pt[0:64, c * 256 + 128:c * 256 + 256], opF[:, h, c, 1, :], ar, True, True, [opk], pk)
                    pv3 = pt[0:64, 0:NCH * 256].rearrange("p (a b) -> p a b", a=NCH)
                    tt(r32(v3(Y)[:, hh * NCH:(hh + 1) * NCH, :]), pv3[:, :, 0:64],
                       mk2[:, 0:64].unsqueeze(1).to_broadcast([64, NCH, 64]), ALU.mult, pk + ["mk2"], [kY])
                    tt(Mb[:, h * NCH:(h + 1) * NCH, :], pv3[:, :, 64:256],
                       mk2[:, 64:256].unsqueeze(1).to_broadcast([64, NCH, 192]), ALU.mult, pk + ["mk2"], [("Mb", par, h)])
                yield
                if ti >= T // WT:
                    cp(TinvT[:, grp * HPG:(grp + 1) * HPG, :, :].rearrange("p h c t -> p (h c) t"),
                       identf[0:64, 0:64].unsqueeze(1).to_broadcast([64, NP_, 64]), ["identf"], [("TinvT", par, grp)], eng="pool")
                    yield
                    return
                pt, pk = psn("T")
                for c in range(NP_):
                    tr(pt[0:64, c * 64:(c + 1) * 64], Y[0:64, c * 64:(c + 1) * 64], identf[0:64, 0:64], [kY, "identf"], [pk])
                cp(r32(X[0:64, :]), pt[0:64, 0:512], [pk], [kX])
                tt(r32(v3(Q)), v3(Y), identf[0:64, 0:64].unsqueeze(1).to_broadcast([64, NP_, 64]), ALU.add, [kY, "identf"], [kQ])
                yield
                for lvl in range(5):
                    pt, pk = psn("T")
                    for c in range(NP_):
                        sl_ = slice(c * 64, (c + 1) * 64)
                        mm(pt[0:64, sl_], r32(Y[0:64, sl_]), r32(X[0:64, sl_]), True, True, [kY, kX], [pk])
                    cp(r32(Xn[0:64, :]), pt[0:64, 0:512], [pk], [kXn], eng="act")
                    if lvl < 4:
                        pt2, pk2 = psn("T")
                        for c in range(NP_):
                            sl_ = slice(c * 64, (c + 1) * 64)
                            mm(pt2[0:64, sl_], r32(X[0:64, sl_]), r32(Y[0:64, sl_]), True, True, [kY, kX], [pk2])
                        cp(r32(Yn[0:64, :]), pt2[0:64, 0:512], [pk2], [kYn], eng="act")
                    yield
                    pt3, pk3 = psn("T")
                    for c in range(NP_):
                        sl_ = slice(c * 64, (c + 1) * 64)
                        mm(pt3[0:64, sl_], r32(Xn[0:64, sl_]), r32(Q[0:64, sl_]), True, True, [kXn, kQ], [pk3])
                    tt(r32(Qn[0:64, :]), pt3[0:64, 0:512], Q[0:64, :], ALU.add, [pk3, kQ], [kQn])
                    Y, Yn, kY, kYn = Yn, Y, kYn, kY
                    X, Xn, kX, kXn = Xn, X, kXn, kX
                    Q, Qn, kQ, kQn = Qn, Q, kQn, kQ
                    yield
                cp(TinvT[:, grp * HPG:(grp + 1) * HPG, :, :], Q[0:64, :].rearrange("p (h c t) -> p h c t", h=HPG, c=NCH),
                   [kQ], [("TinvT", par, grp)], eng="pool")
                yield


            def tinv_pair_gen(ti):
                par = ti % 2
                opF, Mb, TinvT = opFs[ti % 3], Mbs[par], TinvTs[par]
                opk = ("opF", ti % 3)
                PBS = (0, 64)
                B = [list(TVB[0]), list(TVB[1])]
                K = [[("tv", s_, i) for i in range(6)] for s_ in range(2)]
                rows = lambda t, s_: t[PBS[s_]:PBS[s_] + 64, :]
                v3r = lambda t, s_: rows(t, s_).rearrange("p (c t) -> p c t", c=NP_)
                for s_ in range(2):
                    Y, kY = B[s_][0], K[s_][0]
                    for hh in range(HPG):
                        h = HPG * s_ + hh
                        pt, pk0 = psn("T")
                        pk = [pk0]
                        for c in range(NCH):
                            ar = opF[:, h, c, 2:4, :]
                            mm(pt[0:64, c * 256:c * 256 + 128], opF[:, h, c, 0, :], ar, True, True, [opk], pk)
                            mm(pt[0:64, c * 256 + 128:c * 256 + 256], opF[:, h, c, 1, :], ar, True, True, [opk], pk)
                        pv3 = pt[0:64, 0:NCH * 256].rearrange("p (a b) -> p a b", a=NCH)
                        tt(r32(v3r(Y, s_)[:, hh * NCH:(hh + 1) * NCH, :]), pv3[:, :, 0:64],
                           mk2[:, 0:64].unsqueeze(1).to_broadcast([64, NCH, 64]), ALU.mult, pk + ["mk2"], [kY])
                        tt(Mb[:, h * NCH:(h + 1) * NCH, :], pv3[:, :, 64:256],
                           mk2[:, 64:256].unsqueeze(1).to_broadcast([64, NCH, 192]), ALU.mult, pk + ["mk2"], [("Mb", par, h)])
                    yield
                pts = [psn("T") for _ in range(2)]
                for c in range(NP_):
                    for s_ in range(2):
                        pb_ = PBS[s_]
                        tr(pts[s_][0][0:64, c * 64:(c + 1) * 64], B[s_][0][pb_:pb_ + 64, c * 64:(c + 1) * 64],
                           identf[pb_:pb_ + 64, pb_:pb_ + 64], [K[s_][0], "identf"], [pts[s_][1]])
                for s_ in range(2):
                    pb_ = PBS[s_]
                    cp(r32(rows(B[s_][1], s_)), pts[s_][0][0:64, 0:512], [pts[s_][1]], [K[s_][1]])
                    tt(r32(v3r(B[s_][4], s_)), v3r(B[s_][0], s_),
                       identf[pb_:pb_ + 64, pb_:pb_ + 64].unsqueeze(1).to_broadcast([64, NP_, 64]), ALU.add,
                       [K[s_][0], "identf"], [K[s_][4]])
                yield
                for lvl in range(5):
                    pts = [psn("T") for _ in range(2)]
                    for c in range(NP_):
                        sl_ = slice(c * 64, (c + 1) * 64)
                        for s_ in range(2):
                            Y, X = B[s_][0], B[s_][1]
                            mm(pts[s_][0][0:64, sl_], r32(rows(Y, s_)[:, sl_]), r32(rows(X, s_)[:, sl_]), True, True,
                               [K[s_][0], K[s_][1]], [pts[s_][1]])
                    for s_ in range(2):
                        cp(r32(rows(B[s_][3], s_)), pts[s_][0][0:64, 0:512], [pts[s_][1]], [K[s_][3]], eng="act")
                    if lvl < 4:
                        pts2 = [psn("T") for _ in range(2)]
                        for c in range(NP_):
                            sl_ = slice(c * 64, (c + 1) * 64)
                            for s_ in range(2):
                                Y, X = B[s_][0], B[s_][1]
                                mm(pts2[s_][0][0:64, sl_], r32(rows(X, s_)[:, sl_]), r32(rows(Y, s_)[:, sl_]), True, True,
                                   [K[s_][0], K[s_][1]], [pts2[s_][1]])
                        for s_ in range(2):
                            cp(r32(rows(B[s_][2], s_)), pts2[s_][0][0:64, 0:512], [pts2[s_][1]], [K[s_][2]])
                    yield
                    pts3 = [psn("T") for _ in range(2)]
                    for c in range(NP_):
                        sl_ = slice(c * 64, (c + 1) * 64)
                        for s_ in range(2):
                            Xn, Q = B[s_][3], B[s_][4]
                            mm(pts3[s_][0][0:64, sl_], r32(rows(Xn, s_)[:, sl_]), r32(rows(Q, s_)[:, sl_]), True, True,
                               [K[s_][3], K[s_][4]], [pts3[s_][1]])
                    for s_ in range(2):
                        Q, Qn = B[s_][4], B[s_][5]
                        if s_ == 0:
                            tt(r32(rows(Qn, 0)), pts3[0][0][0:64, 0:512], rows(Q, 0), ALU.add, [pts3[0][1], K[0][4]], [K[0][5]])
                        else:
                            cp(rows(SHB, 1), pts3[1][0][0:64, 0:512], [pts3[1][1]], ["shb"], eng="act")
                            tt(r32(rows(Qn, 1)), rows(SHB, 1), rows(Q, 1), ALU.add, ["shb", K[1][4]], [K[1][5]])
                    for s_ in range(2):
                        b_, k_ = B[s_], K[s_]
                        b_[0], b_[2], k_[0], k_[2] = b_[2], b_[0], k_[2], k_[0]
                        b_[1], b_[3], k_[1], k_[3] = b_[3], b_[1], k_[3], k_[1]
                        b_[4], b_[5], k_[4], k_[5] = b_[5], b_[4], k_[5], k_[4]
                    yield
                for s_ in range(2):
                    cp(TinvT[:, s_ * HPG:(s_ + 1) * HPG, :, :], rows(B[s_][4], s_).rearrange("p (h c t) -> p h c t", h=HPG, c=NCH),
                       [K[s_][4]], [("TinvT", par, s_)], eng="act")
                yield

            def scan_gn(ti, par):
                samp = ti >= T // WT
                sbase = (ti - T // WT) * NCH
                c0 = ti * WT
                gbuf, bon = gbufs[ti % 3], bons[ti % 3]
                gbk, bonk = ("gbuf", ti % 3), ("bon", ti % 3)
                p3 = ti % 3
                opF, TM, gamF, Mb, TinvT = opFs[p3], TMs[p3], gamFs[p3], Mbs[par], TinvTs[par]
                opk, gfk = ("opF", p3), ("gamF", p3)
                for c in range(NCH):
                    if samp:
                        dmas(s0in[:], swkv[sbase + c].rearrange("h v k -> v h k"), [], ["s0in"])
                        pt, pk = psn("S")
                        for h in range(8):
                            tr(pt[0:64, h * 64:(h + 1) * 64], s0in[:, h, :], identf[0:64, 0:64], ["s0in", "identf"], [pk])
                        cp(Hf[:], pt[0:64, 0:512].rearrange("p (h v) -> p h v", h=8), [pk], ["Hf"])
                        cp(Hb[:], Hf[:], ["Hf"], ["Hb"], eng="act")
                    Vh = lambda h: TM[h // 2][:, c, 0, (h % 2) * 64:(h % 2) * 64 + 64]
                    Kh = lambda h: TM[h // 2][:, c, 1, (h % 2) * 64:(h % 2) * 64 + 64]
                    Bh = lambda h: TM[h // 2][:, c, 2, (h % 2) * 64:(h % 2) * 64 + 64]
                    Mrb = lambda h: Mb[:, h * NCH + c, 0:64]
                    Mak = lambda h: Mb[:, h * NCH + c, 64:128]
                    Mrk = lambda h: Mb[:, h * NCH + c, 128:192]
                    tt(HG[:], Hf[:], gamF[:, :, c:c + 1].to_broadcast([64, 8, 64]), ALU.mult, ["Hf", gfk], ["HG"], eng="pool")
                    pw_, pwk_ = psn("S")
                    for h in range(8):
                        o_ = pw_[0:64, h * 64:(h + 1) * 64]
                        mm(o_, opF[:, h, c, 2, :], Hb[:, h, :], True, False, [opk, "Hb"], [pwk_])
                        mm(o_, Mak(h), Vh(h), False, True, [("Mb", par, h), ("TM", p3, h // 2)], [pwk_])
                    cp(Wsb[:], pw_[0:64, 0:512].rearrange("p (h v) -> p h v", h=8), [pwk_], ["Wsb"], eng="act")
                    yield
                    pu_, puk_ = psn("S")
                    for h in range(8):
                        mm(pu_[0:64, h * 64:(h + 1) * 64], TinvT[:, h, c, :], Wsb[:, h, :], True, True, [("TinvT", par, h // HPG), "Wsb"], [puk_])
                    cp(Usb[:], pu_[0:64, 0:512].rearrange("p (h v) -> p h v", h=8), [puk_], ["Usb"])
                    yield
                    py_, pyk_ = psn("S")
                    for h in range(8):
                        o_ = py_[0:64, h * 64:(h + 1) * 64]
                        mm(o_, Hb[:, h, :], opF[:, h, c, 3, :], True, False, ["Hb", opk], [pyk_])
                        mm(o_, Usb[:, h, :], Mrb(h), False, False, ["Usb", ("Mb", par, h)], [pyk_])
                        mm(o_, Vh(h), Mrk(h), False, True, [("TM", p3, h // 2), ("Mb", par, h)], [pyk_])
                    pyv = py_[0:64, 0:512].rearrange("p (h e t) -> p h e t", h=4, e=2)
                    for e_ in range(2):
                        cp(r32(w4(ynat)[e_ * 64:(e_ + 1) * 64, :, c * C:(c + 1) * C]), pyv[:, :, e_, :], [pyk_], ["ynat"], eng="act")
                    ph_, phk_ = psn("S")
                    for h in range(8):
                        o_ = ph_[0:64, h * 64:(h + 1) * 64]
                        mm(o_, Bh(h), Usb[:, h, :], True, False, [("TM", p3, h // 2), "Usb"], [phk_])
                        mm(o_, Kh(h), Vh(h), False, True, [("TM", p3, h // 2)], [phk_])
                    phv = ph_[0:64, 0:512].rearrange("p (h v) -> p h v", h=8)
                    tt(Hb[:], phv, HG[:], ALU.add, [phk_, "HG"], ["Hb"])
                    tt(Hf[:], phv, HG[:], ALU.add, [phk_, "HG"], ["Hf"])
                    yield
                    if samp:
                        emit_state_out(wkv_s[sbase + c].rearrange("h v k -> v h k"))
                if ti == T // WT - 1:
                    emit_state_out(wkv_p.rearrange("h v k -> v h k"))
                ync, ysq = GNB
                ynk, ysk = "gny", "gnq"
                SQ = SQ2
                pm, pmk0 = psn("S")
                pmk = [pmk0]
                for hp in range(4):
                    ws_ = slice(hp * WT, (hp + 1) * WT)
                    mm(pm[:, ws_], r32(blkr[:]), r32(ynat[:, ws_]), True, True, ["blkr", "ynat"], pmk)
                stt(ync[:], pm[:, 0:NW], -1.0 / 64, ynat[:], ALU.mult, ALU.add, pmk + ["ynat"], [ynk])
                tt(r32(SQ[:]), ync[:], ync[:], ALU.mult, [ynk], ["SQ2"])
                yield
                pv_, pvk0 = psn("S")
                pvk_ = [pvk0]
                for hp in range(4):
                    ws_ = slice(hp * WT, (hp + 1) * WT)
                    mm(pv_[:, ws_], r32(blkr[:]), r32(SQ[:, ws_]), True, True, ["blkr", "SQ2"], pvk_)
                ts(ysq[:], pv_[:, 0:NW], 1.0 / 64, 64e-5, ALU.mult, ALU.add, pvk_, [ysk])
                yield
                actf(ysq[:], ysq[:], AF.Ln, [ysk], [ysk])
                actf(ysq[:], ysq[:], AF.Exp, [ysk], [ysk], scale=-0.5)
                tt(ync[:], ync[:], ysq[:], ALU.mult, [ynk, ysk], [ynk])
                for hp in range(4):
                    ws_ = slice(hp * WT, (hp + 1) * WT)
                    ts(ync[:, ws_], ync[:, ws_], pcol("gng", hp), pcol("gnb", hp), ALU.mult, ALU.add, [ynk, "pvec"], [ynk])
                yield
                tt(ync[:], ync[:], bon[:], ALU.add, [ynk, bonk], [ynk])
                if not samp:
                    tt(rwkvT[:, :, c0:c0 + WT], w4(ync), w4(gbuf), ALU.mult, [ynk, gbk], ["rwkvT"])
                else:
                    rwt = SCB[0]
                    tt(rwt[:], ync[:], gbuf[:], ALU.mult, [ynk, gbk], ["kts"])
                    cp(rwkvT[:, :, T + sbase:T + sbase + NCH], w4(rwt)[:, :, 0:WT:C], ["kts"], ["rwkvT"], eng="pool")
                yield

            def drain(g_):
                for _ in g_:
                    pass

            def rr(gens):
                alive = True
                while alive:
                    alive = False
                    for g_ in gens:
                        try:
                            next(g_)
                            alive = True
                        except StopIteration:
                            pass

            def tgens(ti):
                if NGRP == 2 and "Q" in BSKIP:
                    return [tinv_pair_gen(ti)]
                return [tinv_gen(g_, g_ % 2, ti) for g_ in range(NGRP)]

            def pgen(ti):
                yield from preA(ti, ti % 2)
                yield from preB(ti, ti % 2)

            drain(pgen(0))
            rr(tgens(0) + ([pgen(1)] if NTILES > 1 else []))
            for ti in range(NTILES):
                gens = [scan_gn(ti, ti % 2)]
                if ti + 1 < NTILES:
                    gens += tgens(ti + 1)
                if ti + 2 < NTILES:
                    gens.append(pgen(ti + 2))
                if "S" in BSKIP:
                    for g_ in gens:
                        drain(g_)
                else:
                    rr(gens)
            pslim[0] = 7
        P.barrier()

    if "rwkvT" in debug:
        dbg_out["rwkvT"] = (rwkvT, [128, 4 * NT], BF16)
    if "attnT" in debug:
        dbg_out["attnT"] = (attnT, [128, 2 * NT], BF16)


    mergedT = P.sbuf([128, 8, NT], BF16)
    if "D" in phases:
        with contextlib.ExitStack() as st:
            hT = P.sbuf([128, 8, NT], BF16, st)
            hsv = hscr.rearrange("p (k t) -> p k t", k=8)
            for n_ in range(5):
                cs_ = slice(n_ * 512, (n_ + 1) * 512) if n_ < 4 else slice(T, NT)
                dmas(hT[:, :, cs_], hsv[:, :, cs_], [], [("hT", n_, k_) for k_ in range(8)])
            gA = [P.sbuf([128, 512], F32, st) for _ in range(2)]
            gB = [P.sbuf([128, 512], F32, st) for _ in range(2)]
            t1b = [P.sbuf([128, 512], F32, st) for _ in range(2)]
            t2b = [P.sbuf([128, 512], F32, st) for _ in range(2)]
            it = 0
            ws = WS(st, 3, 512, [[(w_pa[:, m_ * 128:(m_ + 1) * 128], 2, 0, 128), (w_pb[:, m_ * 128:(m_ + 1) * 128], 4, 128, 128),
                                  (w_in[:, 4096 + m_ * 128:4096 + (m_ + 1) * 128], 8, 256, 128),
                                  (w_in[:, 5120 + m_ * 128:5120 + (m_ + 1) * 128], 8, 384, 128)] for m_ in range(8)])
            for m in range(8):
                wt, wk = ws.get()
                for n in range(5):
                    c0, w_ = (n * 512, 512) if n < 4 else (T, NS)
                    hk = [("hT", n, k) for k in range(8)]
                    p1, p1k = ps_next()
                    for k in range(2):
                        mm(p1[:, 0:w_], wt[:, k, 0:128], attnT[:, k, c0:c0 + w_], k == 0, k == 1, [wk], [p1k])
                    p2, p2k = ps_next()
                    for k in range(4):
                        mm(p2[:, 0:w_], wt[:, k, 128:256], rwkvT[:, k, c0:c0 + w_], k == 0, k == 3, [wk], [p2k])
                    p3, p3k = ps_next()
                    for k in range(8):
                        mm(p3[:, 0:w_], wt[:, k, 256:384], hT[:, k, c0:c0 + w_], k == 0, k == 7, [wk, ("hT", n, k)], [p3k])
                    p4, p4k = ps_next()
                    for k in range(8):
                        mm(p4[:, 0:w_], wt[:, k, 384:512], hT[:, k, c0:c0 + w_], k == 0, k == 7, [wk, ("hT", n, k)], [p4k])
                    b_ = it % 2
                    it += 1
                    actf(gA[b_][:, 0:w_], p3[:, 0:w_], AF.Sigmoid, [p3k, "pvec"], [("gA", b_)], bias=pcol("bgate", m))
                    actf(gB[b_][:, 0:w_], p4[:, 0:w_], AF.Sigmoid, [p4k, "pvec"], [("gB", b_)], bias=pcol("bgate", 8 + m))
                    tt(t1b[b_][:, 0:w_], p1[:, 0:w_], gA[b_][:, 0:w_], ALU.mult, [p1k, ("gA", b_)], [("t1b", b_)])
                    tt(t2b[b_][:, 0:w_], p2[:, 0:w_], gB[b_][:, 0:w_], ALU.mult, [p2k, ("gB", b_)], [("t2b", b_)])
                    tt(mergedT[:, m, c0:c0 + w_], t1b[b_][:, 0:w_], t2b[b_][:, 0:w_], ALU.add, [("t1b", b_), ("t2b", b_)],
                       [("mg", n, m)])
        P.barrier()

    if "D" in phases:
        with contextlib.ExitStack() as st:
            gfin = P.sbuf([128, D], F32, st)
            dmas(gfin[:], normf_d.partition_broadcast(128), [], ["gfin"])
            x2s = [[P.sbuf([128, D], F32, st) for _ in range(5)] for _ in range(2)]
            junk2 = P.sbuf([128, D], BF16, st)
            hmb = [P.sbuf([128, D], BF16, st) for _ in range(2)]
            hmTs = [P.sbuf([128, 8, 516], BF16, st) for _ in range(2)]
            uT = P.sbuf([128, 32, 516], BF16, st)
            rl = [P.sbuf([128, 512], F32, st) for _ in range(2)]
            yst = [P.sbuf([128, D], F32, st) for _ in range(2)]
            st2 = P.sbuf([128, 128], F32, st)
            reqs = [[(w_out[:, h_ * 512:(h_ + 1) * 512], 8, 0, 512)] for h_ in range(2)]
            for n_ in range(4):
                reqs += [[(w_up[:, mb_ * 512:(mb_ + 1) * 512], 8, 0, 512)] for mb_ in range(8)]
                if n_ < 3:
                    reqs += [[(w_out[:, h_ * 512:(h_ + 1) * 512], 8, 0, 512)] for h_ in range(2)]
                reqs += [[(w_down[kb_ * 1024:(kb_ + 1) * 1024, h_ * 512:(h_ + 1) * 512], 8, 0, 512)] for h_ in range(2) for kb_ in range(4)]
            ws = WS(st, 4, 512, reqs)

            def subs_of(n):
                return [(j, 128, n * 512 + j * 128) for j in range(4)] + ([(4, NS, T)] if n == 3 else [])

            def WN(n, psum_fn):
                par = n % 2
                x2, hmT = x2s[par], hmTs[par]
                subs = subs_of(n)
                for (j, rows, t0) in subs:
                    src = xp[t0:t0 + rows, :] if j < 4 else xs
                    dmas(x2[j][0:rows, :], src, [], [("x2", par, j, 0), ("x2", par, j, 1)])
                for half in range(2):
                    wt, wk = ws.get()
                    for (j, rows, t0) in subs:
                        pt, pk = psum_fn()
                        for k in range(8):
                            mm(pt[0:rows, 0:512], mergedT[:, k, t0:t0 + rows], wt[:, k, 0:512], k == 0, k == 7, [wk], [pk])
                        xv = x2[j][0:rows, half * 512:(half + 1) * 512]
                        tt(xv, pt[0:rows, 0:512], xv, ALU.add, [pk, ("x2", par, j, half)], [("x2", par, j, half)])
                yield
                for (j, rows, t0) in subs:
                    sk = ("st2", par, j)
                    s0, s1, s2 = [st2[0:rows, 64 * par + 8 * j + i_:64 * par + 8 * j + i_ + 1] for i_ in range(3)]
                    xk = [("x2", par, j, 0), ("x2", par, j, 1)]
                    actf(junk2[0:rows, :], x2[j][0:rows, :], AF.Square, xk, ["junk2", sk], accum=s0)
                    yield
                    ts(s1, s0, 1.0 / D, 1e-6, ALU.mult, ALU.add, [sk], [sk])
                    actf(s1, s1, AF.Sqrt, [sk], [sk])
                    yield
                    P.dve(lambda e, s1=s1, s2=s2: e.reciprocal(out=s2, in_=s1), [sk], [sk])
                    hb, hbk = hmb[j % 2], ("hmb", j % 2)
                    ts(hb[0:rows, :], x2[j][0:rows, :], s2, None, ALU.mult, None, xk + [sk], [hbk])
                    yield
                    cdst = j * 128 if j < 4 else 512
                    for k4 in range(2):
                        ptb_, ptbk = psb_next()
                        for kk_ in range(4):
                            k = 4 * k4 + kk_
                            tr(ptb_[:, kk_ * 128:kk_ * 128 + rows], hb[0:rows, k * 128:(k + 1) * 128], identb[0:rows, 0:rows],
                               [hbk, "identb"], [ptbk])
                        for kk_ in range(4):
                            k = 4 * k4 + kk_
                            actf(hmT[:, k, cdst:cdst + rows], ptb_[:, kk_ * 128:kk_ * 128 + rows], AF.Copy, [ptbk, "pvec"],
                                 [("hmT", par, k, j)], scale=pcol("norm2", k))
                        yield

            for _ in WN(0, ps_next):
                pass
            for n in range(4):
                par = n % 2
                x2, hmT = x2s[par], hmTs[par]
                subs = subs_of(n)
                ri = 0
                for mb in range(8):
                    wt, wk = ws.get()
                    for jj in range(4):
                        m = 4 * mb + jj
                        pieces = [(0, 512)] + ([(512, NS)] if n == 3 else [])
                        for (cc0, w_) in pieces:
                            pt, pk = ps_next()
                            for k in range(8):
                                mm(pt[:, 0:w_], wt[:, k, jj * 128:(jj + 1) * 128], hmT[:, k, cc0:cc0 + w_], k == 0, k == 7,
                                   [wk] + [("hmT", par, k, j_) for j_ in range(5)], [pk])
                            rb, rbk = rl[ri % 2], ("rl", ri % 2)
                            ri += 1
                            actf(rb[:, 0:w_], pt[:, 0:w_], AF.Relu, [pk], [rbk])
                            tt(uT[:, m, cc0:cc0 + w_], rb[:, 0:w_], rb[:, 0:w_], ALU.mult, [rbk], [("uT", m)])
                nxt = WN(n + 1, ps_next) if n + 1 < 4 else iter(())
                next(nxt, None)
                if "W" in BSKIP:
                    for _ in nxt:
                        pass
                for half in range(2):
                    for kb in range(4):
                        wt, wk = ws.get()
                        for (j, rows, t0) in subs:
                            cdst = j * 128 if j < 4 else 512
                            for k in range(8):
                                mm(psf[j][0:rows, 0:512], uT[:, kb * 8 + k, cdst:cdst + rows], wt[:, k, 0:512],
                                   kb == 0 and k == 0, kb == 3 and k == 7, [wk, ("uT", kb * 8 + k)], [("psf", j)])
                            next(nxt, None)
                    for (j, rows, t0) in subs:
                        xv = x2[j][0:rows, half * 512:(half + 1) * 512]
                        tt(xv, psf[j][0:rows, 0:512], xv, ALU.add, [("psf", j), ("x2", par, j, half)], [("x2", par, j, half)])
                for _ in nxt:
                    pass
                pctr[0] = 5
                for (j, rows, t0) in subs:
                    sk = ("st2f", j)
                    s0, s1, s2 = [st2[0:rows, 8 * j + 3 + i_:8 * j + 4 + i_] for i_ in range(3)]
                    xk = [("x2", par, j, 0), ("x2", par, j, 1)]
                    actf(junk2[0:rows, :], x2[j][0:rows, :], AF.Square, xk, ["junk2", sk], accum=s0)
                    ts(s1, s0, 1.0 / D, 1e-6, ALU.mult, ALU.add, [sk], [sk])
                    actf(s1, s1, AF.Sqrt, [sk], [sk])
                    P.dve(lambda e, s1=s1, s2=s2: e.reciprocal(out=s2, in_=s1), [sk], [sk])
                    yb, ybk = yst[j % 2], ("yst", j % 2)
                    stt(yb[0:rows, :], x2[j][0:rows, :], s2, gfin[0:rows, :], ALU.mult, ALU.mult, xk + [sk, "gfin"], [ybk])
                    dst = y_p[t0:t0 + rows, :] if j < 4 else y_s
                    dmas(dst, yb[0:rows, :], [ybk], [])
        P.barrier()

    for name, (tile_, shp, dt_) in dbg_out.items():
        tmp = P.sbuf(shp, F32)
        do = dout("dbg_" + name, shp)
        cp(tmp[:], tile_[:].rearrange("p a b -> p (a b)") if len(tile_.shape) == 3 else tile_[:], [], ["dbgtmp" + name])
        dmas(do, tmp[:], ["dbgtmp" + name], [])

    P.emit()
    return nc, P


def _consts():
    kj = np.arange(128)[:, None]
    qc = np.arange(256)[None, :]
    delta = qc - kj
    valid = (delta >= 0) & (delta <= 128)
    emat = np.zeros((128, 12, 256), np.float32)
    for h in range(12):
        dil = DILS[h // 4]
        e = np.exp(-np.float64(np.float32(SLOPES[h])) * (delta * dil).astype(np.float64))
        emat[:, h, :] = np.where(valid, e, 0.0).astype(np.float32)
    s = np.arange(64)[:, None]
    t = np.arange(64)[None, :]
    strict = (s < t).astype(np.float32)
    incl = (s <= t).astype(np.float32)
    mk2 = np.concatenate([strict, incl, strict, incl], axis=1)
    ident = np.eye(128, dtype=np.float32)
    blk = np.zeros((128, 128), np.float32)
    blk[:64, :64] = 1.0
    blk[64:, 64:] = 1.0
    return emat, mk2, ident, blk


def _pvec(inp):
    def fm(v):
        v = np.asarray(v, np.float32).reshape(-1, 128)
        return np.ascontiguousarray(v.T)
    cols = [fm(inp["norm1_g"][0]), fm(inp["norm2_g"][0]), fm(inp["b_gate"][0]), fm(inp["mu_shift"][0]),
            fm(inp["w0"][0]), fm(inp["a0"][0]), fm(inp["k_k"][0]), fm(inp["k_a"][0]), fm(inp["r_k"][0].reshape(-1)),
            fm(inp["gn_g"][0]), fm(inp["gn_b"][0]), np.zeros((128, 4), np.float32)]
    return np.ascontiguousarray(np.concatenate(cols, axis=1))


_CACHE = {}


def make_in_maps(inp):
    emat, mk2, ident, blk = _consts()
    pv = _pvec(inp)
    f = lambda a: np.ascontiguousarray(np.asarray(a, np.float32))
    shared = dict(
        w_in=f(inp["w_in"][0]), w_proj_a=f(inp["w_proj_a"][0]), w_proj_b=f(inp["w_proj_b"][0]), w_out=f(inp["w_out"][0]),
        w_up=f(inp["w_up"][0]), w_down=f(inp["w_down"][0]), w_lora_up=f(inp["w_lora_up"][0]), a_lora_up=f(inp["a_lora_up"][0]),
        g_lora_up=f(inp["g_lora_up"][0]), pvec=pv, normf_g=f(inp["normf_g"]), emat=emat, mk2=mk2, ident=ident, blk=blk)
    maps = []
    for c in range(NCORES):
        sl = slice(c * NS, (c + 1) * NS)
        m = dict(shared)
        m["xp"] = f(inp["x_prompt"][c])
        m["xs"] = f(inp["x_sample"][sl, 0])
        m["c128"] = f(np.asarray(inp["cache_kv_w128"][0][sl]).reshape(NS, 128, 512))
        m["c512"] = f(np.asarray(inp["cache_kv_w512"][0][sl]).reshape(NS, 512, 512))
        m["c2048"] = f(np.asarray(inp["cache_kv_w2048"][0][sl]).reshape(NS, 2048, 512))
        m["swkv"] = f(inp["state_wkv"][0][sl])
        m["sshift"] = f(inp["state_shift"][0][sl])
        maps.append(m)
    return maps


def kernel(**inp):
    if "nc" not in _CACHE:
        _CACHE["nc"] = build_program()
    nc, P = _CACHE["nc"]
    maps = make_in_maps(inp)
    res = run_bass_kernel_spmd(nc, maps, core_ids=list(range(NCORES)))
    R = res.results
    cat = lambda name: np.stack([np.asarray(r[name], np.float32) for r in R])
    y_p = cat("y_p")
    y_s = np.concatenate([np.asarray(r["y_s"], np.float32) for r in R])[:, None, :]
    kv128_p = cat("kv128_p").reshape(1, 8, 128, 2, 4, 64)
    kv512_p = cat("kv512_p").reshape(1, 8, 512, 2, 4, 64)
    kv2048_p = cat("kv2048_p").reshape(1, 8, 2048, 2, 4, 64)
    wkv_p = cat("wkv_p").reshape(1, 8, 8, 64, 64)
    shift_p = cat("shift_p").reshape(1, 8, 1792)
    ks = lambda n: np.concatenate([np.asarray(r[n], np.float32) for r in R]).reshape(1, 32, 1, 2, 4, 64)
    wkv_s = np.concatenate([np.asarray(r["wkv_s"], np.float32) for r in R]).reshape(1, 32, 8, 64, 64)
    shift_s = np.concatenate([np.asarray(r["shift_s"], np.float32) for r in R]).reshape(1, 32, 1792)
    return (y_p, y_s, kv128_p, kv512_p, kv2048_p, wkv_p, shift_p, ks("kv128_s"), ks("kv512_s"), ks("kv2048_s"), wkv_s, shift_s)
```

```python
import contextlib
import os
import math
import numpy as np
import concourse.bass as bass
import concourse.mybir as mybir
from concourse.bass_utils import run_bass_kernel_spmd

F32 = mybir.dt.float32
BF16 = mybir.dt.bfloat16
ALU = mybir.AluOpType
AF = mybir.ActivationFunctionType
AX = mybir.AxisListType
ENGS = ("pe", "act", "dve", "pool", "sp")

T = 2048
NS = 4
NT = T + NS
D = 1024
NCORES = 8
C = 64
WT = 128
NRT = T + NS * C
CDEC = -math.exp(-0.5)
SLOPES = [2.0 ** (-8.0 * (h + 1) / 12.0) for h in range(12)]
DILS = [1, 4, 16]


class Op:
    __slots__ = ("idx", "eng", "fn", "deps", "is_dma", "needed", "sig", "sem", "semval", "prev_dma")

    def __init__(self, idx, eng, fn, is_dma):
        self.idx, self.eng, self.fn, self.is_dma = idx, eng, fn, is_dma
        self.deps = ()
        self.needed = False
        self.sig = self.sem = self.semval = self.prev_dma = None


class Prog:
    def __init__(self, nc, n_dma_sems=40, same_engine_sync=True):
        self.nc = nc
        self.ops = []
        self.last_w = {}
        self.readers = {}
        self.n_dma_sems = n_dma_sems
        self.same_engine_sync = same_engine_sync
        self.stack = contextlib.ExitStack()
        self._n = 0
        self.barrier_deps = ()
        self.barrier_id = 0
        self.eng_barrier = {e: 0 for e in ENGS}
        self.last_on_eng = {e: None for e in ENGS}
        self.dma_ops = []

    def sbuf(self, shape, dtype, st=None):
        self._n += 1
        return (st or self.stack).enter_context(self.nc.sbuf_tensor(f"sb{self._n}", list(shape), dtype))

    def psum(self, shape, dtype=F32, st=None):
        self._n += 1
        return (st or self.stack).enter_context(self.nc.psum_tensor(f"ps{self._n}", list(shape), dtype))

    def barrier(self):
        deps = set(i for i in self.last_on_eng.values() if i is not None)
        deps.update(self.dma_ops)
        self.barrier_deps = tuple(deps)
        self.barrier_id += 1
        self.dma_ops = []
        self.last_w = {}
        self.readers = {}

    def op(self, eng, fn, reads=(), writes=(), dma=False):
        o = Op(len(self.ops), eng, fn, dma)
        deps = set()
        if self.eng_barrier[eng] != self.barrier_id:
            deps.update(self.barrier_deps)
            self.eng_barrier[eng] = self.barrier_id
        for k in reads:
            w = self.last_w.get(k)
            if w is not None:
                deps.add(w)
        for k in writes:
            w = self.last_w.get(k)
            if w is not None:
                deps.add(w)
            deps.update(self.readers.get(k, ()))
        deps.discard(o.idx)
        latest = {}
        keep = set()
        for d_ in deps:
            dop = self.ops[d_]
            if dop.is_dma:
                keep.add(d_)
            elif latest.get(dop.eng, -1) < d_:
                latest[dop.eng] = d_
        keep.update(latest.values())
        o.deps = tuple(sorted(keep))
        for k in reads:
            self.readers.setdefault(k, []).append(o.idx)
        for k in writes:
            self.last_w[k] = o.idx
            self.readers[k] = []
        self.ops.append(o)
        self.last_on_eng[eng] = o.idx
        if dma:
            self.dma_ops.append(o.idx)
        return o

    def pe(self, fn, reads=(), writes=()):
        return self.op("pe", fn, reads, writes)

    def act(self, fn, reads=(), writes=()):
        return self.op("act", fn, reads, writes)

    def dve(self, fn, reads=(), writes=()):
        return self.op("dve", fn, reads, writes)

    def pool(self, fn, reads=(), writes=()):
        return self.op("pool", fn, reads, writes)

    def dma(self, fn, reads=(), writes=(), eng="sp"):
        return self.op(eng, fn, reads, writes, dma=True)

    def emit(self):
        nc, ops = self.nc, self.ops
        for o in ops:
            for d in o.deps:
                dop = ops[d]
                if dop.is_dma:
                    continue
                if dop.eng == o.eng and not o.is_dma and (dop.eng == "pe" or not self.same_engine_sync):
                    continue
                dop.needed = True
        cnt = {e: 0 for e in ENGS}
        for o in ops:
            if (not o.is_dma) and o.needed:
                cnt[o.eng] += 1
                o.sig = cnt[o.eng]
        ndma = 0
        dma_cnt = [0] * self.n_dma_sems
        last_on_sem = [None] * self.n_dma_sems
        for o in ops:
            if o.is_dma:
                s = ndma % self.n_dma_sems
                ndma += 1
                dma_cnt[s] += 16
                o.sem, o.semval, o.prev_dma = s, dma_cnt[s], last_on_sem[s]
                last_on_sem[s] = o.idx
        st = self.stack
        esem = {e: st.enter_context(nc.semaphore(f"s_{e}")) for e in ENGS}
        dsem = [st.enter_context(nc.semaphore(f"s_dma{i}")) for i in range(self.n_dma_sems)]
        per_eng = {e: [o for o in ops if o.eng == e] for e in ENGS}
        all_dma = [o for o in ops if o.is_dma]
        same = self.same_engine_sync

        def run(engname, eng):
            waited = {}

            def wait(key, semh, val):
                if waited.get(key, 0) >= val:
                    return
                eng.wait_ge(semh, val)
                waited[key] = val

            for o in per_eng[engname]:
                if o.is_dma and o.prev_dma is not None:
                    p = ops[o.prev_dma]
                    wait(("d", p.sem), dsem[p.sem], p.semval)
                for d in o.deps:
                    dop = ops[d]
                    if dop.is_dma:
                        wait(("d", dop.sem), dsem[dop.sem], dop.semval)
                    else:
                        if dop.eng == engname and not o.is_dma and (engname == "pe" or not same):
                            continue
                        wait(("e", dop.eng), esem[dop.eng], dop.sig)
                ins = o.fn(eng)
                if o.is_dma:
                    ins.then_inc(dsem[o.sem], 16)
                elif o.needed:
                    ins.then_inc(esem[o.eng], 1)
            if engname == "sp":
                lastv = {}
                for o in all_dma:
                    lastv[o.sem] = o.semval
                for s, v in lastv.items():
                    eng.wait_ge(dsem[s], v)

        with nc.Block() as block:
            @block.tensor
            def _(e):
                run("pe", e)

            @block.scalar
            def _(e):
                run("act", e)

            @block.vector
            def _(e):
                run("dve", e)

            @block.gpsimd
            def _(e):
                run("pool", e)

            @block.sync
            def _(e):
                run("sp", e)
        return cnt, ndma


PV = dict(norm1=0, norm2=8, bgate=16, mu=32, w0=46, a0=50, kk=54, ka=58, rk=62, gng=66, gnb=70, oka=74)
NPV = 78


def build_program(phases="AZB1234CD", debug=()):
    BSKIP = os.environ.get("BSKIP", "")
    nc = bass.Bass("TRN2", target_bir_lowering=False)
    P = Prog(nc)

    def din(name, shape):
        return nc.dram_tensor(name, list(shape), F32, kind="ExternalInput").ap()

    def dout(name, shape):
        return nc.dram_tensor(name, list(shape), F32, kind="ExternalOutput").ap()

    xp, xs = din("xp", [T, D]), din("xs", [NS, D])
    cache = [din("c128", [NS, 128, 512]), din("c512", [NS, 512, 512]), din("c2048", [NS, 2048, 512])]
    swkv, sshift = din("swkv", [NS, 8, 64, 64]), din("sshift", [NS, 1792])
    w_in = din("w_in", [D, 6144])
    w_pa, w_pb = din("w_proj_a", [256, D]), din("w_proj_b", [512, D])
    w_out, w_up, w_down = din("w_out", [D, D]), din("w_up", [D, 4096]), din("w_down", [4096, D])
    w_lora, a_lora, g_lora = din("w_lora_up", [64, 512]), din("a_lora_up", [64, 512]), din("g_lora_up", [128, 512])
    pvec_d, normf_d = din("pvec", [128, NPV]), din("normf_g", [D])
    emat_d, mk2_d = din("emat", [128, 12, 256]), din("mk2", [64, 256])
    ident_d, blk_d = din("ident", [128, 128]), din("blk", [128, 128])

    y_p, y_s = dout("y_p", [T, D]), dout("y_s", [NS, D])
    kvp = [dout("kv128_p", [128, 512]), dout("kv512_p", [512, 512]), dout("kv2048_p", [2048, 512])]
    kvs = [dout("kv128_s", [NS, 512]), dout("kv512_s", [NS, 512]), dout("kv2048_s", [NS, 512])]
    wkv_p, wkv_s = dout("wkv_p", [8, 64, 64]), dout("wkv_s", [NS, 8, 64, 64])
    shift_p, shift_s = dout("shift_p", [1792]), dout("shift_s", [NS, 1792])
    zscr = nc.dram_tensor("zscr", [14, 128, NRT], F32, kind="Internal").ap()
    dbg_out = {}

    pvec = P.sbuf([128, NPV], F32)
    identf = P.sbuf([128, 128], F32)
    identb = P.sbuf([128, 128], BF16)
    blk = P.sbuf([128, 128], F32)
    attnT = P.sbuf([128, 2, NT], BF16)
    rwkvT = P.sbuf([128, 4, NT], BF16)
    st_h = contextlib.ExitStack()
    hT = P.sbuf([128, 8, NT], BF16, st_h)
    hscr = nc.dram_tensor("hscr", [128, 8 * NT], BF16, kind="Internal").ap()

    class WS:
        def __init__(self, st, nbuf, width, reqs):
            self.bufs = [P.sbuf([128, 8, width], BF16, st) for _ in range(nbuf)]
            self.nbuf, self.reqs, self.issued, self.got = nbuf, reqs, 0, 0

        def _issue(self, j):
            t = self.bufs[j % self.nbuf]
            key = ("wp", j % self.nbuf)
            for (src2d, kc, co, ncols) in self.reqs[j]:
                src = src2d.rearrange("(k p) c -> p k c", p=128)
                P.dma(lambda e, t=t, src=src, kc=kc, co=co, ncols=ncols: e.dma_start(out=t[:, 0:kc, co:co + ncols], in_=src),
                      [], [key], eng="pool")

        def prime(self):
            while self.issued < min(len(self.reqs), self.nbuf):
                self._issue(self.issued)
                self.issued += 1

        def get(self):
            i = self.got
            self.got += 1
            while self.issued < min(len(self.reqs), i + self.nbuf):
                self._issue(self.issued)
                self.issued += 1
            return self.bufs[i % self.nbuf], ("wp", i % self.nbuf)
    psbig = [P.psum([128, 1024], F32) for _ in range(3)]
    psX = P.psum([128, 512], F32)
    psf = [psbig[i // 2][:, (i % 2) * 512:(i % 2) * 512 + 512] for i in range(6)] + [psX]
    pbig = [0]

    pslim = [7]

    def psbig_next():
        i = pbig[0] % (min(pslim[0], 6) // 2)
        pbig[0] += 1
        return psbig[i], [("psf", 2 * i), ("psf", 2 * i + 1)]

    psb = [P.psum([128, 1024], BF16) for _ in range(1)]
    pctr = [0]
    pbctr = [0]

    def ps_next():
        i = pctr[0] % pslim[0]
        pctr[0] += 1
        return psf[i], ("psf", i)

    def psb_next():
        i = pbctr[0] % len(psb)
        pbctr[0] += 1
        return psb[i], ("psb", i)

    def mm(out, lhsT, rhs, start, stop, reads, writes):
        P.pe(lambda e: e.matmul(out, lhsT=lhsT, rhs=rhs, start=start, stop=stop), reads, writes)

    def tr(out, in_, ident, reads, writes):
        P.pe(lambda e: e.transpose(out, in_, ident), reads, writes)

    def actf(out, in_, func, reads, writes, bias=None, scale=None, accum=None):
        kw = {}
        if bias is not None:
            kw["bias"] = bias
        if scale is not None:
            kw["scale"] = scale
        if accum is not None:
            kw["accum_out"] = accum
        P.act(lambda e: e.activation(out=out, in_=in_, func=func, **kw), reads, writes)

    def tt(out, in0, in1, op, reads, writes, eng="dve"):
        P.op(eng, lambda e: e.tensor_tensor(out=out, in0=in0, in1=in1, op=op), reads, writes)

    def ts(out, in0, s1, s2, op0, op1, reads, writes, eng="dve"):
        if op1 is None:
            P.op(eng, lambda e: e.tensor_scalar(out=out, in0=in0, scalar1=s1, scalar2=None, op0=op0), reads, writes)
        else:
            P.op(eng, lambda e: e.tensor_scalar(out=out, in0=in0, scalar1=s1, scalar2=s2, op0=op0, op1=op1), reads, writes)

    def stt(out, in0, scalar, in1, op0, op1, reads, writes):
        P.dve(lambda e: e.scalar_tensor_tensor(out=out, in0=in0, scalar=scalar, in1=in1, op0=op0, op1=op1), reads, writes)

    def cp(out, in_, reads, writes, eng="dve"):
        if eng == "act":
            P.act(lambda e: e.activation(out=out, in_=in_, func=AF.Copy), reads, writes)
        else:
            P.op(eng, lambda e: e.tensor_copy(out=out, in_=in_), reads, writes)

    def dmas(out, in_, reads, writes, slow=False):
        if slow:
            P.dma(lambda e: e.dma_start(out=out, in_=in_, allow_slow_non_contiguous=True), reads, writes)
        else:
            P.dma(lambda e: e.dma_start(out=out, in_=in_), reads, writes)

    def pcol(name, j=0, lo=0, hi=128):
        c = PV[name] + j
        return pvec[lo:hi, c:c + 1]

    dmas(pvec[:], pvec_d, [], ["pvec"])
    dmas(identf[:], ident_d, [], ["identf"])
    dmas(blk[:], blk_d, [], ["blk"])
    cp(identb[:], identf[:], ["identf"], ["identb"])
    ts(pvec[:, PV["oka"]:PV["oka"] + 4], pvec[:, PV["ka"]:PV["ka"] + 4], -1.0, 1.0, ALU.mult, ALU.add, ["pvec"], ["pvec"])

    tile_cols = [(n * 512, 512) for n in range(4)]
    SC = (T, NS)

    if "A" in phases:
        with contextlib.ExitStack() as st:
            xt = [P.sbuf([128, D], F32, st) for _ in range(6)]
            junk = P.sbuf([128, D], BF16, st)
            xn = [P.sbuf([128, D], BF16, st) for _ in range(5)]
            stat = P.sbuf([128, 4 * 20], F32, st)
            for grp in range(5):
                tiles = range(4 * grp, 4 * grp + 4) if grp < 4 else [16]
                for j, i in enumerate(tiles):
                    rows = 128 if i < 16 else NS
                    xb = xt[i % 6]
                    xk = ("xt", i % 6)
                    src = xp[i * 128:(i + 1) * 128, :] if i < 16 else xs
                    dmas(xb[0:rows, :], src, [], [xk])
                    s0 = stat[0:rows, 4 * i:4 * i + 1]
                    s1 = stat[0:rows, 4 * i + 1:4 * i + 2]
                    s2 = stat[0:rows, 4 * i + 2:4 * i + 3]
                    sk = ("stat", i)
                    actf(junk[0:rows, :], xb[0:rows, :], AF.Square, [xk], ["junk", sk], accum=s0)
                    ts(s1, s0, 1.0 / D, 1e-6, ALU.mult, ALU.add, [sk], [sk])
                    actf(s1, s1, AF.Sqrt, [sk], [sk])
                    P.dve(lambda e, s1=s1, s2=s2: e.reciprocal(out=s2, in_=s1), [sk], [sk])
                    xnb = xn[j if grp < 4 else 4]
                    ts(xnb[0:rows, :], xb[0:rows, :], s2, None, ALU.mult, None, [xk, sk], [("xn", j if grp < 4 else 4)])
                for k in range(8):
                    ptf_, pk = ps_next()
                    pt = ptf_.bitcast(BF16)
                    if grp < 4:
                        for j in range(4):
                            tr(pt[:, j * 128:(j + 1) * 128], xn[j][:, k * 128:(k + 1) * 128], identb[:],
                               [("xn", j), "identb"], [pk])
                        if k % 2 == 0:
                            actf(hT[:, k, grp * 512:(grp + 1) * 512], pt[:, 0:512], AF.Copy, [pk, "pvec"],
                                 [("hT", grp, k)], scale=pcol("norm1", k))
                        else:
                            ts(hT[:, k, grp * 512:(grp + 1) * 512], pt[:, 0:512], pcol("norm1", k), None, ALU.mult, None,
                               [pk, "pvec"], [("hT", grp, k)])
                    else:
                        tr(pt[:, 0:NS], xn[4][0:NS, k * 128:(k + 1) * 128], identb[0:NS, 0:NS], [("xn", 4), "identb"], [pk])
                        actf(hT[:, k, T:NT], pt[:, 0:NS], AF.Copy, [pk, "pvec"], [("hT", 4, k)], scale=pcol("norm1", k))
        P.barrier()

    def hkeys(ns):
        return [("hT", n, k) for n in ns for k in range(8)]

    dmas(hscr, hT[:].rearrange("p k t -> p (k t)"), [], ["hscr"])

    st_b0 = contextlib.ExitStack()
    wsB = WS(st_b0, 5, 128, [[(w_in[:, sec_ + g_ * 256 + p_ * 128: sec_ + g_ * 256 + p_ * 128 + 128], 8, 0, 128)]
                             for p_ in range(2) for g_ in range(3) for sec_ in (0, 768, 1536)])
    emat = P.sbuf([128, 12, 256], F32, st_b0)

    if "Z" in phases:
        with contextlib.ExitStack() as st:
            zraw = [P.sbuf([128, 516], F32, st) for _ in range(2)]
            dlt = [P.sbuf([128, 512], F32, st) for _ in range(2)]
            zmix = [P.sbuf([128, 512], F32, st) for _ in range(3)]
            zprev = P.sbuf([128, 14], F32, st)
            sprev = P.sbuf([128, 14, NS], F32, st)
            zsraw = P.sbuf([128, 14, NS], F32, st)
            zsd = P.sbuf([128, NS], F32, st)
            zspad = [P.sbuf([128, NS * C], F32, st) for _ in range(2)]
            sst = P.sbuf([NS, 1792], F32, st)
            dmas(sst[:], sshift, [], ["sst"])
            for c in range(14):
                pt, pk = ps_next()
                tr(pt[:, 0:NS], sst[0:NS, c * 128:(c + 1) * 128], identf[0:NS, 0:NS], ["sst", "identf"], [pk])
                cp(sprev[:, c, :], pt[:, 0:NS], [pk], [("sprev", c)])
            P.pool(lambda e: e.memset(zprev[:], 0.0), [], ["zprev"])
            for b in range(2):
                P.pool(lambda e, b=b: e.memset(zspad[b][:], 0.0), [], [("zspad", b)])
            it = 0
            ws = WS(st, 5, 128, [[(w_in[:, 2304 + c * 128: 2304 + (c + 1) * 128], 8, 0, 128)] for c in range(14)])
            for c in range(14):
                wt, wk = ws.get()
                for n in range(4):
                    pt, pk = ps_next()
                    for k in range(8):
                        mm(pt[:, 0:512], wt[:, k, 0:128], hT[:, k, n * 512:(n + 1) * 512], k == 0, k == 7,
                           [wk, ("hT", n, k)], [pk])
                    zr, zk = zraw[it % 2], ("zraw", it % 2)
                    dl, dk = dlt[it % 2], ("dlt", it % 2)
                    zm, mk = zmix[it % 3], ("zmix", it % 3)
                    it += 1
                    cp(zr[:, 1:513], pt[:, 0:512], [pk], [zk], eng="act")
                    cp(zr[:, 0:1], zprev[:, c:c + 1], ["zprev"], [zk], eng="pool")
                    tt(dl[:], zr[:, 0:512], zr[:, 1:513], ALU.subtract, [zk], [dk])
                    stt(zm[:], dl[:], pcol("mu", c), zr[:, 1:513], ALU.mult, ALU.add, [dk, zk, "pvec"], [mk])
                    cp(zprev[:, c:c + 1], zr[:, 512:513], [zk], ["zprev"], eng="pool")
                    dmas(zscr[c, :, n * 512:(n + 1) * 512], zm[:], [mk], [("zscr", c, n)])
                pt, pk = ps_next()
                for k in range(8):
                    mm(pt[:, 0:NS], wt[:, k, 0:128], hT[:, k, T:NT], k == 0, k == 7, [wk, ("hT", 4, k)], [pk])
                cp(zsraw[:, c, :], pt[:, 0:NS], [pk], [("zsraw", c)], eng="act")
                tt(zsd[:], sprev[:, c, :], zsraw[:, c, :], ALU.subtract, [("sprev", c), ("zsraw", c)], ["zsd"])
                zp, zpk = zspad[c % 2], ("zspad", c % 2)
                stt(zp[:, 0:NS * C:C], zsd[:], pcol("mu", c), zsraw[:, c, :], ALU.mult, ALU.add,
                    ["zsd", ("zsraw", c), "pvec"], [zpk])
                dmas(zscr[c, :, T:NRT], zp[:], [zpk], [("zscr", c, 4)])
            shst = P.sbuf([14, 5, 128], F32, st)
            for q_ in range(5):
                src_ = zprev[:, 0:14] if q_ == 0 else zsraw[:, :, q_ - 1]
                rk_ = ["zprev"] if q_ == 0 else [("zsraw", c_) for c_ in range(14)]
                pt, pk = ps_next()
                tr(pt[0:14, 0:128], src_, identf[:, :], rk_ + ["identf"], [pk])
                cp(shst[:, q_, :], pt[0:14, 0:128], [pk], [("shst", q_)])
                dst_ = shift_p if q_ == 0 else shift_s[q_ - 1]
                dmas(dst_.rearrange("(c p) -> c p", p=128), shst[:, q_, :], [("shst", q_)], [])
        if "B" in phases:
            wsB.prime()
            P.dma(lambda e: e.dma_start(out=emat[:], in_=emat_d), [], ["emat_pre"])
        P.barrier()

    if "B" in phases:
        with contextlib.ExitStack() as st:
            qT = [P.sbuf([128, NT], BF16, st) for _ in range(3)]
            kT = [P.sbuf([128, NT], BF16, st) for _ in range(3)]
            vaug = [P.sbuf([128, 16, 2, 128], BF16, st) for _ in range(3)]
            oacc = [P.sbuf([128, NT], F32, st) for _ in range(2)]
            exb = [P.sbuf([128, 256], F32, st) for _ in range(4)]
            ptb = [P.sbuf([128, 256], BF16, st) for _ in range(8)]
            kvst = [P.sbuf([128, 2, 128], F32, st) for _ in range(3)]
            rden = P.sbuf([128, NT], F32, st)
            cch = [P.sbuf([128, 2, 128], F32, st) for _ in range(2)]
            kcb = P.sbuf([128, 128], BF16, st)
            kcT = P.sbuf([128, 128], BF16, st)
            vca = P.sbuf([128, 2, 128], BF16, st)
            vnew = P.sbuf([1, NS, 3, 2, 128], BF16, st)
            knst = P.sbuf([1, 2, 128], F32, st)
            vnst = P.sbuf([1, 2, 128], F32, st)
            pnew = P.sbuf([1, 4], BF16, st)
            exs = P.sbuf([128, 4], F32, st)
            pts = P.sbuf([128, 4], BF16, st)
            for g in range(3):
                P.pool(lambda e, g=g: e.memset(vaug[g][:], 1.0), [], [("vaug", g, bl_, h_) for bl_ in range(16) for h_ in range(2)])
            P.pool(lambda e: e.memset(vnew[:], 1.0), [], ["vnew"])
            P.pool(lambda e: e.memset(vca[:], 1.0), [], ["vca"])
            ws = wsB
            for ps_ in range(2):
                wq, wkk, wv = [], [], []
                for g in range(3):
                    d = DILS[g]
                    L = T // d
                    col = g * 256 + ps_ * 128
                    for sec, dst in ((0, qT), (768, kT)):
                        wt, wk = ws.get()
                        for n in range(4):
                            pt, pk = ps_next()
                            for k in range(8):
                                mm(pt[:, 0:512], wt[:, k, 0:128], hT[:, k, n * 512:(n + 1) * 512], k == 0, k == 7,
                                   [wk, ("hT", n, k)], [pk])
                            if d == 1 or "p" in BSKIP:
                                cp(dst[g][:, n * 512:(n + 1) * 512], pt[:, 0:512], [pk], [("qk", sec, g)], eng="act")
                            else:
                                ov = dst[g][:, 0:T].rearrange("p (r i) -> p r i", r=d)[:, :, n * 512 // d:(n + 1) * 512 // d]
                                iv = pt[:, 0:512].rearrange("p (i r) -> p r i", r=d)
                                cp(ov, iv, [pk], [("qk", sec, g)], eng=("dve" if "q" in BSKIP else "act"))
                        pt, pk = ps_next()
                        for k in range(8):
                            mm(pt[:, 0:NS], wt[:, k, 0:128], hT[:, k, T:NT], k == 0, k == 7, [wk, ("hT", 4, k)], [pk])
                        cp(dst[g][:, T:NT], pt[:, 0:NS], [pk], [("qk", sec, g)], eng="act")
                        if sec == 768 and "k" not in BSKIP:
                            need = {0: [15], 1: [3, 7, 11, 15], 2: list(range(16))}[g]
                            rows_g = [128, 512, 2048][g]
                            for bl in need:
                                r_, i0 = (bl * 128) // L, (bl * 128) % L
                                t0 = i0 * d + r_
                                pt, pk = ps_next()
                                for k in range(8):
                                    lh = hT[:, k, bl * 128:(bl + 1) * 128] if "n" in BSKIP else hT[:, k, t0:t0 + 127 * d + 1:d]
                                    mm(pt[:, 0:128], lh, wt[:, k, 0:128], k == 0, k == 7,
                                       [wk] + hkeys(range(4)), [pk])
                                sb, sbk = kvst[bl % 3], ("kvst", bl % 3)
                                cp(sb[:, 0, :], pt[:, 0:128], [pk], [sbk])
                                row0 = t0 - (T - rows_g)
                                dst_ap = kvp[g].rearrange("t (kv h c) -> t kv h c", kv=2, h=2)[row0:row0 + 127 * d + 1:d, 0, ps_, :]
                                if "d" not in BSKIP:
                                    dmas(dst_ap, sb[:, 0, :], [sbk], [])
                            for s in (range(NS) if "s" not in BSKIP else []):
                                pt, pk = ps_next()
                                for k in range(8):
                                    mm(pt[0:1, 0:128], hT[:, k, T + s:T + s + 1], wt[:, k, 0:128], k == 0, k == 7,
                                       [wk, ("hT", 4, k)], [pk])
                                cp(knst[0:1, s % 2, :], pt[0:1, 0:128], [pk], [("knst", s % 2)])
                                dst_ap = kvs[g].rearrange("s (kv h c) -> s kv h c", kv=2, h=2)[s:s + 1, 0, ps_, :]
                                dmas(dst_ap, knst[0:1, s % 2, :], [("knst", s % 2)], [])
                    if "v" in BSKIP:
                        continue
                    wt, wk = ws.get()
                    rows_g = [128, 512, 2048][g]
                    need = {0: [15], 1: [3, 7, 11, 15], 2: list(range(16))}[g]
                    for bl in range(16):
                        r_, i0 = (bl * 128) // L, (bl * 128) % L
                        t0 = i0 * d + r_
                        pt, pk = ps_next()
                        for k in range(8):
                            mm(pt[:, 0:128], hT[:, k, t0:t0 + 127 * d + 1:d], wt[:, k, 0:128], k == 0, k == 7,
                               [wk] + hkeys(range(4)), [pk])
                        sb, sbk = kvst[bl % 3], ("kvst", bl % 3)
                        cp(sb[:, 1, :], pt[:, 0:128], [pk], [sbk])
                        cp(vaug[g][:, bl, 0, 0:64], sb[:, 1, 0:64], [sbk], [("vaug", g, bl, 0)], eng="act")
                        cp(vaug[g][:, bl, 1, 64:128], sb[:, 1, 64:128], [sbk], [("vaug", g, bl, 1)], eng="act")
                        if bl in need:
                            row0 = t0 - (T - rows_g)
                            dst_ap = kvp[g].rearrange("t (kv h c) -> t kv h c", kv=2, h=2)[row0:row0 + 127 * d + 1:d, 1, ps_, :]
                            dmas(dst_ap, sb[:, 1, :], [sbk], [])
                    for s in (range(NS) if "s" not in BSKIP else []):
                        pt, pk = ps_next()
                        for k in range(8):
                            mm(pt[0:1, 0:128], hT[:, k, T + s:T + s + 1], wt[:, k, 0:128], k == 0, k == 7,
                               [wk, ("hT", 4, k)], [pk])
                        cp(vnst[0:1, s % 2, :], pt[0:1, 0:128], [pk], [("vnst", s % 2)])
                        cp(vnew[0:1, s, g, 0, 0:64], vnst[0:1, s % 2, 0:64], [("vnst", s % 2)], ["vnew"], eng="act")
                        cp(vnew[0:1, s, g, 1, 64:128], vnst[0:1, s % 2, 64:128], [("vnst", s % 2)], ["vnew"], eng="act")
                        dst_ap = kvs[g].rearrange("s (kv h c) -> s kv h c", kv=2, h=2)[s:s + 1, 1, ps_, :]
                        dmas(dst_ap, vnst[0:1, s % 2, :], [("vnst", s % 2)], [])
                it = 0
                for hl in (range(2) if "1" in phases else []):
                    pb = hl * 64
                    oa, oak = oacc[hl], ("oacc", hl)
                    oask = ("oaccS", hl)

                    def samp_gen():
                        for s in (range(NS) if "3" in phases else []):
                            for g in range(3):
                                d = DILS[g]
                                head = 4 * g + 2 * ps_ + hl
                                cb, cbk = cch[(s * 3 + g) % 2], ("cch", (s * 3 + g) % 2)
                                src = cache[g].rearrange("s t (kv h c) -> s t kv h c", kv=2, h=2)[s, 0:127 * d + 1:d, :, ps_, :]
                                dmas(cb[:], src, [], [cbk])
                                yield
                                cp(kcb[:], cb[:, 0, :], [cbk], ["kcb"], eng="pool")
                                if hl == 0:
                                    cp(vca[:, 0, 0:64], cb[:, 1, 0:64], [cbk], ["vca"], eng="pool")
                                else:
                                    cp(vca[:, 1, 64:128], cb[:, 1, 64:128], [cbk], ["vca"], eng="pool")
                                yield
                                ptr, ptrk = psb_next()
                                tr(ptr[:, 0:128], kcb[:], identb[:], ["kcb", "identb"], [ptrk])
                                cp(kcT[:], ptr[:, 0:128], [ptrk], ["kcT"], eng="act")
                                yield
                                sp_, spk = ps_next()
                                qcol = qT[g][pb:pb + 64, T + s:T + s + 1]
                                mm(sp_[:, 0:1], kcT[pb:pb + 64, :], qcol, True, True, ["kcT", ("qk", 0, g)], [spk])
                                mm(sp_[0:1, 1:2], kT[g][pb:pb + 64, T + s:T + s + 1], qcol, True, True,
                                   [("qk", 0, g), ("qk", 768, g)], [spk])
                                actf(exs[:, 0:1], sp_[:, 0:1], AF.Exp, [spk], ["exs"], scale=0.125)
                                actf(pnew[0:1, 0:1], sp_[0:1, 1:2], AF.Exp, [spk], ["pnew"], scale=0.125)
                                yield
                                tt(pts[:, 0:1], exs[:, 0:1], emat[:, head, 128:129], ALU.mult, ["exs", "emat"], ["pts"])
                                yield
                                op_, opk = ps_next()
                                mm(op_[:, 0:1], vca[:, hl, :], pts[:, 0:1], True, False, ["vca", "pts"], [opk])
                                mm(op_[:, 0:1], vnew[0:1, s, g, hl, :], pnew[0:1, 0:1], False, True, ["vnew", "pnew"], [opk])
                                ov = oa[:, T + s:T + s + 1]
                                if g == 0:
                                    cp(ov, op_[:, 0:1], [opk], [oask])
                                else:
                                    tt(ov, ov, op_[:, 0:1], ALU.add, [opk, oask], [oask])
                                yield

                    steps = []
                    for g in (range(3) if "2" in phases else []):
                        d = DILS[g]
                        L = T // d
                        nb = L // 128
                        for r_ in range(d):
                            for kb in range(nb):
                                steps.append((g, r_, kb, nb, L, d))
                    LOOK = 3
                    NPB = len(ptb)
                    sgen = samp_gen()
                    pend = []
                    prevs = {}
                    accq = []

                    def stage_pv(info):
                        (g, r_, kb, nb, L, d, pt_, ptk, bl) = info
                        op_, opk = ps_next()
                        prev = prevs.get((g, r_)) if kb > 0 else None
                        if prev is not None:
                            ppt, pptk, pbl = prev
                            mm(op_[:, 0:128], vaug[g][:, pbl, hl, :], ppt[:, 128:256], True, False, [("vaug", g, pbl, hl), pptk], [opk])
                        mm(op_[:, 0:128], vaug[g][:, bl, hl, :], pt_[:, 0:128], prev is None, True, [("vaug", g, bl, hl), ptk], [opk])
                        prevs[(g, r_)] = (pt_, ptk, bl)
                        accq.append((g, r_, kb, d, op_, opk))
                        if len(accq) > 1:
                            stage_acc(accq.pop(0))

                    def stage_acc(a_):
                        (g, r_, kb, d, op_, opk) = a_
                        t0 = kb * 128 * d + r_
                        ov = oa[:, t0:t0 + 127 * d + 1:d]
                        if g == 0:
                            okeys = [("oacc", hl, kb)]
                        elif g == 1:
                            okeys = [("oacc", hl, 4 * kb + i_) for i_ in range(4)]
                        else:
                            okeys = [("oacc", hl, i_) for i_ in range(16)]
                        if g == 0:
                            cp(ov, op_[:, 0:128], [opk], okeys)
                        else:
                            tt(ov, ov, op_[:, 0:128], ALU.add, [opk] + okeys, okeys)

                    for (g, r_, kb, nb, L, d) in steps:
                        head = 4 * g + 2 * ps_ + hl
                        base = r_ * L + kb * 128
                        ncols = 256 if kb + 1 < nb else 128
                        sp_, spk = ps_next()
                        mm(sp_[:, 0:ncols], kT[g][pb:pb + 64, base:base + 128], qT[g][pb:pb + 64, base:base + ncols],
                           True, True, [("qk", 0, g), ("qk", 768, g)], [spk])
                        ex, exk = exb[it % 4], ("exb", it % 4)
                        pt_, ptk = ptb[it % NPB], ("ptb", it % NPB)
                        it += 1
                        actf(ex[:, 0:ncols], sp_[:, 0:ncols], AF.Exp, [spk], [exk], scale=0.125)
                        tt(pt_[:, 0:ncols], ex[:, 0:ncols], emat[:, head, 0:ncols], ALU.mult, [exk, "emat"], [ptk],
                           eng="pool" if it % 2 else "dve")
                        pend.append((g, r_, kb, nb, L, d, pt_, ptk, base // 128))
                        if len(pend) > LOOK:
                            stage_pv(pend.pop(0))
                        next(sgen, None)
                    while pend:
                        stage_pv(pend.pop(0))
                    while accq:
                        stage_acc(accq.pop(0))
                    for _ in sgen:
                        pass
                    nb_, db_ = (0, 64) if hl == 0 else (64, 0)
                    if "4" not in phases:
                        continue
                    oall = [("oacc", hl, i_) for i_ in range(16)]
                    actf(rden[nb_:nb_ + 64, :], oa[db_:db_ + 64, :], AF.Ln, oall + [oask], ["rden"])
                    actf(rden[nb_:nb_ + 64, :], rden[nb_:nb_ + 64, :], AF.Exp, ["rden"], ["rden"], scale=-1.0)
                    tt(attnT[nb_:nb_ + 64, ps_, :], oa[nb_:nb_ + 64, :], rden[nb_:nb_ + 64, :], ALU.mult, oall + [oask, "rden"],
                       [("attnT", ps_, hl)])
        P.barrier()


    st_b0.close()
    st_h.close()

    if "C" in phases:
        with contextlib.ExitStack() as st:
            F32R = mybir.dt.float32r
            USE_R = "R" not in os.environ.get("BSKIP", "")
            r32 = (lambda ap: ap.bitcast(F32R)) if USE_R else (lambda ap: ap)
            NCH = WT // C
            NW = 4 * WT
            mk2 = P.sbuf([64, 256], F32, st)
            dmas(mk2[:], mk2_d, [], ["mk2"])
            cmask = P.sbuf([128, NW], BF16, st)
            smask = P.sbuf([128, WT], F32, st)
            blkr = P.sbuf([128, 128], F32, st)
            idr = P.sbuf([64, 64], F32, st)
            P.pool(lambda e: e.memset(cmask[:], 1.0), [], ["cmask"])
            P.pool(lambda e: e.memset(cmask[:, 0:NW:C], 0.0), ["cmask"], ["cmask"])
            P.pool(lambda e: e.memset(smask[:], 0.0), [], ["smask"])
            P.pool(lambda e: e.memset(smask[:, 0:WT:C], 1.0), ["smask"], ["smask"])
            cp(r32(blkr[:]), blk[:], ["blk"], ["blkr"])
            cp(r32(idr[:]), identf[0:64, 0:64], ["identf"], ["idr"])
            wl_b = P.sbuf([128, 512], BF16, st)
            al_b = P.sbuf([128, 512], BF16, st)
            gl_b = P.sbuf([128, 512], BF16, st)
            P.dma(lambda e: e.dma_start(out=wl_b[0:64, :], in_=w_lora), [], ["wl_b"], eng="pool")
            P.dma(lambda e: e.dma_start(out=al_b[64:128, :], in_=a_lora), [], ["al_b"], eng="pool")
            P.dma(lambda e: e.dma_start(out=gl_b[:], in_=g_lora), [], ["gl_b"], eng="pool")
            z12 = P.sbuf([128, WT], F32, st)
            z13 = P.sbuf([128, WT], F32, st)
            twza = P.sbuf([128, WT], BF16, st)
            sgb = P.sbuf([128, WT], BF16, st)
            NTB = 12
            TB = [P.sbuf([128, NW], F32, st) for _ in range(NTB)]
            SQ = P.sbuf([128, NW], F32, st)
            TVB = [[P.sbuf([128 if s_ else 64, 8 * 64], F32, st) for _ in range(6)] for s_ in range(2)]
            SHB = P.sbuf([128, 8 * 64], F32, st)
            SCB = [P.sbuf([128, NW], BF16, st) for _ in range(2)]
            HG = P.sbuf([64, 8, 64], F32, st)
            TK = [("T", i) for i in range(NTB)]
            natbs = [[P.sbuf([128, NW], BF16, st) for _ in range(5)] for _ in range(2)]
            nks = [[("natb", p_, i) for i in range(5)] for p_ in range(2)]
            gbufs = [P.sbuf([128, NW], F32, st) for _ in range(3)]
            bons = [P.sbuf([128, NW], F32, st) for _ in range(3)]
            GNB = [P.sbuf([128, NW], F32, st) for _ in range(2)]
            SQ2 = P.sbuf([128, NW], F32, st)
            gams = [P.sbuf([128, 4 * NCH], F32, st) for _ in range(2)]
            gamFs = [P.sbuf([64, 8, NCH], F32, st) for _ in range(3)]
            opFs = [P.sbuf([64, 8, NCH, 4, 64], BF16, st) for _ in range(3)]
            TMs = [[P.sbuf([64, NCH, 3, 128], BF16, st) for _ in range(4)] for _ in range(3)]
            Mbs = [P.sbuf([64, 8 * NCH, 192], BF16, st) for _ in range(2)]
            TinvTs = [P.sbuf([64, 8, NCH, 64], BF16, st) for _ in range(2)]
            Hf = P.sbuf([64, 8, 64], F32, st)
            Hb = P.sbuf([64, 8, 64], BF16, st)
            Wsb = P.sbuf([64, 8, 64], BF16, st)
            Usb = P.sbuf([64, 8, 64], BF16, st)
            ynat = P.sbuf([128, NW], F32, st)
            s0in = P.sbuf([64, 8, 64], F32, st)
            sout = P.sbuf([64, 8, 64], F32, st)
            P.pool(lambda e: e.memset(Hf[:], 0.0), [], ["Hf"])
            P.pool(lambda e: e.memset(Hb[:], 0.0), [], ["Hb"])
            w4 = lambda t: t[:, :].rearrange("p (h t) -> p h t", h=4)
            pool_ctr = {"T": 0, "S": 0}
            pool_ids = {"T": [0, 1, 2], "S": [3, 6]}

            def psn(stage):
                ids = pool_ids[stage]
                i = ids[pool_ctr[stage] % len(ids)]
                pool_ctr[stage] += 1
                return psf[i], ("psf", i)

            def emit_state_out(dst_view):
                pt, pk = psn("S")
                for h in range(8):
                    tr(pt[0:64, h * 64:(h + 1) * 64], Hf[:, h, :], identf[0:64, 0:64], ["Hf", "identf"], [pk])
                cp(sout[:], pt[0:64, 0:512].rearrange("p (h k) -> p h k", h=8), [pk], ["sout"])
                dmas(dst_view, sout[:], ["sout"], [])

            NTILES = T // WT + NS * C // WT
            pslim[0] = 4
            zk = lambda c: [("zscr", c, n) for n in range(5)]

            def preA(ti, par):
                samp = ti >= T // WT
                c0 = ti * WT
                natb, nk, gbuf, bon, gam = natbs[par], nks[par], gbufs[ti % 3], bons[ti % 3], gams[par]
                gbk, bonk, gamk = ("gbuf", ti % 3), ("bon", ti % 3), ("gam", par)
                bt, kt, at, rt, vb = natb
                sgy, a_, kk, rn, f_, cs_, eg, egi, r_, k_, v_, egm = TB[0:12]
                sgyk, ak_, kkk, rnk, fk, csk, egk, egik, rk_, kk_, vk_, egmk = TK[0:12]
                P0, P0k, P1, P1k = psf[4], [("psf", 4)], psf[5], [("psf", 5)]
                hsl = [(slice(hp * 128, (hp + 1) * 128), slice(hp * WT, (hp + 1) * WT)) for hp in range(4)]
                dmas(z12[:], zscr[12, :, c0:c0 + WT], zk(12), ["z12"])
                dmas(z13[:], zscr[13, :, c0:c0 + WT], zk(13), ["z13"])
                for j, (dst, key) in enumerate(((r_, rk_), (k_, kk_), (v_, vk_))):
                    dmas(w4(dst), zscr[4 * j:4 * j + 4, :, c0:c0 + WT].rearrange("c p t -> p c t"),
                         [x for c in range(4 * j, 4 * j + 4) for x in zk(c)], [key])
                yield
                actf(twza[0:64, :], z12[0:64, :], AF.Tanh, ["z12"], ["twza0"])
                cp(twza[64:128, :], z12[64:128, :], ["z12"], ["twza1"], eng="pool")
                actf(sgb[:], z13[:], AF.Sigmoid, ["z13"], ["sgb"])
                for hp, (cs, ws_) in enumerate(hsl):
                    ts(kk[:, ws_], k_[:, ws_], pcol("kk", hp), None, ALU.mult, None, [kk_, "pvec"], [kkk])
                yield
                for hp, (cs, ws_) in enumerate(hsl):
                    mm(P0[:, ws_], wl_b[0:64, cs], twza[0:64, :], True, True, ["wl_b", "twza0"], P0k)
                for hp, (cs, ws_) in enumerate(hsl):
                    mm(P1[:, ws_], al_b[64:128, cs], twza[64:128, :], True, True, ["al_b", "twza1"], P1k)
                tt(r32(SQ[:]), kk[:], kk[:], ALU.mult, [kkk], ["SQ"])
                yield
                for hp, (cs, ws_) in enumerate(hsl):
                    actf(sgy[:, ws_], P0[:, ws_], AF.Sigmoid, P0k + ["pvec"], [sgyk], bias=pcol("w0", hp))
                for hp, (cs, ws_) in enumerate(hsl):
                    actf(a_[:, ws_], P1[:, ws_], AF.Sigmoid, P1k + ["pvec"], [ak_], bias=pcol("a0", hp))
                if samp:
                    tt(w4(sgy), w4(sgy), smask[:, :].unsqueeze(1).to_broadcast([128, 4, WT]), ALU.mult, [sgyk, "smask"], [sgyk])
                yield
                P.dve(lambda e: e.tensor_tensor_scan(out=cs_[:], data0=cmask[:], data1=sgy[:], initial=0.0,
                                                     op0=ALU.mult, op1=ALU.add), ["cmask", sgyk], [csk])
                for hp, (cs, ws_) in enumerate(hsl):
                    mm(P0[:, ws_], r32(blkr[:]), r32(SQ[:, ws_]), True, True, ["blkr", "SQ"], P0k)
                for hp, (cs, ws_) in enumerate(hsl):
                    mm(P1[:, ws_], gl_b[:, cs], sgb[:], True, True, ["gl_b", "sgb"], P1k)
                for hp, (cs, ws_) in enumerate(hsl):
                    ts(f_[:, ws_], a_[:, ws_], pcol("ka", hp), pcol("oka", hp), ALU.mult, ALU.add, [ak_, "pvec"], [fk], eng="pool")
                yield
                actf(eg[:], cs_[:], AF.Exp, [csk], [egk], scale=CDEC)
                actf(egi[:], cs_[:], AF.Exp, [csk], [egik], scale=-CDEC)
                tt(egm[:], cs_[:], sgy[:], ALU.subtract, [csk, sgyk], [egmk])
                ts(rn[:], P0[:, 0:NW], 6e-20, None, ALU.max, None, P0k, [rnk])
                cp(gbuf[:], P1[:, 0:NW], P1k, [gbk], eng="act")
                tt(f_[:], k_[:], f_[:], ALU.mult, [kk_, fk], [fk], eng="pool")
                yield
                actf(egm[:], egm[:], AF.Exp, [egmk], [egmk], scale=CDEC)
                actf(rn[:], rn[:], AF.Ln, [rnk], [rnk])
                tt(rt[:], r_[:], eg[:], ALU.mult, [rk_, egk], [nk[3]])
                tt(kt[:], f_[:], egi[:], ALU.mult, [fk, egik], [nk[1]])
                cp(gam[:], eg[:, C - 1:NW:C], [egk], [gamk], eng="pool")
                tt(cs_[:], r_[:], f_[:], ALU.mult, [rk_, fk, csk], [csk], eng="pool")
                cp(vb[:], v_[:], [vk_], [nk[4]], eng="pool")
                yield
                actf(rn[:], rn[:], AF.Exp, [rnk], [rnk], scale=-0.5)
                for hp, (cs, ws_) in enumerate(hsl):
                    ts(r32(SQ[:, ws_]), cs_[:, ws_], pcol("rk", hp), None, ALU.mult, None, [csk, "pvec"], ["SQ"])
                yield
                tt(kk[:], kk[:], rn[:], ALU.mult, [kkk, rnk], [kkk])
                for hp, (cs, ws_) in enumerate(hsl):
                    mm(P0[:, ws_], r32(blkr[:]), r32(SQ[:, ws_]), True, True, ["blkr", "SQ"], P0k)
                yield
                tt(sgy[:], kk[:], a_[:], ALU.mult, [kkk, ak_], [sgyk])
                stt(at[:], kk[:], -1.0, egm[:], ALU.mult, ALU.mult, [kkk, egmk], [nk[2]])
                tt(bon[:], P0[:, 0:NW], v_[:], ALU.mult, P0k + [vk_], [bonk])
                yield
                tt(bt[:], sgy[:], egi[:], ALU.mult, [sgyk, egik], [nk[0]])
                yield

            def preB(ti, par):
                natb, nk, gam = natbs[par], nks[par], gams[par]
                gamk = ("gam", par)
                p3 = ti % 3
                opF, TM, gamF = opFs[p3], TMs[p3], gamFs[p3]
                opk, gfk = ("opF", p3), ("gamF", p3)
                bt, kt, at, rt, vb = natb
                for e_ in range(2):
                    for j, src in enumerate((bt, kt, at, rt)):
                        cp(opF[:, e_:8:2, :, j, :], src[e_ * 64:(e_ + 1) * 64, :].rearrange("p (h c t) -> p h c t", h=4, c=NCH),
                           [nk[j]], [opk], eng="act")
                    cp(gamF[:, e_:8:2, :], gam[e_ * 64:(e_ + 1) * 64, :].rearrange("p (h c) -> p h c", h=4), [gamk], [gfk], eng="pool")
                gB = gam[:, :].unsqueeze(2).to_broadcast([128, 4 * NCH, C])
                kts, bts = SCB
                tt(kts[:, :].rearrange("p (c t) -> p c t", t=C), kt[:, :].rearrange("p (c t) -> p c t", t=C), gB, ALU.mult,
                   [nk[1], gamk], ["kts"], eng="pool")
                tt(bts[:, :].rearrange("p (c t) -> p c t", t=C), bt[:, :].rearrange("p (c t) -> p c t", t=C), gB, ALU.mult,
                   [nk[0], gamk], ["bts"], eng="pool")
                yield
                for hp in range(4):
                    for c2 in range(0, NCH, 2):
                        ptr, ptrk = psb_next()
                        for cc in range(2):
                            c = c2 + cc
                            for j, (src, skey) in enumerate(((vb, nk[4]), (kts, "kts"), (bts, "bts"))):
                                col = hp * WT + c * C
                                tr(ptr[0:64, (cc * 3 + j) * 128:(cc * 3 + j + 1) * 128], src[:, col:col + C], identb[:],
                                   [skey, "identb"], [ptrk])
                        cp(TM[hp][:, c2:c2 + 2, :, :], ptr[0:64, 0:768].rearrange("p (c j f) -> p c j f", c=2, j=3), [ptrk], [("TM", p3, hp)])
                yield

            HPG = 8 // NCH
            NGRP = 8 // HPG
            NP_ = HPG * NCH
            v3 = lambda t: t[0:64, :].rearrange("p (c t) -> p c t", c=NP_)

            def tinv_gen(grp, slot, ti):
                par = ti % 2
                opF, Mb, TinvT = opFs[ti % 3], Mbs[par], TinvTs[par]
                opk = ("opF", ti % 3)
                Y, X, Yn, Xn, Q, Qn = TVB[slot]
                kY, kX, kYn, kXn, kQ, kQn = [("tv", slot, i) for i in range(6)]
                for hh in range(HPG):
                    h = HPG * grp + hh
                    pt, pk0 = psn("T")
                    pk = [pk0]
                    for c in range(NCH):
                        ar = opF[:, h, c, 2:4, :]
                        mm(pt[0:64, c * 256:c * 256 + 128], opF[:, h, c, 0, :], ar, True, True, [opk], pk)
                        mm(pt[0:64, c * 256 + 128:c * 256 + 256], opF[:, h, c, 1, :], ar, True, True, [opk], pk)
                    pv3 = pt[0:64, 0:NCH * 256].rearrange("p (a b) -> p a b", a=NCH)
                    tt(r32(v3(Y)[:, hh * NCH:(hh + 1) * NCH, :]), pv3[:, :, 0:64],
                       mk2[:, 0:64].unsqueeze(1).to_broadcast([64, NCH, 64]), ALU.mult, pk + ["mk2"], [kY])
                    tt(Mb[:, h * NCH:(h + 1) * NCH, :], pv3[:, :, 64:256],
                       mk2[:, 64:256].unsqueeze(1).to_broadcast([64, NCH, 192]), ALU.mult, pk + ["mk2"], [("Mb", par, h)])
                yield
                if ti >= T // WT:
                    cp(TinvT[:, grp * HPG:(grp + 1) * HPG, :, :].rearrange("p h c t -> p (h c) t"),
                       identf[0:64, 0:64].unsqueeze(1).to_broadcast([64, NP_, 64]), ["identf"], [("TinvT", par, grp)], eng="pool")
                    yield
                    return
                pt, pk = psn("T")
                for c in range(NP_):
                    tr(pt[0:64, c * 64:(c + 1) * 64], Y[0:64, c * 64:(c + 1) * 64], identf[0:64, 0:64], [kY, "identf"], [pk])
                cp(r32(X[0:64, :]), pt[0:64, 0:512], [pk], [kX])
                tt(r32(v3(Q)), v3(Y), identf[0:64, 0:64].unsqueeze(1).to_broadcast([64, NP_, 64]), ALU.add, [kY, "identf"], [kQ])
                yield
                for lvl in range(5):
                    pt, pk = psn("T")
                    for c in range(NP_):
                        sl_ = slice(c * 64, (c + 1) * 64)
                        mm(pt[0:64, sl_], r32(Y[0:64, sl_]), r32(X[0:64, sl_]), True, True, [kY, kX], [pk])
                    cp(r32(Xn[0:64, :]), pt[0:64, 0:512], [pk], [kXn], eng="act")
                    if lvl < 4:
                        pt2, pk2 = psn("T")
                        for c in range(NP_):
                            sl_ = slice(c * 64, (c + 1) * 64)
                            mm(pt2[0:64, sl_], r32(X[0:64, sl_]), r32(Y[0:64, sl_]), True, True, [kY, kX], [pk2])
                        cp(r32(Yn[0:64, :]), pt2[0:64, 0:512], [pk2], [kYn], eng="act")
                    yield
                    pt3, pk3 = psn("T")
                    for c in range(NP_):
                        sl_ = slice(c * 64, (c + 1) * 64)
                        mm(pt3[0:64, sl_], r32(Xn[0:64, sl_]), r32(Q[0:64, sl_]), True, True, [kXn, kQ], [pk3])
                    tt(r32(Qn[0:64, :]), pt3[0:64, 0:512], Q[0:64, :], ALU.add, [pk3, kQ], [kQn])
                    Y, Yn, kY, kYn = Yn, Y, kYn, kY
                    X, Xn, kX, kXn = Xn, X, kXn, kX
                    Q, Qn, kQ, kQn = Qn, Q, kQn, kQ
                    yield
                cp(TinvT[:, grp * HPG:(grp + 1) * HPG, :, :], Q[0:64, :].rearrange("p (h c t) -> p h c t", h=HPG, c=NCH),
                   [kQ], [("TinvT", par, grp)], eng="pool")
                yield


            def tinv_pair_gen(ti):
                par = ti % 2
                opF, Mb, TinvT = opFs[ti % 3], Mbs[par], TinvTs[par]
                opk = ("opF", ti % 3)
                PBS = (0, 64)
                B = [list(TVB[0]), list(TVB[1])]
                K = [[("tv", s_, i) for i in range(6)] for s_ in range(2)]
                rows = lambda t, s_: t[PBS[s_]:PBS[s_] + 64, :]
                v3r = lambda t, s_: rows(t, s_).rearrange("p (c t) -> p c t", c=NP_)
                for s_ in range(2):
                    Y, kY = B[s_][0], K[s_][0]
                    for hh in range(HPG):
                        h = HPG * s_ + hh
                        pt, pk0 = psn("T")
                        pk = [pk0]
                        for c in range(NCH):
                            ar = opF[:, h, c, 2:4, :]
                            mm(pt[0:64, c * 256:c * 256 + 128], opF[:, h, c, 0, :], ar, True, True, [opk], pk)
                            mm(pt[0:64, c * 256 + 128:c * 256 + 256], opF[:, h, c, 1, :], ar, True, True, [opk], pk)
                        pv3 = pt[0:64, 0:NCH * 256].rearrange("p (a b) -> p a b", a=NCH)
                        tt(r32(v3r(Y, s_)[:, hh * NCH:(hh + 1) * NCH, :]), pv3[:, :, 0:64],
                           mk2[:, 0:64].unsqueeze(1).to_broadcast([64, NCH, 64]), ALU.mult, pk + ["mk2"], [kY])
                        tt(Mb[:, h * NCH:(h + 1) * NCH, :], pv3[:, :, 64:256],
                           mk2[:, 64:256].unsqueeze(1).to_broadcast([64, NCH, 192]), ALU.mult, pk + ["mk2"], [("Mb", par, h)])
                    yield
                pts = [psn("T") for _ in range(2)]
                for c in range(NP_):
                    for s_ in range(2):
                        pb_ = PBS[s_]
                        tr(pts[s_][0][0:64, c * 64:(c + 1) * 64], B[s_][0][pb_:pb_ + 64, c * 64:(c + 1) * 64],
                           identf[pb_:pb_ + 64, pb_:pb_ + 64], [K[s_][0], "identf"], [pts[s_][1]])
                for s_ in range(2):
                    pb_ = PBS[s_]
                    cp(r32(rows(B[s_][1], s_)), pts[s_][0][0:64, 0:512], [pts[s_][1]], [K[s_][1]])
                    tt(r32(v3r(B[s_][4], s_)), v3r(B[s_][0], s_),
                       identf[pb_:pb_ + 64, pb_:pb_ + 64].unsqueeze(1).to_broadcast([64, NP_, 64]), ALU.add,
                       [K[s_][0], "identf"], [K[s_][4]])
                yield
                for lvl in range(5):
                    pts = [psn("T") for _ in range(2)]
                    for c in range(NP_):
                        sl_ = slice(c * 64, (c + 1) * 64)
                        for s_ in range(2):
                            Y, X = B[s_][0], B[s_][1]
                            mm(pts[s_][0][0:64, sl_], r32(rows(Y, s_)[:, sl_]), r32(rows(X, s_)[:, sl_]), True, True,
                               [K[s_][0], K[s_][1]], [pts[s_][1]])
                    for s_ in range(2):
                        cp(r32(rows(B[s_][3], s_)), pts[s_][0][0:64, 0:512], [pts[s_][1]], [K[s_][3]], eng="act")
                    if lvl < 4:
                        pts2 = [psn("T") for _ in range(2)]
                        for c in range(NP_):
                            sl_ = slice(c * 64, (c + 1) * 64)
                            for s_ in range(2):
                                Y, X = B[s_][0], B[s_][1]
                                mm(pts2[s_][0][0:64, sl_], r32(rows(X, s_)[:, sl_]), r32(rows(Y, s_)[:, sl_]), True, True,
                                   [K[s_][0], K[s_][1]], [pts2[s_][1]])
                        for s_ in range(2):
                            cp(r32(rows(B[s_][2], s_)), pts2[s_][0][0:64, 0:512], [pts2[s_][1]], [K[s_][2]])
                    yield
                    pts3 = [psn("T") for _ in range(2)]
                    for c in range(NP_):
                        sl_ = slice(c * 64, (c + 1) * 64)
                        for s_ in range(2):
                            Xn, Q = B[s_][3], B[s_][4]
                            mm(pts3[s_][0][0:64, sl_], r32(rows(Xn, s_)[:, sl_]), r32(rows(Q, s_)[:, sl_]), True, True,
                               [K[s_][3], K[s_][4]], [pts3[s_][1]])
                    for s_ in range(2):
                        Q, Qn = B[s_][4], B[s_][5]
                        if s_ == 0:
                            tt(r32(rows(Qn, 0)), pts3[0][0][0:64, 0:512], rows(Q, 0), ALU.add, [pts3[0][1], K[0][4]], [K[0][5]])
                        else:
                            cp(rows(SHB, 1), pts3[1][0][0:64, 0:512], [pts3[1][1]], ["shb"], eng="act")
                            tt(r32(rows(Qn, 1)), rows(SHB, 1), rows(Q, 1), ALU.add, ["shb", K[1][4]], [K[1][5]])
                    for s_ in range(2):
                        b_, k_ = B[s_], K[s_]
                        b_[0], b_[2], k_[0], k_[2] = b_[2], b_[0], k_[2], k_[0]
                        b_[1], b_[3], k_[1], k_[3] = b_[3], b_[1], k_[3], k_[1]
                        b_[4], b_[5], k_[4], k_[5] = b_[5], b_[4], k_[5], k_[4]
                    yield
                for s_ in range(2):
                    cp(TinvT[:, s_ * HPG:(s_ + 1) * HPG, :, :], rows(B[s_][4], s_).rearrange("p (h c t) -> p h c t", h=HPG, c=NCH),
                       [K[s_][4]], [("TinvT", par, s_)], eng="act")
                yield

            def scan_gn(ti, par):
                samp = ti >= T // WT
                sbase = (ti - T // WT) * NCH
                c0 = ti * WT
                gbuf, bon = gbufs[ti % 3], bons[ti % 3]
                gbk, bonk = ("gbuf", ti % 3), ("bon", ti % 3)
                p3 = ti % 3
                opF, TM, gamF, Mb, TinvT = opFs[p3], TMs[p3], gamFs[p3], Mbs[par], TinvTs[par]
                opk, gfk = ("opF", p3), ("gamF", p3)
                for c in range(NCH):
                    if samp:
                        dmas(s0in[:], swkv[sbase + c].rearrange("h v k -> v h k"), [], ["s0in"])
                        pt, pk = psn("S")
                        for h in range(8):
                            tr(pt[0:64, h * 64:(h + 1) * 64], s0in[:, h, :], identf[0:64, 0:64], ["s0in", "identf"], [pk])
                        cp(Hf[:], pt[0:64, 0:512].rearrange("p (h v) -> p h v", h=8), [pk], ["Hf"])
                        cp(Hb[:], Hf[:], ["Hf"], ["Hb"], eng="act")
                    Vh = lambda h: TM[h // 2][:, c, 0, (h % 2) * 64:(h % 2) * 64 + 64]
                    Kh = lambda h: TM[h // 2][:, c, 1, (h % 2) * 64:(h % 2) * 64 + 64]
                    Bh = lambda h: TM[h // 2][:, c, 2, (h % 2) * 64:(h % 2) * 64 + 64]
                    Mrb = lambda h: Mb[:, h * NCH + c, 0:64]
                    Mak = lambda h: Mb[:, h * NCH + c, 64:128]
                    Mrk = lambda h: Mb[:, h * NCH + c, 128:192]
                    tt(HG[:], Hf[:], gamF[:, :, c:c + 1].to_broadcast([64, 8, 64]), ALU.mult, ["Hf", gfk], ["HG"], eng="pool")
                    pw_, pwk_ = psn("S")
                    for h in range(8):
                        o_ = pw_[0:64, h * 64:(h + 1) * 64]
                        mm(o_, opF[:, h, c, 2, :], Hb[:, h, :], True, False, [opk, "Hb"], [pwk_])
                        mm(o_, Mak(h), Vh(h), False, True, [("Mb", par, h), ("TM", p3, h // 2)], [pwk_])
                    cp(Wsb[:], pw_[0:64, 0:512].rearrange("p (h v) -> p h v", h=8), [pwk_], ["Wsb"], eng="act")
                    yield
                    pu_, puk_ = psn("S")
                    for h in range(8):
                        mm(pu_[0:64, h * 64:(h + 1) * 64], TinvT[:, h, c, :], Wsb[:, h, :], True, True, [("TinvT", par, h // HPG), "Wsb"], [puk_])
                    cp(Usb[:], pu_[0:64, 0:512].rearrange("p (h v) -> p h v", h=8), [puk_], ["Usb"])
                    yield
                    py_, pyk_ = psn("S")
                    for h in range(8):
                        o_ = py_[0:64, h * 64:(h + 1) * 64]
                        mm(o_, Hb[:, h, :], opF[:, h, c, 3, :], True, False, ["Hb", opk], [pyk_])
                        mm(o_, Usb[:, h, :], Mrb(h), False, False, ["Usb", ("Mb", par, h)], [pyk_])
                        mm(o_, Vh(h), Mrk(h), False, True, [("TM", p3, h // 2), ("Mb", par, h)], [pyk_])
                    pyv = py_[0:64, 0:512].rearrange("p (h e t) -> p h e t", h=4, e=2)
                    for e_ in range(2):
                        cp(r32(w4(ynat)[e_ * 64:(e_ + 1) * 64, :, c * C:(c + 1) * C]), pyv[:, :, e_, :], [pyk_], ["ynat"], eng="act")
                    ph_, phk_ = psn("S")
                    for h in range(8):
                        o_ = ph_[0:64, h * 64:(h + 1) * 64]
                        mm(o_, Bh(h), Usb[:, h, :], True, False, [("TM", p3, h // 2), "Usb"], [phk_])
                        mm(o_, Kh(h), Vh(h), False, True, [("TM", p3, h // 2)], [phk_])
                    phv = ph_[0:64, 0:512].rearrange("p (h v) -> p h v", h=8)
                    tt(Hb[:], phv, HG[:], ALU.add, [phk_, "HG"], ["Hb"])
                    tt(Hf[:], phv, HG[:], ALU.add, [phk_, "HG"], ["Hf"])
                    yield
                    if samp:
                        emit_state_out(wkv_s[sbase + c].rearrange("h v k -> v h k"))
                if ti == T // WT - 1:
                    emit_state_out(wkv_p.rearrange("h v k -> v h k"))
                ync, ysq = GNB
                ynk, ysk = "gny", "gnq"
                SQ = SQ2
                pm, pmk0 = psn("S")
                pmk = [pmk0]
                for hp in range(4):
                    ws_ = slice(hp * WT, (hp + 1) * WT)
                    mm(pm[:, ws_], r32(blkr[:]), r32(ynat[:, ws_]), True, True, ["blkr", "ynat"], pmk)
                stt(ync[:], pm[:, 0:NW], -1.0 / 64, ynat[:], ALU.mult, ALU.add, pmk + ["ynat"], [ynk])
                tt(r32(SQ[:]), ync[:], ync[:], ALU.mult, [ynk], ["SQ2"])
                yield
                pv_, pvk0 = psn("S")
                pvk_ = [pvk0]
                for hp in range(4):
                    ws_ = slice(hp * WT, (hp + 1) * WT)
                    mm(pv_[:, ws_], r32(blkr[:]), r32(SQ[:, ws_]), True, True, ["blkr", "SQ2"], pvk_)
                ts(ysq[:], pv_[:, 0:NW], 1.0 / 64, 64e-5, ALU.mult, ALU.add, pvk_, [ysk])
                yield
                actf(ysq[:], ysq[:], AF.Ln, [ysk], [ysk])
                actf(ysq[:], ysq[:], AF.Exp, [ysk], [ysk], scale=-0.5)
                tt(ync[:], ync[:], ysq[:], ALU.mult, [ynk, ysk], [ynk])
                for hp in range(4):
                    ws_ = slice(hp * WT, (hp + 1) * WT)
                    ts(ync[:, ws_], ync[:, ws_], pcol("gng", hp), pcol("gnb", hp), ALU.mult, ALU.add, [ynk, "pvec"], [ynk])
                yield
                tt(ync[:], ync[:], bon[:], ALU.add, [ynk, bonk], [ynk])
                if not samp:
                    tt(rwkvT[:, :, c0:c0 + WT], w4(ync), w4(gbuf), ALU.mult, [ynk, gbk], ["rwkvT"])
                else:
                    rwt = SCB[0]
                    tt(rwt[:], ync[:], gbuf[:], ALU.mult, [ynk, gbk], ["kts"])
                    cp(rwkvT[:, :, T + sbase:T + sbase + NCH], w4(rwt)[:, :, 0:WT:C], ["kts"], ["rwkvT"], eng="pool")
                yield

            def drain(g_):
                for _ in g_:
                    pass

            def rr(gens):
                alive = True
                while alive:
                    alive = False
                    for g_ in gens:
                        try:
                            next(g_)
                            alive = True
                        except StopIteration:
                            pass

            def tgens(ti):
                if NGRP == 2 and "Q" in BSKIP:
                    return [tinv_pair_gen(ti)]
                return [tinv_gen(g_, g_ % 2, ti) for g_ in range(NGRP)]

            def pgen(ti):
                yield from preA(ti, ti % 2)
                yield from preB(ti, ti % 2)

            drain(pgen(0))
            rr(tgens(0) + ([pgen(1)] if NTILES > 1 else []))
            for ti in range(NTILES):
                gens = [scan_gn(ti, ti % 2)]
                if ti + 1 < NTILES:
                    gens += tgens(ti + 1)
                if ti + 2 < NTILES:
                    gens.append(pgen(ti + 2))
                if "S" in BSKIP:
                    for g_ in gens:
                        drain(g_)
                else:
                    rr(gens)
            pslim[0] = 7
        P.barrier()

    if "rwkvT" in debug:
        dbg_out["rwkvT"] = (rwkvT, [128, 4 * NT], BF16)
    if "attnT" in debug:
        dbg_out["attnT"] = (attnT, [128, 2 * NT], BF16)


    mergedT = P.sbuf([128, 8, NT], BF16)
    st_d2e = contextlib.ExitStack()
    if "D" in phases:
        x2s = [[P.sbuf([128, D], F32, st_d2e) for _ in range(5)], None]
        junk2 = P.sbuf([128, D], BF16, st_d2e)
        hmb = [P.sbuf([128, D], BF16, st_d2e) for _ in range(2)]
        hmTs = [P.sbuf([128, 8, 516], BF16, st_d2e), None]
        st2 = P.sbuf([128, 128], F32, st_d2e)
        reqs = [[(w_out[:, h_ * 512:(h_ + 1) * 512], 8, 0, 512)] for h_ in range(2)]
        for n_ in range(4):
            reqs += [[(w_up[:, mb_ * 512:(mb_ + 1) * 512], 8, 0, 512)] for mb_ in range(8)]
            if n_ < 3:
                reqs += [[(w_out[:, h_ * 512:(h_ + 1) * 512], 8, 0, 512)] for h_ in range(2)]
            reqs += [[(w_down[kb_ * 1024:(kb_ + 1) * 1024, h_ * 512:(h_ + 1) * 512], 8, 0, 512)] for h_ in range(2) for kb_ in range(4)]
        wsD2 = WS(st_d2e, 4, 512, reqs)

    def subs_of(n):
        return [(j, 128, n * 512 + j * 128) for j in range(4)] + ([(4, NS, T)] if n == 3 else [])

    def WN(n, psum_fn):
        par = n % 2
        x2, hmT = x2s[par], hmTs[par]
        subs = subs_of(n)
        for (j, rows, t0) in subs:
            src = xp[t0:t0 + rows, :] if j < 4 else xs
            dmas(x2[j][0:rows, :], src, [], [("x2", par, j, 0), ("x2", par, j, 1)])
        for half in range(2):
            wt, wk = wsD2.get()
            for (j, rows, t0) in subs:
                pt, pk = psum_fn()
                for k in range(8):
                    mm(pt[0:rows, 0:512], mergedT[:, k, t0:t0 + rows], wt[:, k, 0:512], k == 0, k == 7,
                       [wk, ("mg", (n if j < 4 else 4), k)], [pk])
                xv = x2[j][0:rows, half * 512:(half + 1) * 512]
                tt(xv, pt[0:rows, 0:512], xv, ALU.add, [pk, ("x2", par, j, half)], [("x2", par, j, half)])
        yield
        for (j, rows, t0) in subs:
            sk = ("st2", par, j)
            s0, s1, s2 = [st2[0:rows, 64 * par + 8 * j + i_:64 * par + 8 * j + i_ + 1] for i_ in range(3)]
            xk = [("x2", par, j, 0), ("x2", par, j, 1)]
            actf(junk2[0:rows, :], x2[j][0:rows, :], AF.Square, xk, ["junk2", sk], accum=s0)
            yield
            ts(s1, s0, 1.0 / D, 1e-6, ALU.mult, ALU.add, [sk], [sk])
            actf(s1, s1, AF.Sqrt, [sk], [sk])
            yield
            P.dve(lambda e, s1=s1, s2=s2: e.reciprocal(out=s2, in_=s1), [sk], [sk])
            hb, hbk = hmb[j % 2], ("hmb", j % 2)
            ts(hb[0:rows, :], x2[j][0:rows, :], s2, None, ALU.mult, None, xk + [sk], [hbk])
            yield
            cdst = j * 128 if j < 4 else 512
            for k4 in range(2):
                ptb_, ptbk = psb_next()
                for kk_ in range(4):
                    k = 4 * k4 + kk_
                    tr(ptb_[:, kk_ * 128:kk_ * 128 + rows], hb[0:rows, k * 128:(k + 1) * 128], identb[0:rows, 0:rows],
                       [hbk, "identb"], [ptbk])
                for kk_ in range(4):
                    k = 4 * k4 + kk_
                    actf(hmT[:, k, cdst:cdst + rows], ptb_[:, kk_ * 128:kk_ * 128 + rows], AF.Copy, [ptbk, "pvec"],
                         [("hmT", par, k, j)], scale=pcol("norm2", k))
                yield


    if "D" in phases:
        with contextlib.ExitStack() as st:
            hT = P.sbuf([128, 8, NT], BF16, st)
            hsv = hscr.rearrange("p (k t) -> p k t", k=8)
            for n_ in range(5):
                cs_ = slice(n_ * 512, (n_ + 1) * 512) if n_ < 4 else slice(T, NT)
                dmas(hT[:, :, cs_], hsv[:, :, cs_], [], [("hT", n_, k_) for k_ in range(8)])
            gA = [P.sbuf([128, 512], F32, st) for _ in range(2)]
            gB = [P.sbuf([128, 512], F32, st) for _ in range(2)]
            t1b = [P.sbuf([128, 512], F32, st) for _ in range(2)]
            t2b = [P.sbuf([128, 512], F32, st) for _ in range(2)]
            it = 0
            ws = WS(st, 3, 512, [[(w_pa[:, m_ * 128:(m_ + 1) * 128], 2, 0, 128), (w_pb[:, m_ * 128:(m_ + 1) * 128], 4, 128, 128),
                                  (w_in[:, 4096 + m_ * 128:4096 + (m_ + 1) * 128], 8, 256, 128),
                                  (w_in[:, 5120 + m_ * 128:5120 + (m_ + 1) * 128], 8, 384, 128)] for m_ in range(8)])
            for m in range(8):
                wt, wk = ws.get()
                for n in range(5):
                    c0, w_ = (n * 512, 512) if n < 4 else (T, NS)
                    hk = [("hT", n, k) for k in range(8)]
                    p1, p1k = ps_next()
                    for k in range(2):
                        mm(p1[:, 0:w_], wt[:, k, 0:128], attnT[:, k, c0:c0 + w_], k == 0, k == 1, [wk], [p1k])
                    p2, p2k = ps_next()
                    for k in range(4):
                        mm(p2[:, 0:w_], wt[:, k, 128:256], rwkvT[:, k, c0:c0 + w_], k == 0, k == 3, [wk], [p2k])
                    p3, p3k = ps_next()
                    for k in range(8):
                        mm(p3[:, 0:w_], wt[:, k, 256:384], hT[:, k, c0:c0 + w_], k == 0, k == 7, [wk, ("hT", n, k)], [p3k])
                    p4, p4k = ps_next()
                    for k in range(8):
                        mm(p4[:, 0:w_], wt[:, k, 384:512], hT[:, k, c0:c0 + w_], k == 0, k == 7, [wk, ("hT", n, k)], [p4k])
                    b_ = it % 2
                    it += 1
                    actf(gA[b_][:, 0:w_], p3[:, 0:w_], AF.Sigmoid, [p3k, "pvec"], [("gA", b_)], bias=pcol("bgate", m))
                    actf(gB[b_][:, 0:w_], p4[:, 0:w_], AF.Sigmoid, [p4k, "pvec"], [("gB", b_)], bias=pcol("bgate", 8 + m))
                    tt(t1b[b_][:, 0:w_], p1[:, 0:w_], gA[b_][:, 0:w_], ALU.mult, [p1k, ("gA", b_)], [("t1b", b_)])
                    tt(t2b[b_][:, 0:w_], p2[:, 0:w_], gB[b_][:, 0:w_], ALU.mult, [p2k, ("gB", b_)], [("t2b", b_)])
                    tt(mergedT[:, m, c0:c0 + w_], t1b[b_][:, 0:w_], t2b[b_][:, 0:w_], ALU.add, [("t1b", b_), ("t2b", b_)],
                       [("mg", n, m)])
                    if m == 7:
                        if n == 0:
                            wn0 = WN(0, ps_next)
                        for _ in range(7):
                            next(wn0, None)
            for _ in wn0:
                pass
        P.barrier()

    if "D" in phases:
        with contextlib.ExitStack() as st:
            gfin = P.sbuf([128, D], F32, st)
            dmas(gfin[:], normf_d.partition_broadcast(128), [], ["gfin"])
            x2s[1] = [P.sbuf([128, D], F32, st) for _ in range(5)]
            hmTs[1] = P.sbuf([128, 8, 516], BF16, st)
            uT = P.sbuf([128, 32, 516], BF16, st)
            rl = [P.sbuf([128, 512], F32, st) for _ in range(2)]
            yst = [P.sbuf([128, D], F32, st) for _ in range(2)]
            for n in range(4):
                par = n % 2
                x2, hmT = x2s[par], hmTs[par]
                subs = subs_of(n)
                ri = 0
                for mb in range(8):
                    wt, wk = wsD2.get()
                    for jj in range(4):
                        m = 4 * mb + jj
                        pieces = [(0, 512)] + ([(512, NS)] if n == 3 else [])
                        for (cc0, w_) in pieces:
                            pt, pk = ps_next()
                            for k in range(8):
                                mm(pt[:, 0:w_], wt[:, k, jj * 128:(jj + 1) * 128], hmT[:, k, cc0:cc0 + w_], k == 0, k == 7,
                                   [wk] + [("hmT", par, k, j_) for j_ in range(5)], [pk])
                            rb, rbk = rl[ri % 2], ("rl", ri % 2)
                            ri += 1
                            actf(rb[:, 0:w_], pt[:, 0:w_], AF.Relu, [pk], [rbk])
                            tt(uT[:, m, cc0:cc0 + w_], rb[:, 0:w_], rb[:, 0:w_], ALU.mult, [rbk], [("uT", m)])
                nxt = WN(n + 1, ps_next) if n + 1 < 4 else iter(())
                next(nxt, None)
                if "W" in BSKIP:
                    for _ in nxt:
                        pass
                for half in range(2):
                    for kb in range(4):
                        wt, wk = wsD2.get()
                        for (j, rows, t0) in subs:
                            cdst = j * 128 if j < 4 else 512
                            for k in range(8):
                                mm(psf[j][0:rows, 0:512], uT[:, kb * 8 + k, cdst:cdst + rows], wt[:, k, 0:512],
                                   kb == 0 and k == 0, kb == 3 and k == 7, [wk, ("uT", kb * 8 + k)], [("psf", j)])
                            next(nxt, None)
                    for (j, rows, t0) in subs:
                        xv = x2[j][0:rows, half * 512:(half + 1) * 512]
                        tt(xv, psf[j][0:rows, 0:512], xv, ALU.add, [("psf", j), ("x2", par, j, half)], [("x2", par, j, half)])
                for _ in nxt:
                    pass
                pctr[0] = 5
                for (j, rows, t0) in subs:
                    sk = ("st2f", j)
                    s0, s1, s2 = [st2[0:rows, 8 * j + 3 + i_:8 * j + 4 + i_] for i_ in range(3)]
                    xk = [("x2", par, j, 0), ("x2", par, j, 1)]
                    actf(junk2[0:rows, :], x2[j][0:rows, :], AF.Square, xk, ["junk2", sk], accum=s0)
                    ts(s1, s0, 1.0 / D, 1e-6, ALU.mult, ALU.add, [sk], [sk])
                    actf(s1, s1, AF.Sqrt, [sk], [sk])
                    P.dve(lambda e, s1=s1, s2=s2: e.reciprocal(out=s2, in_=s1), [sk], [sk])
                    yb, ybk = yst[j % 2], ("yst", j % 2)
                    stt(yb[0:rows, :], x2[j][0:rows, :], s2, gfin[0:rows, :], ALU.mult, ALU.mult, xk + [sk, "gfin"], [ybk])
                    dst = y_p[t0:t0 + rows, :] if j < 4 else y_s
                    dmas(dst, yb[0:rows, :], [ybk], [])
        P.barrier()
    st_d2e.close()

    for name, (tile_, shp, dt_) in dbg_out.items():
        tmp = P.sbuf(shp, F32)
        do = dout("dbg_" + name, shp)
        cp(tmp[:], tile_[:].rearrange("p a b -> p (a b)") if len(tile_.shape) == 3 else tile_[:], [], ["dbgtmp" + name])
        dmas(do, tmp[:], ["dbgtmp" + name], [])

    P.emit()
    return nc, P


def _consts():
    kj = np.arange(128)[:, None]
    qc = np.arange(256)[None, :]
    delta = qc - kj
    valid = (delta >= 0) & (delta <= 128)
    emat = np.zeros((128, 12, 256), np.float32)
    for h in range(12):
        dil = DILS[h // 4]
        e = np.exp(-np.float64(np.float32(SLOPES[h])) * (delta * dil).astype(np.float64))
        emat[:, h, :] = np.where(valid, e, 0.0).astype(np.float32)
    s = np.arange(64)[:, None]
    t = np.arange(64)[None, :]
    strict = (s < t).astype(np.float32)
    incl = (s <= t).astype(np.float32)
    mk2 = np.concatenate([strict, incl, strict, incl], axis=1)
    ident = np.eye(128, dtype=np.float32)
    blk = np.zeros((128, 128), np.float32)
    blk[:64, :64] = 1.0
    blk[64:, 64:] = 1.0
    return emat, mk2, ident, blk


def _pvec(inp):
    def fm(v):
        v = np.asarray(v, np.float32).reshape(-1, 128)
        return np.ascontiguousarray(v.T)
    cols = [fm(inp["norm1_g"][0]), fm(inp["norm2_g"][0]), fm(inp["b_gate"][0]), fm(inp["mu_shift"][0]),
            fm(inp["w0"][0]), fm(inp["a0"][0]), fm(inp["k_k"][0]), fm(inp["k_a"][0]), fm(inp["r_k"][0].reshape(-1)),
            fm(inp["gn_g"][0]), fm(inp["gn_b"][0]), np.zeros((128, 4), np.float32)]
    return np.ascontiguousarray(np.concatenate(cols, axis=1))


_CACHE = {}


def make_in_maps(inp):
    emat, mk2, ident, blk = _consts()
    pv = _pvec(inp)
    f = lambda a: np.ascontiguousarray(np.asarray(a, np.float32))
    shared = dict(
        w_in=f(inp["w_in"][0]), w_proj_a=f(inp["w_proj_a"][0]), w_proj_b=f(inp["w_proj_b"][0]), w_out=f(inp["w_out"][0]),
        w_up=f(inp["w_up"][0]), w_down=f(inp["w_down"][0]), w_lora_up=f(inp["w_lora_up"][0]), a_lora_up=f(inp["a_lora_up"][0]),
        g_lora_up=f(inp["g_lora_up"][0]), pvec=pv, normf_g=f(inp["normf_g"]), emat=emat, mk2=mk2, ident=ident, blk=blk)
    maps = []
    for c in range(NCORES):
        sl = slice(c * NS, (c + 1) * NS)
        m = dict(shared)
        m["xp"] = f(inp["x_prompt"][c])
        m["xs"] = f(inp["x_sample"][sl, 0])
        m["c128"] = f(np.asarray(inp["cache_kv_w128"][0][sl]).reshape(NS, 128, 512))
        m["c512"] = f(np.asarray(inp["cache_kv_w512"][0][sl]).reshape(NS, 512, 512))
        m["c2048"] = f(np.asarray(inp["cache_kv_w2048"][0][sl]).reshape(NS, 2048, 512))
        m["swkv"] = f(inp["state_wkv"][0][sl])
        m["sshift"] = f(inp["state_shift"][0][sl])
        maps.append(m)
    return maps


def kernel(**inp):
    if "nc" not in _CACHE:
        _CACHE["nc"] = build_program()
    nc, P = _CACHE["nc"]
    maps = make_in_maps(inp)
    res = run_bass_kernel_spmd(nc, maps, core_ids=list(range(NCORES)))
    R = res.results
    cat = lambda name: np.stack([np.asarray(r[name], np.float32) for r in R])
    y_p = cat("y_p")
    y_s = np.concatenate([np.asarray(r["y_s"], np.float32) for r in R])[:, None, :]
    kv128_p = cat("kv128_p").reshape(1, 8, 128, 2, 4, 64)
    kv512_p = cat("kv512_p").reshape(1, 8, 512, 2, 4, 64)
    kv2048_p = cat("kv2048_p").reshape(1, 8, 2048, 2, 4, 64)
    wkv_p = cat("wkv_p").reshape(1, 8, 8, 64, 64)
    shift_p = cat("shift_p").reshape(1, 8, 1792)
    ks = lambda n: np.concatenate([np.asarray(r[n], np.float32) for r in R]).reshape(1, 32, 1, 2, 4, 64)
    wkv_s = np.concatenate([np.asarray(r["wkv_s"], np.float32) for r in R]).reshape(1, 32, 8, 64, 64)
    shift_s = np.concatenate([np.asarray(r["shift_s"], np.float32) for r in R]).reshape(1, 32, 1792)
    return (y_p, y_s, kv128_p, kv512_p, kv2048_p, wkv_p, shift_p, ks("kv128_s"), ks("kv512_s"), ks("kv2048_s"), wkv_s, shift_s)
```
